# Optimizing a Trainium2 kernel written in Bass

```python
import math
import functools
import jax
import jax.numpy as jnp
from jax import lax
import numpy as np

D_MODEL = 1024
BATCH = 4
SEQ = 4096
DEPTH = 4
DEC_BATCH = 128
DEC_SEQ = 4
PAST_LEN = 8192
PAGE_SIZE = 128

D_MIX = D_MODEL
HEAD_DIM = 64
N_HEADS = (D_MIX // 2) // HEAD_DIM
N_KV_HEADS = 2
KV_GROUP = N_HEADS // N_KV_HEADS
D_ATTN = N_HEADS * HEAD_DIM
D_KV = N_KV_HEADS * HEAD_DIM
WINDOW = 128
ROPE_THETA = 10000.0
D_SSM = D_MIX // 4
SSM_GROUP = 16
N_SSM_GROUPS = D_SSM // SSM_GROUP
SSM_STATE = 64
D_LRU = D_MIX // 4
N_LRU_BLOCKS = 4
LRU_BLOCK = D_LRU // N_LRU_BLOCKS
LRU_CONV = 4
LRU_C = 8.0
D_FF = 11 * D_MODEL // 4
FFN_CONV = 3
D_IN = D_ATTN + 2 * D_KV + D_SSM + 2 * D_LRU
N_MOD = 6
EPS = 1e-6

kernel_name = 'hymba_s5_swa_rglru_convffn_step'


def rms_norm(x, w):
    xf = x.astype(jnp.float32)
    y = xf * lax.rsqrt(jnp.mean(xf * xf, axis=-1, keepdims=True) + EPS)
    return (y * w.astype(jnp.float32)).astype(x.dtype)


def rotary(x, pos):
    half = HEAD_DIM // 2
    inv = ROPE_THETA ** (-jnp.arange(half, dtype=jnp.float32) / half)
    ang = pos[:, None] * inv[None, :]
    cos = jnp.cos(ang)[None, :, None, :]
    sin = jnp.sin(ang)[None, :, None, :]
    xf = x.astype(jnp.float32)
    x1, x2 = xf[..., :half], xf[..., half:]
    return jnp.concatenate([x1 * cos - x2 * sin, x2 * cos + x1 * sin], axis=-1).astype(x.dtype)


def causal_dwconv(x, buf, w, b):
    k = w.shape[0]
    t = x.shape[1]
    xp = jnp.concatenate([buf.astype(x.dtype), x], axis=1)
    y = b + sum(w[j] * xp[:, j:j + t] for j in range(k))
    return y.astype(x.dtype), xp[:, t:]


def linear_recurrence(a, b, h0):
    b = b.at[:, 0].add(a[:, 0] * h0)

    def combine(l, r):
        return l[0] * r[0], r[0] * l[1] + r[1]

    _, h = lax.associative_scan(combine, (a, b), axis=1)
    return h


def sink_softmax_attend(q, k, v, allowed, sinks):
    s = jnp.einsum('...qkgd,...skd->...kgqs', q.astype(jnp.float32), k.astype(jnp.float32)) * (HEAD_DIM ** -0.5)
    s = jnp.where(allowed, s, -jnp.inf)
    sink = sinks.astype(jnp.float32)[..., None, None]
    m = jnp.maximum(jnp.max(s, axis=-1, keepdims=True), sink)
    p = jnp.exp(s - m)
    p = p / (jnp.sum(p, axis=-1, keepdims=True) + jnp.exp(sink - m))
    o = jnp.einsum('...kgqs,...skd->...qkgd', p, v.astype(jnp.float32))
    return o.astype(v.dtype)


def window_attention_prompt(q, k, v, sinks):
    bsz, s = q.shape[:2]
    nb = s // WINDOW
    qb = q.reshape(bsz, nb, WINDOW, N_KV_HEADS, KV_GROUP, HEAD_DIM)

    def band(t):
        tb = t.reshape(bsz, nb, WINDOW, N_KV_HEADS, HEAD_DIM)
        prev = jnp.concatenate([jnp.zeros_like(tb[:, :1]), tb[:, :-1]], axis=1)
        return jnp.concatenate([prev, tb], axis=2)

    qi = jnp.arange(WINDOW)[:, None]
    si = jnp.arange(2 * WINDOW)[None, :]
    blk = jnp.arange(nb)[:, None, None]
    diff = qi + WINDOW - si
    key_pos = blk * WINDOW - WINDOW + si
    allowed = ((diff >= 0) & (diff < WINDOW) & (key_pos >= 0))[:, None, None]
    o = sink_softmax_attend(qb, band(k), band(v), allowed, sinks.reshape(N_KV_HEADS, KV_GROUP))
    keep = min(WINDOW, s)
    return o.reshape(bsz, s, D_ATTN), k[:, s - keep:], v[:, s - keep:]


def window_attention_sample(q, k, v, sinks, k_buf, v_buf):
    n, t = q.shape[:2]
    w_buf = k_buf.shape[1]
    kk = jnp.concatenate([k_buf.astype(k.dtype), k], axis=1)
    vv = jnp.concatenate([v_buf.astype(v.dtype), v], axis=1)
    q_pos = PAST_LEN + jnp.arange(t)
    k_pos = PAST_LEN - w_buf + jnp.arange(w_buf + t)
    diff = q_pos[:, None] - k_pos[None, :]
    allowed = (diff >= 0) & (diff < WINDOW)
    qg = q.reshape(n, t, N_KV_HEADS, KV_GROUP, HEAD_DIM)
    o = sink_softmax_attend(qg, kk, vv, allowed, sinks.reshape(N_KV_HEADS, KV_GROUP))
    return o.reshape(n, t, D_ATTN), kk[:, t:], vv[:, t:]


def s5_mixer(u, h0_re, h0_im, lp):
    f32 = jnp.float32
    n, t = u.shape[:2]
    lam = lax.complex(lp['ssm_a_re'].astype(f32), lp['ssm_a_im'].astype(f32))
    dt = jnp.exp(lp['ssm_log_dt'].astype(f32))[:, None]
    a_bar = jnp.exp(lam * dt)
    b_mat = lax.complex(lp['ssm_b_re'].astype(f32), lp['ssm_b_im'].astype(f32))
    b_bar = ((a_bar - 1.0) / lam)[..., None] * b_mat
    uf = u.astype(f32)
    ug = uf.reshape(n, t, N_SSM_GROUPS, SSM_GROUP).astype(jnp.complex64)
    bu = jnp.einsum('ntgc,gpc->ntgp', ug, b_bar)
    h0 = lax.complex(h0_re.astype(f32), h0_im.astype(f32))
    h = linear_recurrence(jnp.broadcast_to(a_bar, bu.shape), bu, h0)
    c_mat = lax.complex(lp['ssm_c_re'].astype(f32), lp['ssm_c_im'].astype(f32))
    y = jnp.real(jnp.einsum('ntgp,gcp->ntgc', h, c_mat)).reshape(n, t, D_SSM)
    y = y + lp['ssm_d'].astype(f32) * uf
    g = jax.nn.gelu(y)
    out = g * jax.nn.sigmoid(g @ lp['ssm_w_glu'].astype(f32) + lp['ssm_b_glu'].astype(f32))
    h_last = h[:, -1]
    return out.astype(u.dtype), jnp.real(h_last), jnp.imag(h_last)


def rglru_mixer(xr, yg, conv_buf, h0, pos, lp):
    f32 = jnp.float32
    n, t = xr.shape[:2]
    xc, new_buf = causal_dwconv(xr, conv_buf, lp['lru_conv_w'], lp['lru_conv_b'])
    xf = xc.astype(f32)
    xh = xf.reshape(n, t, N_LRU_BLOCKS, LRU_BLOCK)
    r = jax.nn.sigmoid(jnp.einsum('nthi,hij->nthj', xh, lp['lru_w_a'].astype(f32)).reshape(n, t, D_LRU)
                       + lp['lru_b_a'].astype(f32))
    gi = jax.nn.sigmoid(jnp.einsum('nthi,hij->nthj', xh, lp['lru_w_i'].astype(f32)).reshape(n, t, D_LRU)
                        + lp['lru_b_i'].astype(f32))
    log_a = -LRU_C * r * jax.nn.softplus(-lp['lru_lambda'].astype(f32))
    a = jnp.exp(log_a)
    mult = jnp.sqrt(-jnp.expm1(2.0 * log_a))
    mult = jnp.where((pos == 0)[None, :, None], 1.0, mult)
    h = linear_recurrence(a, mult * gi * xf, h0.astype(f32))
    out = h * jax.nn.gelu(yg.astype(f32))
    return out.astype(xr.dtype), new_buf, h[:, -1]


def conv_ffn(h, buf, lp):
    up = h @ lp['ffn_w_up']
    upc, new_buf = causal_dwconv(up, buf, lp['ffn_conv_w'], lp['ffn_conv_b'])
    gate, val = upc[..., :D_FF], upc[..., D_FF:]
    return (jax.nn.gelu(gate) * val) @ lp['ffn_w_down'], new_buf


def trunk_layer(x, c, pos, lp, ssm_re, ssm_im, lru_h, lru_conv, ffn_conv, attend):
    n, t, _ = x.shape
    mod = jax.nn.silu(c) @ lp['w_ada'] + lp['b_ada']
    sh1, sc1, g1, sh2, sc2, g2 = [m[:, None, :] for m in jnp.split(mod, N_MOD, axis=-1)]
    h = rms_norm(x, lp['norm1']) * (1.0 + sc1) + sh1
    proj = h @ lp['w_in']
    c1 = D_ATTN
    c2 = c1 + D_KV
    c3 = c2 + D_KV
    c4 = c3 + D_SSM
    c5 = c4 + D_LRU
    q, k, v, u, xr, yg = jnp.split(proj, (c1, c2, c3, c4, c5), axis=-1)
    q = rotary(rms_norm(q.reshape(n, t, N_HEADS, HEAD_DIM), lp['q_norm']), pos)
    k = rotary(rms_norm(k.reshape(n, t, N_KV_HEADS, HEAD_DIM), lp['k_norm']), pos)
    v = v.reshape(n, t, N_KV_HEADS, HEAD_DIM)
    o_attn, k_new, v_new = attend(q, k, v, lp['sinks'])
    o_ssm, ssm_re_new, ssm_im_new = s5_mixer(u, ssm_re, ssm_im, lp)
    o_lru, lru_conv_new, lru_h_new = rglru_mixer(xr, yg, lru_conv, lru_h, pos, lp)
    on = lp['out_norm']
    o = jnp.concatenate([rms_norm(o_attn, on[:D_ATTN]),
                         rms_norm(o_ssm, on[D_ATTN:D_ATTN + D_SSM]),
                         rms_norm(o_lru, on[D_ATTN + D_SSM:])], axis=-1)
    x = x + g1 * (o @ lp['w_o'])
    h2 = rms_norm(x, lp['norm2']) * (1.0 + sc2) + sh2
    f, ffn_conv_new = conv_ffn(h2, ffn_conv, lp)
    x = x + g2 * f
    return x, (k_new, v_new, ssm_re_new, ssm_im_new, lru_h_new, lru_conv_new, ffn_conv_new)


def setup_inputs(seed: int = 0) -> dict:
    key = jax.random.key(seed)
    keys = iter(jax.random.split(key, 64))
    f32 = jnp.float32

    def nrm(shape, scale):
        return scale * jax.random.normal(next(keys), shape, f32)

    def gain(shape):
        return 1.0 + 0.02 * jax.random.normal(next(keys), shape, f32)

    L = DEPTH
    G, P = N_SSM_GROUPS, SSM_STATE
    w_buf = min(WINDOW, PAST_LEN)
    n_idx = jnp.arange(SSM_STATE, dtype=f32)
    a8 = jax.random.uniform(next(keys), (L, D_LRU), f32, 0.9, 0.999)
    a_base = a8 ** (1.0 / LRU_C)
    log_dt = jax.random.uniform(next(keys), (L, G), f32, math.log(1e-3), math.log(1e-1))
    return {
        'x_prompt': nrm((BATCH, SEQ, D_MODEL), 1.0),
        'x_sample': nrm((DEC_BATCH, DEC_SEQ, D_MODEL), 1.0),
        'cache_k': nrm((L, DEC_BATCH, w_buf, N_KV_HEADS, HEAD_DIM), 1.0),
        'cache_v': nrm((L, DEC_BATCH, w_buf, N_KV_HEADS, HEAD_DIM), 1.0),
        'state_ssm_re': nrm((L, DEC_BATCH, G, P), 0.1),
        'state_ssm_im': nrm((L, DEC_BATCH, G, P), 0.1),
        'state_lru_h': nrm((L, DEC_BATCH, D_LRU), 0.5),
        'state_lru_conv': nrm((L, DEC_BATCH, LRU_CONV - 1, D_LRU), 1.0),
        'state_ffn_conv': nrm((L, DEC_BATCH, FFN_CONV - 1, 2 * D_FF), 1.0),
        'c_prompt': nrm((BATCH, D_MODEL), 1.0),
        'c_sample': nrm((DEC_BATCH, D_MODEL), 1.0),
        'w_ada': nrm((L, D_MODEL, N_MOD * D_MODEL), 0.5 * D_MODEL ** -0.5),
        'b_ada': nrm((L, N_MOD * D_MODEL), 0.02),
        'norm1': gain((L, D_MODEL)),
        'w_in': nrm((L, D_MODEL, D_IN), D_MODEL ** -0.5),
        'q_norm': gain((L, HEAD_DIM)),
        'k_norm': gain((L, HEAD_DIM)),
        'sinks': nrm((L, N_HEADS), 0.5),
        'ssm_a_re': -0.5 + nrm((L, G, P), 0.01),
        'ssm_a_im': math.pi * n_idx + nrm((L, G, P), 0.01),
        'ssm_b_re': nrm((L, G, P, SSM_GROUP), (2 * SSM_GROUP) ** -0.5),
        'ssm_b_im': nrm((L, G, P, SSM_GROUP), (2 * SSM_GROUP) ** -0.5),
        'ssm_c_re': nrm((L, G, SSM_GROUP, P), 0.5),
        'ssm_c_im': nrm((L, G, SSM_GROUP, P), 0.5),
        'ssm_d': nrm((L, D_SSM), 1.0),
        'ssm_log_dt': log_dt,
        'ssm_w_glu': nrm((L, D_SSM, D_SSM), D_SSM ** -0.5),
        'ssm_b_glu': nrm((L, D_SSM), 0.02),
        'lru_conv_w': nrm((L, LRU_CONV, D_LRU), LRU_CONV ** -0.5),
        'lru_conv_b': nrm((L, D_LRU), 0.02),
        'lru_w_a': nrm((L, N_LRU_BLOCKS, LRU_BLOCK, LRU_BLOCK), LRU_BLOCK ** -0.5),
        'lru_b_a': nrm((L, D_LRU), 0.02),
        'lru_w_i': nrm((L, N_LRU_BLOCKS, LRU_BLOCK, LRU_BLOCK), LRU_BLOCK ** -0.5),
        'lru_b_i': nrm((L, D_LRU), 0.02),
        'lru_lambda': jnp.log(a_base) - jnp.log1p(-a_base),
        'out_norm': gain((L, D_MIX)),
        'w_o': nrm((L, D_MIX, D_MODEL), D_MIX ** -0.5),
        'norm2': gain((L, D_MODEL)),
        'ffn_w_up': nrm((L, D_MODEL, 2 * D_FF), D_MODEL ** -0.5),
        'ffn_conv_w': nrm((L, FFN_CONV, 2 * D_FF), FFN_CONV ** -0.5),
        'ffn_conv_b': nrm((L, 2 * D_FF), 0.02),
        'ffn_w_down': nrm((L, D_FF, D_MODEL), D_FF ** -0.5),
    }


def reference(x_prompt, x_sample, cache_k, cache_v, state_ssm_re, state_ssm_im, state_lru_h,
              state_lru_conv, state_ffn_conv, c_prompt, c_sample, w_ada, b_ada, norm1, w_in,
              q_norm, k_norm, sinks, ssm_a_re, ssm_a_im, ssm_b_re, ssm_b_im, ssm_c_re, ssm_c_im,
              ssm_d, ssm_log_dt, ssm_w_glu, ssm_b_glu, lru_conv_w, lru_conv_b, lru_w_a, lru_b_a,
              lru_w_i, lru_b_i, lru_lambda, out_norm, w_o, norm2, ffn_w_up, ffn_conv_w,
              ffn_conv_b, ffn_w_down):
    params = dict(w_ada=w_ada, b_ada=b_ada, norm1=norm1, w_in=w_in, q_norm=q_norm, k_norm=k_norm,
                  sinks=sinks, ssm_a_re=ssm_a_re, ssm_a_im=ssm_a_im, ssm_b_re=ssm_b_re,
                  ssm_b_im=ssm_b_im, ssm_c_re=ssm_c_re, ssm_c_im=ssm_c_im, ssm_d=ssm_d,
                  ssm_log_dt=ssm_log_dt, ssm_w_glu=ssm_w_glu, ssm_b_glu=ssm_b_glu,
                  lru_conv_w=lru_conv_w, lru_conv_b=lru_conv_b, lru_w_a=lru_w_a, lru_b_a=lru_b_a,
                  lru_w_i=lru_w_i, lru_b_i=lru_b_i, lru_lambda=lru_lambda, out_norm=out_norm,
                  w_o=w_o, norm2=norm2, ffn_w_up=ffn_w_up, ffn_conv_w=ffn_conv_w,
                  ffn_conv_b=ffn_conv_b, ffn_w_down=ffn_w_down)
    f32 = jnp.float32
    bsz, seq = x_prompt.shape[:2]
    dseq = x_sample.shape[1]
    pos_prompt = jnp.arange(seq, dtype=f32)
    pos_sample = PAST_LEN + jnp.arange(dseq, dtype=f32)
    xp, xs = x_prompt, x_sample
    new_p, new_s = [], []
    for i in range(DEPTH):
        lp = {name: arr[i] for name, arr in params.items()}
        xp, st_p = trunk_layer(
            xp, c_prompt, pos_prompt, lp,
            jnp.zeros((bsz, N_SSM_GROUPS, SSM_STATE), f32),
            jnp.zeros((bsz, N_SSM_GROUPS, SSM_STATE), f32),
            jnp.zeros((bsz, D_LRU), f32),
            jnp.zeros((bsz, LRU_CONV - 1, D_LRU), xp.dtype),
            jnp.zeros((bsz, FFN_CONV - 1, 2 * D_FF), xp.dtype),
            window_attention_prompt)
        xs, st_s = trunk_layer(
            xs, c_sample, pos_sample, lp,
            state_ssm_re[i], state_ssm_im[i], state_lru_h[i], state_lru_conv[i], state_ffn_conv[i],
            functools.partial(window_attention_sample, k_buf=cache_k[i], v_buf=cache_v[i]))
        new_p.append(st_p)
        new_s.append(st_s)
    pk, pv, p_re, p_im, p_lh, p_lc, p_fc = [jnp.stack(s) for s in zip(*new_p)]
    sk, sv, s_re, s_im, s_lh, s_lc, s_fc = [jnp.stack(s) for s in zip(*new_s)]
    return (xp, xs, pk, pv, p_re, p_im, p_lh, p_lc, p_fc, sk, sv, s_re, s_im, s_lh, s_lc, s_fc)
```

```python
import contextlib
import math
import numpy as np
import ml_dtypes
import concourse.bass as bass
import concourse.mybir as mybir
from concourse.bass_utils import run_bass_kernel_spmd

F32 = mybir.dt.float32
BF16 = mybir.dt.bfloat16
AF = mybir.ActivationFunctionType
ALU = mybir.AluOpType

NL = 4
D = 1024
DFF = 2816
NJ = 22
TP = 512
NSEG = 8
TS = 64
NSQ = 16
EPS = 1e-6


class Buf:
    __slots__ = ("name", "w", "rd")

    def __init__(self, name):
        self.name = name
        self.w = None
        self.rd = []


class Op:
    __slots__ = ("eng", "fn", "deps", "idx", "dma", "sem", "semval", "prevdma", "wkey")

    def __init__(self, eng, fn, dma):
        self.eng = eng
        self.fn = fn
        self.deps = set()
        self.dma = dma
        self.sem = None
        self.semval = None
        self.prevdma = None


class Sched:
    ENG = ("pe", "act", "dve", "pool", "sp")

    def __init__(self, nc, n_dma_sems=14):
        self.nc = nc
        self.ops = []
        self.n_dma_sems = n_dma_sems

    def add(self, eng, fn, reads=(), writes=(), dma=False):
        op = Op(eng, fn, dma)
        op.idx = len(self.ops)
        op.wkey = id(writes[0]) if len(writes) else None
        for b in reads:
            if b.w is not None:
                op.deps.add(b.w)
        for b in writes:
            if b.w is not None:
                op.deps.add(b.w)
            for r in b.rd:
                op.deps.add(r)
        for b in reads:
            if not dma:
                b.rd = [r for r in b.rd if r.dma or r.eng != eng]
            b.rd.append(op)
        for b in writes:
            b.w = op
            b.rd = []
        op.deps.discard(op)
        self.ops.append(op)
        return op

    def emit(self):
        nc = self.nc
        ops = self.ops
        need = set()
        for op in ops:
            for d in op.deps:
                if d.eng == "pe" and op.eng == "pe" and not d.dma and not op.dma:
                    continue
                need.add(d)
        cnt = {e: 0 for e in self.ENG}
        dmacnt = {e: 0 for e in self.ENG}
        dma_last = {}
        for op in ops:
            if op.dma:
                slot = dmacnt[op.eng] % self.n_dma_sems
                dmacnt[op.eng] += 1
                key = (op.eng, slot)
                prev = dma_last.get(key)
                op.prevdma = prev
                op.sem = key
                op.semval = (prev.semval if prev else 0) + 16
                dma_last[key] = op
            elif op in need:
                cnt[op.eng] += 1
                op.sem = (op.eng, None)
                op.semval = cnt[op.eng]
        with contextlib.ExitStack() as st:
            sems = {}
            for e in self.ENG:
                sems[(e, None)] = st.enter_context(nc.semaphore("s_" + e))
                for k in range(min(self.n_dma_sems, dmacnt[e])):
                    sems[(e, k)] = st.enter_context(nc.semaphore("d_%s%d" % (e, k)))
            block = st.enter_context(nc.Block())
            decos = {"pe": block.tensor, "act": block.scalar, "dve": block.vector,
                     "pool": block.gpsimd, "sp": block.sync}
            last_dmas = list(dma_last.values())
            for e in self.ENG:
                mine = [op for op in ops if op.eng == e]

                def body(eng, mine=mine, e=e):
                    waited = {}

                    def wait(key, val):
                        if waited.get(key, 0) >= val:
                            return
                        eng.wait_ge(sems[key], val)
                        waited[key] = val

                    for i_, op in enumerate(mine):
                        for d in sorted(op.deps, key=lambda o: o.idx):
                            if d.sem is None:
                                continue
                            wait(d.sem, d.semval)
                        if e == "pe":
                            for nx in mine[i_ + 1:i_ + 40]:
                                if nx.wkey != op.wkey:
                                    break
                                bad = False
                                for d in nx.deps:
                                    if d.sem is None:
                                        continue
                                    if d.idx > op.idx and not (d.eng == "pe" and not d.dma):
                                        bad = True
                                if bad:
                                    break
                                for d in sorted(nx.deps, key=lambda o: o.idx):
                                    if d.sem is not None and d.idx < op.idx:
                                        wait(d.sem, d.semval)
                        if op.dma and op.prevdma is not None:
                            wait(op.prevdma.sem, op.prevdma.semval)
                        ins = op.fn(eng)
                        if op.dma:
                            ins.then_inc(sems[op.sem], 16)
                        elif op.sem is not None:
                            ins.then_inc(sems[op.sem], 1)
                    if e == "sp":
                        for op in last_dmas:
                            wait(op.sem, op.semval)

                decos[e](body)


class V:
    __slots__ = ("ap", "bufs")

    def __init__(self, ap, bufs):
        self.ap = ap
        self.bufs = bufs

    def __getitem__(self, idx):
        return V(self.ap[idx], self.bufs)

    def re(self, pat, **kw):
        return V(self.ap.rearrange(pat, **kw), self.bufs)


class T:
    def __init__(self, ap, name, split=0, axis=1):
        self.t = ap
        self.split = split
        self.axis = axis
        self.bufs = [Buf("%s%d" % (name, i)) for i in range(split)] if split else [Buf(name)]

    def __getitem__(self, idx):
        ap = self.t[idx]
        if self.split:
            ax = self.axis
            if isinstance(idx, tuple):
                i1 = idx[ax] if len(idx) > ax else slice(None)
            else:
                i1 = idx if ax == 0 else slice(None)
            if isinstance(i1, int):
                bufs = [self.bufs[i1]]
            else:
                bufs = self.bufs[i1]
        else:
            bufs = self.bufs
        return V(ap, bufs)


def _ap(x):
    return x.ap if isinstance(x, V) else x


def _bufs(*xs):
    out = []
    for x in xs:
        if isinstance(x, V):
            out.extend(x.bufs)
    return out


def _pp_layout():
    sizes = [("n1", 8), ("n2", 8), ("on", 8), ("qn", 1), ("kn", 1), ("sk", 4), ("are", 8), ("aim", 8),
             ("ldt", 8), ("sd", 2), ("bglu", 2), ("lcw", 8), ("lcb", 2), ("ba", 2), ("bi", 2), ("lam", 2),
             ("fcw", 132), ("fcb", 44), ("bada", 48)]
    off = {}
    o = 0
    for n, s in sizes:
        off[n] = (o, s)
        o += s
    return off, o


PP_OFF, NPP = _pp_layout()
SW_OFF = {"bre": 0, "bim": 1024, "cre": 2048, "cim": 3072, "wa": 4096, "wi": 4352, "glu": 4608}
NSW = 5120
CST_OFF = {"ident": 0, "ones": 128, "blk": 256, "rot": 384, "coss": 512, "sins": 576}
NCST = 640


def _perm_attn_rows():
    idx = []
    for c in range(4):
        idx += list(range(64 * c, 64 * c + 64)) + list(range(256 + 64 * c, 256 + 64 * c + 64))
    return np.array(idx)


def _fm(v, nch):
    return np.ascontiguousarray(v.reshape(nch, 128).T)


def _host_params(inp):
    f32 = np.float32
    pp = np.zeros((128, NL, NPP), f32)
    sw = np.zeros((NL, 128, NSW), f32)
    perm = _perm_attn_rows()

    def put(l, name, arr):
        o, s = PP_OFF[name]
        pp[:, l, o:o + s] = arr.reshape(128, s)

    for l in range(NL):
        put(l, "n1", _fm(inp["norm1"][l], 8))
        put(l, "n2", _fm(inp["norm2"][l], 8))
        on = inp["out_norm"][l].copy()
        on[:512] = on[:512][perm]
        put(l, "on", _fm(on, 8))
        put(l, "qn", np.tile(inp["q_norm"][l], 2))
        put(l, "kn", np.tile(inp["k_norm"][l], 2))
        sk = np.zeros((128, 4), f32)
        for c in range(4):
            sk[:64, c] = inp["sinks"][l][c]
            sk[64:, c] = inp["sinks"][l][4 + c]
        put(l, "sk", sk)
        put(l, "are", _fm(inp["ssm_a_re"][l].reshape(-1), 8))
        put(l, "aim", _fm(inp["ssm_a_im"][l].reshape(-1), 8))
        put(l, "ldt", _fm(np.repeat(inp["ssm_log_dt"][l], 64), 8))
        put(l, "sd", _fm(inp["ssm_d"][l], 2))
        put(l, "bglu", _fm(inp["ssm_b_glu"][l], 2))
        lcw = np.zeros((128, 2, 4), f32)
        for j in range(4):
            lcw[:, :, j] = _fm(inp["lru_conv_w"][l][j], 2)
        put(l, "lcw", lcw)
        put(l, "lcb", _fm(inp["lru_conv_b"][l], 2))
        put(l, "ba", _fm(inp["lru_b_a"][l], 2))
        put(l, "bi", _fm(inp["lru_b_i"][l], 2))
        put(l, "lam", _fm(inp["lru_lambda"][l], 2))
        fcw = np.zeros((128, 44, 3), f32)
        for j in range(3):
            fcw[:, :, j] = _fm(inp["ffn_conv_w"][l][j], 44)
        put(l, "fcw", fcw)
        put(l, "fcb", _fm(inp["ffn_conv_b"][l], 44))
        put(l, "bada", _fm(inp["b_ada"][l], 48))
        bre = inp["ssm_b_re"][l]
        bim = inp["ssm_b_im"][l]
        cre = inp["ssm_c_re"][l]
        cim = inp["ssm_c_im"][l]
        for j in range(8):
            for gl in range(2):
                g = 2 * j + gl
                r0 = 32 * (j % 4) + 16 * gl
                sw[l, r0:r0 + 16, SW_OFF["bre"] + j * 128 + gl * 64: SW_OFF["bre"] + j * 128 + gl * 64 + 64] = bre[g].T
                sw[l, r0:r0 + 16, SW_OFF["bim"] + j * 128 + gl * 64: SW_OFF["bim"] + j * 128 + gl * 64 + 64] = bim[g].T
                sw[l, gl * 64:gl * 64 + 64, SW_OFF["cre"] + j * 128 + r0: SW_OFF["cre"] + j * 128 + r0 + 16] = cre[g].T
                sw[l, gl * 64:gl * 64 + 64, SW_OFF["cim"] + j * 128 + r0: SW_OFF["cim"] + j * 128 + r0 + 16] = cim[g].T
        for m in range(2):
            for hb in range(2):
                h = 2 * m + hb
                sw[l, hb * 64:hb * 64 + 64, SW_OFF["wa"] + m * 128 + hb * 64: SW_OFF["wa"] + m * 128 + hb * 64 + 64] = inp["lru_w_a"][l][h]
                sw[l, hb * 64:hb * 64 + 64, SW_OFF["wi"] + m * 128 + hb * 64: SW_OFF["wi"] + m * 128 + hb * 64 + 64] = inp["lru_w_i"][l][h]
        wg = inp["ssm_w_glu"][l]
        for kc in range(2):
            sw[l, :, SW_OFF["glu"] + kc * 256: SW_OFF["glu"] + kc * 256 + 256] = wg[kc * 128:(kc + 1) * 128, :]
    return pp.reshape(128, NL * NPP), sw


def _host_consts():
    f32 = np.float32
    cst = np.zeros((128, NCST), f32)
    cst[:, 0:128] = np.eye(128, dtype=f32)
    cst[:, 128:256] = 1.0
    cst[0:64, 256:320] = 1.0
    cst[64:128, 320:384] = 1.0
    rot = np.zeros((128, 128), f32)
    for m in range(128):
        if (m % 64) < 32:
            rot[m + 32, m] = -1.0
        else:
            rot[m - 32, m] = 1.0
    cst[:, 384:512] = rot
    s = np.arange(128)[:, None]
    q = np.arange(128)[None, :]
    prev = (s > q).astype(f32)
    diag = (s <= q).astype(f32)
    msk = np.zeros((128, 1536), f32)
    msk[:, 0:512] = np.concatenate([prev, prev, diag, diag], axis=1)
    t = np.arange(4)[None, :]
    mc = (np.arange(128)[:, None] > t).astype(f32)
    msk[:, 512:1024] = np.tile(mc, (1, 8 * 16))
    kk = np.arange(64)
    ks, kt = kk // 4, kk % 4
    mn = ((ks[:, None] == ks[None, :]) & (kt[:, None] <= kt[None, :])).astype(f32)
    msk[0:64, 1024:1536] = np.tile(mn, (1, 8))
    half = 32
    inv = (f32(10000.0) ** (-np.arange(half, dtype=f32) / f32(half))).astype(f32)
    fi = (np.arange(128) % 64) % 32

    def tabs(pos):
        ang = (pos.astype(f32)[:, None] * inv[None, :]).astype(f32)
        return np.ascontiguousarray(np.cos(ang).astype(f32)[:, fi].T), np.ascontiguousarray(np.sin(ang).astype(f32)[:, fi].T)

    cp, sp = tabs(np.arange(4096))
    cs, ss = tabs(8192 + np.arange(4))
    cst[:, 512:576] = np.tile(cs, (1, 16))
    cst[:, 576:640] = np.tile(ss, (1, 16))
    return cst, cp, sp, msk


class KB:
    def __init__(self):
        self.nc = bass.Bass("TRN2", target_bir_lowering=False)
        self.S = Sched(self.nc)
        self.st = contextlib.ExitStack()
        self.bank_i = 0
        self.pool_i = {}
        self.rot = {}

    def dram(self, name, shape, dtype=F32, kind="ExternalInput", split=0):
        h = self.nc.dram_tensor(name, list(shape), dtype, kind=kind)
        return T(h.ap(), name, split, axis=0)

    def sb(self, name, shape, dtype=F32, split=0):
        t = self.st.enter_context(self.nc.sbuf_tensor(name, list(shape), dtype))
        return T(t, name, split)

    def rots(self, name, n, shape, dtype=F32):
        self.rot[name] = ([self.sb("%s%d" % (name, i), shape, dtype) for i in range(n)], 0)

    def nxt(self, name):
        lst, i = self.rot[name]
        self.rot[name] = (lst, i + 1)
        return lst[i % len(lst)]

    POOLS = {"attn": [0, 1, 2], "ssm": [3, 4], "lru": [5]}
    cur_pool = None

    def bank(self):
        if self.cur_pool is not None:
            lst = self.POOLS[self.cur_pool]
            i = self.pool_i.get(self.cur_pool, 0)
            self.pool_i[self.cur_pool] = i + 1
            return self.ps[lst[i % len(lst)]]
        b = self.ps[self.bank_i % 6]
        self.bank_i += 1
        return b

    def mm(self, out, lhsT, rhs, start=True, stop=True):
        self.S.add("pe", lambda e: e.matmul(out.ap, lhsT.ap, rhs.ap, start=start, stop=stop),
                   reads=_bufs(lhsT, rhs), writes=_bufs(out))

    def tr(self, out, in_, ident):
        self.S.add("pe", lambda e: e.transpose(out.ap, in_.ap, ident.ap), reads=_bufs(in_, ident), writes=_bufs(out))

    def act(self, out, in_, func, scale=None, bias=None):
        kw = {}
        if scale is not None:
            kw["scale"] = _ap(scale)
        if bias is not None:
            kw["bias"] = _ap(bias)
        self.S.add("act", lambda e: e.activation(out.ap, in_.ap, func, **kw),
                   reads=_bufs(in_, scale, bias), writes=_bufs(out))

    def tt(self, eng, out, a, b, op):
        self.S.add(eng, lambda e: e.tensor_tensor(out.ap, a.ap, b.ap, op), reads=_bufs(a, b), writes=_bufs(out))

    def ts(self, eng, out, a, s1, op0, s2=None, op1=None):
        if op1 is None:
            fn = lambda e: e.tensor_scalar(out.ap, a.ap, _ap(s1), None, op0)
        else:
            fn = lambda e: e.tensor_scalar(out.ap, a.ap, _ap(s1), _ap(s2), op0, op1)
        self.S.add(eng, fn, reads=_bufs(a, s1, s2), writes=_bufs(out))

    def stt(self, out, a, scalar, b, op0, op1):
        self.S.add("dve", lambda e: e.scalar_tensor_tensor(out.ap, a.ap, _ap(scalar), b.ap, op0, op1),
                   reads=_bufs(a, scalar, b), writes=_bufs(out))

    def cp(self, eng, out, in_):
        if eng == "act":
            self.S.add("act", lambda e: e.copy(out.ap, in_.ap), reads=_bufs(in_), writes=_bufs(out))
        else:
            self.S.add(eng, lambda e: e.tensor_copy(out.ap, in_.ap), reads=_bufs(in_), writes=_bufs(out))

    def scan(self, out, d0, d1):
        self.S.add("dve", lambda e: e.tensor_tensor_scan(out.ap, d0.ap, d1.ap, 0.0, ALU.mult, ALU.add),
                   reads=_bufs(d0, d1), writes=_bufs(out))

    def recip(self, out, in_):
        self.S.add("dve", lambda e: e.reciprocal(out.ap, in_.ap), reads=_bufs(in_), writes=_bufs(out))

    def memset(self, eng, out, val):
        self.S.add(eng, lambda e: e.memset(out.ap, val), writes=_bufs(out))

    def dma(self, out, in_, eng="sp", nc_ok=False, **kw):
        if nc_ok:
            kw["allow_slow_non_contiguous"] = True
        self.S.add(eng, lambda e: e.dma_start(out=out.ap, in_=in_.ap, **kw), reads=_bufs(in_), writes=_bufs(out), dma=True)

    def pcol(self, name, l, i=0, n=1):
        o, s = PP_OFF[name]
        return self.PP[:, l * NPP + o + i: l * NPP + o + i + n]

    def cst(self, name, n=128, rows=slice(None)):
        o = CST_OFF[name]
        return self.CST[rows, o:o + n]

    def rstd_from(self, bank, Tn, inv_n):
        rs = self.nxt("RS")
        self.act(rs[:, :Tn], bank[:, :Tn], AF.Ln, scale=inv_n, bias=self.EPSC[:, 0:1])
        self.act(rs[:, :Tn], rs[:, :Tn], AF.Exp, scale=-0.5)
        return rs

    def load_T(self, dst, src, n):
        stg = self.nxt("STG")
        self.dma(stg[:n, :], src)
        b = self.bank()
        self.tr(b[:, :n], stg[:n, :], self.cst("ident", n, slice(0, n)))
        self.cp("act", dst, b[:, :n])

    def store_T(self, dst, src, n):
        b = self.bank()
        self.tr(b[:n, 0:128], src, self.cst("ident"))
        stg = self.nxt("STG")
        self.cp("act", stg[:n, :], b[:n, 0:128])
        self.dma(dst, stg[:n, :])


class Stream:
    def __init__(self, kb, name, nslots, shape, dtype, items):
        self.kb = kb
        self.tiles = [kb.sb("%s_%d" % (name, i), shape, dtype) for i in range(nslots)]
        self.items = items
        self.i = 0
        self.n = nslots

    def _load(self, idx):
        if idx < len(self.items):
            t = self.tiles[idx % self.n]
            self.kb.dma(t[:], self.items[idx])

    def start(self):
        for i in range(self.n):
            self._load(i)

    def get(self):
        return self.tiles[self.i % self.n]

    def done(self):
        self._load(self.i + self.n)
        self.i += 1


def build():
    kb = KB()
    nc = kb.nc
    st = kb.st
    with st:
        _build(kb)
    return nc


def _build(kb):
    nc = kb.nc
    xp = kb.dram("xp", [4096, D])
    xs = kb.dram("xs", [TS, D])
    ck = kb.dram("ck", [NL, NSQ, 128, 128])
    cv = kb.dram("cv", [NL, NSQ, 128, 128])
    sre = kb.dram("sre", [NL, NSQ, 1024])
    sim = kb.dram("sim", [NL, NSQ, 1024])
    slh = kb.dram("slh", [NL, NSQ, 256])
    slc = kb.dram("slc", [NL, NSQ * 3, 256])
    sfc = kb.dram("sfc", [NL, NSQ * 2, 5632])
    c17 = kb.dram("c17", [17, D])
    wada = kb.dram("wada", [NL, D, 6 * D])
    win = kb.dram("win", [NL, D, 1536])
    wo = kb.dram("wo", [NL, D, D])
    wup = kb.dram("wup", [NL, D, 2 * DFF])
    wdn = kb.dram("wdn", [NL, DFF, D])
    ppd = kb.dram("pp", [128, NL * NPP])
    swd = kb.dram("sw", [NL, 128, NSW])
    cstd = kb.dram("cst", [128, NCST])
    mskd = kb.dram("msk", [128, 1536])
    cosP = kb.dram("cosP", [128, 4096])
    sinP = kb.dram("sinP", [128, 4096])
    O = "ExternalOutput"
    yp = kb.dram("yp", [4096, D], kind=O)
    ys = kb.dram("ys", [TS, D], kind=O)
    pk = kb.dram("pk", [NL, 128, 128], kind=O)
    pv = kb.dram("pv", [NL, 128, 128], kind=O)
    pre = kb.dram("pre", [NL, 1, 1024], kind=O)
    pim = kb.dram("pim", [NL, 1, 1024], kind=O)
    plh = kb.dram("plh", [NL, 1, 256], kind=O)
    plc = kb.dram("plc", [NL, 3, 256], kind=O)
    pfc = kb.dram("pfc", [NL, 2, 5632], kind=O)
    osk = kb.dram("osk", [NL, NSQ, 128, 128], kind=O)
    osv = kb.dram("osv", [NL, NSQ, 128, 128], kind=O)
    osre = kb.dram("osre", [NL, NSQ, 1024], kind=O)
    osim = kb.dram("osim", [NL, NSQ, 1024], kind=O)
    oslh = kb.dram("oslh", [NL, NSQ, 256], kind=O)
    oslc = kb.dram("oslc", [NL, NSQ * 3, 256], kind=O)
    osfc = kb.dram("osfc", [NL, NSQ * 2, 5632], kind=O)
    I = "Internal"
    win_s = kb.dram("win_s", [NL, 12, 128, 8, 128], BF16, kind=I, split=NL)
    wo_s = kb.dram("wo_s", [NL, 8, 128, 8, 128], BF16, kind=I, split=NL)
    wup_s = kb.dram("wup_s", [NL, NJ, 128, 8, 256], BF16, kind=I, split=NL)
    wdn_s = kb.dram("wdn_s", [NL, 8, 128, NJ, 128], BF16, kind=I, split=NL)
    sw_s = kb.dram("sw_s", [NL, 128, NSW], BF16, kind=I, split=NL)
    tabp = kb.dram("tabp", [NL, 8, 128, 4, TP], F32, kind=I, split=NL)
    tabs = kb.dram("tabs", [NL, 8, 128, 4, TS], F32, kind=I, split=NL)

    kb.ps = [T(kb.st.enter_context(nc.psum_tensor("ps%d" % i, [128, 512], F32)), "ps%d" % i) for i in range(8)]
    kb.CST = kb.sb("CST", [128, NCST])
    kb.PP = kb.sb("PP", [128, NL * NPP])
    kb.EPSC = kb.sb("EPSC", [128, 4])
    kb.ZERO = kb.sb("ZERO", [128, TP])
    ONEB = kb.sb("ONEB", [128, 128], BF16)
    BLKB = kb.sb("BLKB", [128, 128], BF16)
    MASK4 = kb.sb("MASK4", [128, 512], BF16)
    MASKC = kb.sb("MASKC", [128, 512], BF16)
    MASKN = kb.sb("MASKN", [64, 512], BF16)
    X = kb.sb("X", [128, 8, TP], split=8)
    H = kb.sb("H", [128, 8, TP], BF16, split=8)
    Ob = H
    PROJ = kb.sb("PROJ", [128, 11, TP], split=11)

    class GAlias:
        def __getitem__(self, idx):
            p_, j_, c_ = idx
            v = PROJ[:, j_ // 2, :]
            ap = v.ap.bitcast(BF16)[p_, (j_ % 2) * TP:(j_ % 2 + 1) * TP][:, c_]
            return V(ap, v.bufs)

    class Sub:
        def __init__(self, off):
            self.off = off

        def __getitem__(self, idx):
            p_, c_, f_ = idx
            return PROJ[p_, c_ + self.off, f_]

    G = GAlias()
    MODA1 = kb.sb("MODA1", [128, NL, 8, 17])
    MODB1 = kb.sb("MODB1", [128, NL, 8, 17])
    MODG1 = kb.sb("MODG1", [128, NL, 8, 17])
    MODA2 = kb.sb("MODA2", [128, NL, 8, 17])
    MODB2 = kb.sb("MODB2", [128, NL, 8, 17])
    MODG2 = kb.sb("MODG2", [128, NL, 8, 17])
    RHO = kb.sb("RHO", [128, NL, 8])
    NSP8 = kb.sb("NSP8", [128, NL, 2])
    NSP16 = kb.sb("NSP16", [128, NL, 2])
    SINKE = kb.sb("SINKE", [128, NL, 4])
    kb.rots("RS", 2, [128, TP])
    kb.rots("SQ", 3, [128, TP], BF16)
    kb.rots("W", 7, [128, TP])
    kb.rots("STG", 5, [128, 128])
    kb.rots("XT", 1, [128, D])
    kb.rots("E", 2, [128, 512], BF16)
    kb.rots("UE", 2, [128, TP + 2 * NSQ])
    QR = kb.sb("QR", [128, 4, TP], BF16, split=4)
    KRA = kb.sb("KRA", [128, TP], BF16)
    KRB = kb.sb("KRB", [128, TP], BF16)
    KF = kb.sb("KF", [128, TP])
    VT = kb.sb("VT", [128, 4, 128], BF16)
    VF = kb.sb("VF", [128, 128])
    CS = kb.sb("CS", [128, TP])
    SN = kb.sb("SN", [128, TP])
    OA = Sub(0)
    DEN = kb.sb("DEN", [128, 512])
    UB = kb.sb("UB", [128, 2, TP], BF16)
    kb.rots("HR", 2, [128, TP], BF16)
    kb.rots("HI", 2, [128, TP], BF16)
    GLB = kb.sb("GLB", [128, 2, TP], BF16)
    GLF = kb.sb("GLF", [128, 2, TP])
    OS = Sub(5)
    XC = Sub(7)
    XCB = kb.sb("XCB", [128, 2, TP], BF16)
    XE = kb.sb("XE", [128, TP + 3 * NSQ])
    OL = Sub(9)
    stP = []
    for l in range(NL):
        stP.append(dict(kprevA=kb.sb("kprevA%d" % l, [128, 128], BF16), kprevB=kb.sb("kprevB%d" % l, [128, 128], BF16), vprev=kb.sb("vprev%d" % l, [128, 128], BF16),
                        sr=kb.sb("sr%d" % l, [128, 8, 1]), si=kb.sb("si%d" % l, [128, 8, 1]),
                        lh=kb.sb("lh%d" % l, [128, 2, 1]), lc=kb.sb("lc%d" % l, [128, 2, 3]),
                        fc=kb.sb("fc%d" % l, [128, 44, 2])))
    stS = dict(sr=kb.sb("srS", [128, 8, NSQ]), si=kb.sb("siS", [128, 8, NSQ]), lh=kb.sb("lhS", [128, 2, NSQ]),
               lc=kb.sb("lcS", [128, 2, NSQ * 3]), fc=kb.sb("fcS", [128, 44, NSQ * 2]))
    class XAlias:
        def __init__(self, kc0):
            self.kc0 = kc0

        def __getitem__(self, idx):
            p_, s_, f_ = idx
            kc = self.kc0 + s_ // 4
            v = X[:, kc, 64 + (s_ % 4) * 64: 64 + (s_ % 4 + 1) * 64]
            return V(v.ap.bitcast(BF16)[p_, f_], v.bufs)

    KCT = XAlias(0)
    VCT = XAlias(4)

    ident = kb.cst("ident")
    ones = kb.cst("ones")
    blk = kb.cst("blk")
    rotm = kb.cst("rot")

    kb.dma(kb.CST[:], cstd[:, :])
    kb.dma(kb.PP[:], ppd[:, :])
    kb.memset("dve", kb.EPSC[:, 0:1], EPS)
    kb.memset("dve", kb.EPSC[:, 1:2], math.pi / 2)
    kb.memset("dve", kb.EPSC[:, 2:3], 1.0)
    kb.memset("dve", kb.EPSC[:, 3:4], 0.0)
    kb.memset("pool", kb.ZERO[:], 0.0)
    kb.memset("pool", ONEB[:], 1.0)
    kb.cp("dve", BLKB[:], kb.cst("blk"))
    kb.memset("pool", KRA[:], 0.0)
    kb.memset("pool", KRB[:], 0.0)
    kb.dma(MASK4[:], mskd[:, 0:512], eng="pool")
    kb.dma(MASKC[:], mskd[:, 512:1024], eng="pool")
    kb.dma(MASKN[:], mskd[0:64, 1024:1536], eng="pool")
    for l in range(NL):
        for nm in ("sr", "si", "lh", "lc", "fc", "kprevA", "kprevB"):
            kb.memset("pool", stP[l][nm][:], 0.0)

    def emit_casts(l, part=None):
        def P(n):
            return part is None or part == n
        if P(0):
            for c in range(4):
                for two in range(2):
                    kb.dma(win_s[l, c, :, :, two * 64: two * 64 + 64],
                           win[l, :, two * 256 + c * 64: two * 256 + c * 64 + 64].re("(kc p) e -> p kc e", p=128), eng="pool")
            for ci in range(4, 12):
                kb.dma(win_s[l, ci, :, :, :], win[l, :, 512 + (ci - 4) * 128: 512 + (ci - 3) * 128].re("(kc p) e -> p kc e", p=128), eng="pool")
            kb.dma(sw_s[l, :, :], swd[l, :, :], eng="pool", max_dma_last_dim=4096)
        if P(1):
            for kc in range(4):
                for two in range(2):
                    kb.dma(wo_s[l, :, two * 64:(two + 1) * 64, kc, :].re("m p e -> p m e"),
                           wo[l, two * 256 + kc * 64: two * 256 + kc * 64 + 64, :].re("p (m e) -> p m e", e=128), eng="pool")
            for kc in range(4, 8):
                kb.dma(wo_s[l, :, :, kc, :].re("m p e -> p m e"),
                       wo[l, kc * 128:(kc + 1) * 128, :].re("p (m e) -> p m e", e=128), eng="pool")
            for kc in range(0, 3):
                rows = slice(kc * 128, (kc + 1) * 128)
                for two in range(2):
                    kb.dma(wup_s[l, :, :, kc, two * 128:(two + 1) * 128].re("j p e -> p j e"),
                           wup[l, rows, two * DFF:(two + 1) * DFF].re("p (j e) -> p j e", e=128), eng="pool")
        if P(2):
            for kc in range(3, 8):
                rows = slice(kc * 128, (kc + 1) * 128)
                for two in range(2):
                    kb.dma(wup_s[l, :, :, kc, two * 128:(two + 1) * 128].re("j p e -> p j e"),
                           wup[l, rows, two * DFF:(two + 1) * DFF].re("p (j e) -> p j e", e=128), eng="pool")
        if P(3):
            for j in range(NJ):
                kb.dma(wdn_s[l, :, :, j, :].re("m p e -> p m e"),
                       wdn[l, j * 128:(j + 1) * 128, :].re("p (m e) -> p m e", e=128), eng="pool")

    emit_casts(0)

    PW = kb.sb("PW", [128, 16, 8])
    PWL = kb.sb("PWL", [128, NL, 4, 8])
    TABT = kb.sb("TABT", [128, 4, TP])
    TB = TABT
    TBS = kb.sb("TBS", [128, 4, TS])
    for l in range(NL):
        kb.act(SINKE[:, l, :], kb.pcol("sk", l, 0, 4), AF.Exp)
        ex = PW[:, 15, 0:2]
        kb.act(ex, kb.pcol("lam", l, 0, 2), AF.Exp, scale=-1.0)
        kb.act(ex, ex, AF.Ln, bias=kb.EPSC[:, 2:3])
        kb.ts("dve", NSP8[:, l, :], ex, -8.0, ALU.mult)
        kb.ts("dve", NSP16[:, l, :], ex, -16.0, ALU.mult)
        dt, th, c1, s1, t0, t1, nr, ni, dn, cr, ci = [PW[:, i, :] for i in range(11)]
        are = kb.pcol("are", l, 0, 8)
        aim = kb.pcol("aim", l, 0, 8)
        kb.act(dt, kb.pcol("ldt", l, 0, 8), AF.Exp)
        kb.tt("dve", t0, are, dt, ALU.mult)
        kb.act(RHO[:, l, :], t0, AF.Exp)
        kb.tt("dve", th, aim, dt, ALU.mult)
        kb.act(s1, th, AF.Sin, scale=1.0 / 16)
        kb.act(c1, th, AF.Sin, scale=1.0 / 16, bias=kb.EPSC[:, 1:2])
        for _ in range(4):
            kb.tt("dve", t0, c1, c1, ALU.mult)
            kb.tt("dve", t1, s1, s1, ALU.mult)
            kb.tt("dve", s1, s1, c1, ALU.mult)
            kb.ts("dve", s1, s1, 2.0, ALU.mult)
            kb.tt("dve", c1, t0, t1, ALU.subtract)
        kb.tt("dve", nr, RHO[:, l, :], c1, ALU.mult)
        kb.ts("dve", nr, nr, -1.0, ALU.add)
        kb.tt("dve", ni, RHO[:, l, :], s1, ALU.mult)
        kb.tt("dve", t0, are, are, ALU.mult)
        kb.tt("dve", t1, aim, aim, ALU.mult)
        kb.tt("dve", dn, t0, t1, ALU.add)
        kb.recip(dn, dn)
        kb.tt("dve", t0, nr, are, ALU.mult)
        kb.tt("dve", t1, ni, aim, ALU.mult)
        kb.tt("dve", cr, t0, t1, ALU.add)
        kb.tt("dve", cr, cr, dn, ALU.mult)
        kb.tt("dve", t0, ni, are, ALU.mult)
        kb.tt("dve", t1, nr, aim, ALU.mult)
        kb.tt("dve", ci, t0, t1, ALU.subtract)
        kb.tt("dve", ci, ci, dn, ALU.mult)
        for i_, src_ in enumerate((c1, s1, cr, ci)):
            kb.cp("dve", PWL[:, l, i_, :], src_)
    C17 = kb.sb("C17", [128, 8, 17])
    for s in range(17):
        kb.dma(C17[:, :, s], c17[s, :].re("(kc p) -> p kc", p=128), nc_ok=True)
    SC17 = kb.sb("SC17", [128, 8, 17])
    kb.act(SC17[:], C17[:], AF.Silu)
    wav = PROJ.t[:, 0:8, :].rearrange("p a (b n) -> p (a b) n", b=2)
    kb.rot["WA"] = ([T(wav[:, 0:8, :], "WA0"), T(wav[:, 8:16, :], "WA1")], 0)
    MODT = kb.sb("MODT", [128, 48, 17])
    def tab_gen():
        for l in range(NL):
            c1, s1, cr, ci = [PWL[:, l, i_, :] for i_ in range(4)]
            for j in range(8):
                cosk = TB[:, 2, :]
                sink = TB[:, 3, :]
                kb.cp("dve", TB[:, 2, 0:1], c1[:, j:j + 1])
                kb.cp("dve", TB[:, 3, 0:1], s1[:, j:j + 1])
                n = 1
                while n < TP:
                    cn = TB[:, 2, n - 1:n]
                    sn = TB[:, 3, n - 1:n]
                    w0 = kb.nxt("W")
                    w1 = kb.nxt("W")
                    kb.ts("dve", w0[:, 0:n], TB[:, 3, 0:n], sn, ALU.mult)
                    kb.ts("dve", w1[:, 0:n], TB[:, 2, 0:n], sn, ALU.mult)
                    kb.stt(TB[:, 2, n:2 * n], TB[:, 2, 0:n], cn, w0[:, 0:n], ALU.mult, ALU.subtract)
                    kb.stt(TB[:, 3, n:2 * n], TB[:, 3, 0:n], cn, w1[:, 0:n], ALU.mult, ALU.add)
                    n *= 2
                w0 = kb.nxt("W")
                kb.ts("dve", w0[:], sink, ci[:, j:j + 1], ALU.mult)
                kb.stt(TB[:, 0, :], cosk, cr[:, j:j + 1], w0[:], ALU.mult, ALU.add)
                w1 = kb.nxt("W")
                kb.ts("dve", w1[:], sink, cr[:, j:j + 1], ALU.mult)
                kb.stt(TB[:, 1, :], cosk, ci[:, j:j + 1], w1[:], ALU.mult, ALU.subtract)
                kb.dma(tabp[l, j, :, :, :], TB[:])
                for s in range(NSQ):
                    kb.cp("dve", TBS[:, :, s * 4:(s + 1) * 4], TB[:, :, 0:4])
                kb.dma(tabs[l, j, :, :, :], TBS[:])
                yield

    def ada_gen():
        for l in range(NL):
            for m2 in range(24):
                wa = kb.nxt("WA")
                kb.dma(wa[:], wada[l, :, m2 * 256:(m2 + 1) * 256].re("(kc p) n -> p kc n", p=128))
                for mi in range(2):
                    m = m2 * 2 + mi
                    b = kb.bank()
                    for kc in range(8):
                        kb.mm(b[:, 0:17], wa[:, kc, mi * 128:(mi + 1) * 128], SC17[:, kc, :], start=kc == 0, stop=kc == 7)
                    kb.act(MODT[:, m, :], b[:, 0:17], AF.Identity, bias=kb.pcol("bada", l, m))
                    yield
            for kc in range(8):
                kb.cp("act", MODB1[:, l, kc, :], MODT[:, 0 + kc, :])
                kb.ts("dve", MODA1[:, l, kc, :], MODT[:, 8 + kc, :], 1.0, ALU.add, kb.pcol("n1", l, kc), ALU.mult)
                kb.cp("act", MODG1[:, l, kc, :], MODT[:, 16 + kc, :])
                kb.cp("act", MODB2[:, l, kc, :], MODT[:, 24 + kc, :])
                kb.ts("dve", MODA2[:, l, kc, :], MODT[:, 32 + kc, :], 1.0, ALU.add, kb.pcol("n2", l, kc), ALU.mult)
                kb.cp("act", MODG2[:, l, kc, :], MODT[:, 40 + kc, :])


    def _nx(g):
        try:
            next(g)
            return True
        except StopIteration:
            return False

    gt_, ga_ = tab_gen(), ada_gen()
    at_ = aa_ = True
    while at_ or aa_:
        for _ in range(6):
            if aa_:
                aa_ = _nx(ga_)
        if at_:
            at_ = _nx(gt_)

    wa_l = kb.rot["WA"][0]
    import os
    STAGE = int(os.environ.get("KSTAGE", "99"))
    segs = [("s", 0)] + [("p", k) for k in range(NSEG)]
    segs = segs[:STAGE]
    if STAGE == 0:
        kb.S.emit()
        return
    it_win, it_wo, it_sw, it_up, it_dn, it_tab = [], [], [], [], [], []
    for kind, k in segs:
        for l in range(NL):
            for ci in [0, 1, 2, 3, 4, 6, 7, 8, 9, 10, 11, 5]:
                it_win.append(win_s[l, ci, :, :, :])
            for mo in range(8):
                it_wo.append(wo_s[l, mo, :, :, :])
            it_sw.append(sw_s[l, :, :])
            for j in range(NJ):
                for half in range(2):
                    it_up.append(wup_s[l, j, :, :, half * 128:(half + 1) * 128])
            for m in range(8):
                for hj in range(2):
                    it_dn.append(wdn_s[l, m, :, hj * 11:(hj + 1) * 11, :])
            for j in range(8):
                it_tab.append(tabp[l, j, :, :, :] if kind == "p" else tabs[l, j, :, :, :])
    sWIN = Stream(kb, "WIN", 4, [128, 8, 128], BF16, it_win)
    sWO = Stream(kb, "WO", 3, [128, 8, 128], BF16, it_wo)
    sSW = Stream(kb, "SWB", 1, [128, NSW], BF16, it_sw)
    sUP = Stream(kb, "WUP", 6, [128, 8, 128], BF16, it_up)
    sDN = Stream(kb, "WDN", 3, [128, 11, 128], BF16, it_dn)
    tab_tiles = [TABT]
    tab_i = [0]

    def tab_load(idx):
        if idx < len(it_tab):
            src = it_tab[idx]
            tn = TS if idx < NL * 8 else TP
            kb.dma(tab_tiles[0][:, :, 0:tn], src)

    kb.memset("dve", V(PROJ.t[:, 0:8, 0:1], PROJ.bufs[0:8] + wa_l[0].bufs + wa_l[1].bufs), 0.0)
    for s_ in (sWIN, sWO, sSW, sUP, sDN):
        s_.start()
    tab_load(0)

    SUB = float(os.environ.get("KSUB", "99"))

    class _Stop(Exception):
        pass

    def chk(n):
        if n > SUB:
            raise _Stop()

    try:
        _main(kb, locals())
    except _Stop:
        pass
    kb.S.emit()


def _main(kb, env):
    globals().update({k_: v_ for k_, v_ in env.items() if k_ not in ("kb",)})
    for kind, k in segs:
        Tn = TS if kind == "s" else TP
        nseq = NSQ if kind == "s" else 1
        L = Tn // nseq
        ntb = (Tn + 127) // 128
        tbs = min(128, Tn)

        def v3(v):
            return v.re("p (s l) -> p s l", l=L)

        def sidx(s):
            return 0 if kind == "p" else 1 + s

        PL = "dve" if kind == "s" else "pool"

        src = xs if kind == "s" else xp
        r0 = 0 if kind == "s" else k * TP
        for tb in range(ntb):
            xt = kb.nxt("XT")
            kb.dma(xt[:tbs, :], src[r0 + tb * 128: r0 + tb * 128 + tbs, :])
            for half in range(2):
                b = kb.bank()
                for q4 in range(4):
                    kc = half * 4 + q4
                    kb.tr(b[:, q4 * 128: q4 * 128 + tbs], xt[:tbs, kc * 128:(kc + 1) * 128],
                          kb.cst("ident", tbs, slice(0, tbs)))
                kb.cp("act", X[:, half * 4:half * 4 + 4, tb * 128: tb * 128 + tbs],
                      b[:, :].re("p (q t) -> p q t", t=128)[:, :, 0:tbs])
        if kind == "s":
            kb.cp(PL, CS[:, :Tn], kb.cst("coss", 64))
            kb.cp(PL, SN[:, :Tn], kb.cst("sins", 64))
        else:
            kb.dma(CS[:], cosP[:, k * TP:(k + 1) * TP])
            kb.dma(SN[:], sinP[:, k * TP:(k + 1) * TP])

        for l in range(NL):
            S_ = stS if kind == "s" else stP[l]
            last = (kind == "s") or (k == NSEG - 1)

            def modulate(dst_bf, MA, MB, nrm_name):
                b = kb.bank()
                for kc in range(8):
                    sq = kb.nxt("SQ")
                    kb.act(sq[:, :Tn], X[:, kc, :Tn], AF.Square)
                    kb.mm(b[:, :Tn], ONEB[:, :], sq[:, :Tn], start=kc == 0, stop=kc == 7)
                rs = kb.rstd_from(b, Tn, 1.0 / D)
                for kc in range(8):
                    w = kb.nxt("W")
                    kb.tt("dve", w[:, :Tn], X[:, kc, :Tn], rs[:, :Tn], ALU.mult)
                    for s in range(nseq):
                        cs_ = slice(s * L, (s + 1) * L)
                        kb.act(dst_bf[:, kc, cs_], w[:, cs_], AF.Identity,
                               scale=MA[:, l, kc, sidx(s):sidx(s) + 1], bias=MB[:, l, kc, sidx(s):sidx(s) + 1])

            def resid(bank_, mo, MG):
                for s in range(nseq):
                    cs_ = slice(s * L, (s + 1) * L)
                    kb.stt(X[:, mo, cs_], bank_[:, cs_], MG[:, l, mo, sidx(s):sidx(s) + 1], X[:, mo, cs_],
                           ALU.mult, ALU.add)

            def outnorm(src_chunks, inv_n, o0):
                b = kb.bank()
                n = len(src_chunks)
                for i, sc in enumerate(src_chunks):
                    sq = kb.nxt("SQ")
                    kb.act(sq[:, :Tn], sc, AF.Square)
                    kb.mm(b[:, :Tn], ONEB[:, :], sq[:, :Tn], start=i == 0, stop=i == n - 1)
                rs = kb.rstd_from(b, Tn, inv_n)
                for i, sc in enumerate(src_chunks):
                    kb.stt(Ob[:, o0 + i, :Tn], sc, kb.pcol("on", l, o0 + i), rs[:, :Tn], ALU.mult, ALU.mult)

            if kind == "s":
                if l + 1 < NL:
                    emit_casts(l + 1, 0)
                for s in range(NSQ):
                    stg = kb.nxt("STG")
                    kb.dma(stg[:], ck[l, s, :, :])
                    b = kb.bank()
                    kb.tr(b[:, 0:128], stg[:], ident)
                    kb.cp("act", KCT[:, s, :], b[:, 0:128])
                    stg2 = kb.nxt("STG")
                    kb.dma(stg2[:], cv[l, s, :, :])
                    kb.cp(PL, VCT[:, s, :], stg2[:])
                    kb.dma(osk[l, s, 0:124, :], ck[l, s, 4:128, :], eng="pool")
                    kb.dma(osv[l, s, 0:124, :], cv[l, s, 4:128, :], eng="pool")
                for j in range(8):
                    kb.load_T(S_["sr"][:, j, :], sre[l, :, j * 128:(j + 1) * 128], NSQ)
                    kb.load_T(S_["si"][:, j, :], sim[l, :, j * 128:(j + 1) * 128], NSQ)
                for m in range(2):
                    kb.load_T(S_["lh"][:, m, :], slh[l, :, m * 128:(m + 1) * 128], NSQ)
                    kb.load_T(S_["lc"][:, m, :], slc[l, :, m * 128:(m + 1) * 128], NSQ * 3)
                for ch in range(44):
                    kb.load_T(S_["fc"][:, ch, :], sfc[l, :, ch * 128:(ch + 1) * 128], NSQ * 2)

            chk(1)
            modulate(H, MODA1, MODB1, "n1")
            chk(2)

            for i in range(11):
                WIN = sWIN.get()
                b = kb.bank()
                for kc in range(8):
                    kb.mm(b[:, :Tn], WIN[:, kc, :], H[:, kc, :Tn], start=kc == 0, stop=kc == 7)
                kb.cp("act", PROJ[:, i, :Tn], b[:, :Tn])
                sWIN.done()
            WIN = sWIN.get()
            for tb in range(ntb):
                b = kb.bank()
                for kc in range(8):
                    kb.mm(b[:tbs, 0:128], H[:, kc, tb * 128: tb * 128 + tbs], WIN[:, kc, :],
                          start=kc == 0, stop=kc == 7)
                kb.cp("act", VT[:tbs, tb, :], b[:tbs, 0:128])
                if tb == ntb - 1:
                    kb.cp("act", VF[:tbs, :], b[:tbs, 0:128])
            sWIN.done()

            chk(3)
            def rope(srcv, wname, out_bf, out_f):
                sq = kb.nxt("SQ")
                kb.act(sq[:, :Tn], srcv, AF.Square)
                b = kb.bank()
                kb.mm(b[:, :Tn], BLKB[:, :], sq[:, :Tn])
                yield
                rs = kb.rstd_from(b, Tn, 1.0 / 64)
                yield
                qn = kb.nxt("W")
                kb.stt(qn[:, :Tn], srcv, kb.pcol(wname, l), rs[:, :Tn], ALU.mult, ALU.mult)
                b2 = kb.bank()
                kb.mm(b2[:, :Tn], rotm, qn[:, :Tn])
                yield
                t1 = kb.nxt("W")
                t2 = kb.nxt("W")
                kb.tt("dve", t1[:, :Tn], b2[:, :Tn], SN[:, :Tn], ALU.mult)
                kb.tt(PL, t2[:, :Tn], qn[:, :Tn], CS[:, :Tn], ALU.mult)
                yield
                if isinstance(out_bf, list):
                    for R_, o_ in out_bf:
                        kb.tt("dve", o_, t1[R_, :Tn], t2[R_, :Tn], ALU.add)
                else:
                    kb.tt("dve", out_bf, t1[:, :Tn], t2[:, :Tn], ALU.add)
                if out_f is not None:
                    kb.tt(PL, out_f, t1[:, :Tn], t2[:, :Tn], ALU.add)

            def lockstep(gens):
                gens = list(gens)
                while gens:
                    for g in list(gens):
                        try:
                            next(g)
                        except StopIteration:
                            gens.remove(g)

            lockstep([rope(PROJ[:, 0, :Tn], "qn", QR[:, 0, :Tn], None), rope(PROJ[:, 1, :Tn], "qn", QR[:, 1, :Tn], None)])
            lockstep([rope(PROJ[:, 2, :Tn], "qn", QR[:, 2, :Tn], None), rope(PROJ[:, 3, :Tn], "qn", QR[:, 3, :Tn], None)])
            lockstep([rope(PROJ[:, 4, :Tn], "kn", [(slice(0, 64), KRA[0:64, :Tn]), (slice(64, 128), KRB[64:128, :Tn])], KF[:, :Tn])])

            chk(4)
            if kind == "s" and l + 1 < NL:
                emit_casts(l + 1, 1)
            def ssm_tail():
                for m in range(2):
                    b = kb.ps[6 + m]
                    y = kb.nxt("W")
                    kb.stt(y[:, :Tn], PROJ[:, 5 + m, :Tn], kb.pcol("sd", l, m), b[:, :Tn], ALU.mult, ALU.add)
                    kb.act(GLF[:, m, :Tn], y[:, :Tn], AF.Gelu_apprx_tanh)
                    kb.cp("act", GLB[:, m, :Tn], GLF[:, m, :Tn])
                for mo in range(2):
                    b = kb.bank()
                    for kc in range(2):
                        o_ = SW_OFF["glu"] + kc * 256 + mo * 128
                        kb.mm(b[:, :Tn], SWB[:, o_:o_ + 128], GLB[:, kc, :Tn], start=kc == 0, stop=kc == 1)
                    sg = kb.nxt("W")
                    kb.act(sg[:, :Tn], b[:, :Tn], AF.Sigmoid, bias=kb.pcol("bglu", l, mo))
                    kb.tt("dve", OS[:, mo, :Tn], GLF[:, mo, :Tn], sg[:, :Tn], ALU.mult)
                outnorm([OS[:, m, :Tn] for m in range(2)], 1.0 / 256, 4)

            def ssm_items():
                for j in range(8):
                    m = j // 4
                    br = kb.bank()
                    bi = kb.bank()
                    kb.mm(br[:, :Tn], SWB[:, SW_OFF["bre"] + j * 128: SW_OFF["bre"] + (j + 1) * 128], UB[:, m, :Tn])
                    kb.mm(bi[:, :Tn], SWB[:, SW_OFF["bim"] + j * 128: SW_OFF["bim"] + (j + 1) * 128], UB[:, m, :Tn])
                    TABt = tab_tiles[0]
                    er, ei, ckk, skk = [TABt[:, i, :Tn] for i in range(4)]
                    a1 = kb.nxt("W"); a2 = kb.nxt("W")
                    kb.tt("dve", a1[:, :Tn], br[:, :Tn], er, ALU.mult)
                    kb.tt("dve", a2[:, :Tn], bi[:, :Tn], ei, ALU.mult)
                    gr = kb.nxt("W")
                    kb.tt("dve", gr[:, :Tn], a1[:, :Tn], a2[:, :Tn], ALU.subtract)
                    a3 = kb.nxt("W"); a4 = kb.nxt("W")
                    kb.tt("dve", a3[:, :Tn], br[:, :Tn], ei, ALU.mult)
                    kb.tt("dve", a4[:, :Tn], bi[:, :Tn], er, ALU.mult)
                    gi = kb.nxt("W")
                    kb.tt("dve", gi[:, :Tn], a3[:, :Tn], a4[:, :Tn], ALU.add)
                    rho = RHO[:, l, j:j + 1]
                    kb.stt(v3(gr[:, :Tn])[:, :, 0], S_["sr"][:, j, :], rho, v3(gr[:, :Tn])[:, :, 0], ALU.mult, ALU.add)
                    kb.stt(v3(gi[:, :Tn])[:, :, 0], S_["si"][:, j, :], rho, v3(gi[:, :Tn])[:, :, 0], ALU.mult, ALU.add)
                    dec = kb.nxt("W")
                    kb.act(dec[:, :Tn], kb.ZERO[:, :Tn], AF.Identity, bias=rho)
                    kb.memset("dve", v3(dec[:, :Tn])[:, :, 0], 0.0)
                    g2r = a1
                    g2i = a2
                    kb.scan(g2r[:, :Tn], dec[:, :Tn], gr[:, :Tn])
                    kb.scan(g2i[:, :Tn], dec[:, :Tn], gi[:, :Tn])
                    b1 = a3; b2_ = a4; b3 = gr; b4 = gi
                    kb.tt("dve", b1[:, :Tn], g2r[:, :Tn], ckk, ALU.mult)
                    kb.tt("dve", b2_[:, :Tn], g2i[:, :Tn], skk, ALU.mult)
                    hr_ = kb.nxt("HR")
                    hi_ = kb.nxt("HI")
                    kb.tt("dve", hr_[:, :Tn], b1[:, :Tn], b2_[:, :Tn], ALU.subtract)
                    kb.tt(PL, S_["sr"][:, j, :], v3(b1[:, :Tn])[:, :, L - 1], v3(b2_[:, :Tn])[:, :, L - 1], ALU.subtract)
                    kb.tt("dve", b3[:, :Tn], g2i[:, :Tn], ckk, ALU.mult)
                    kb.tt("dve", b4[:, :Tn], g2r[:, :Tn], skk, ALU.mult)
                    kb.stt(hi_[:, :Tn], b3[:, :Tn], -1.0, b4[:, :Tn], ALU.mult, ALU.subtract)
                    yb = kb.ps[6 + m]
                    jj = j % 4
                    kb.mm(yb[:, :Tn], SWB[:, SW_OFF["cre"] + j * 128: SW_OFF["cre"] + (j + 1) * 128], hr_[:, :Tn],
                          start=jj == 0, stop=False)
                    kb.mm(yb[:, :Tn], SWB[:, SW_OFF["cim"] + j * 128: SW_OFF["cim"] + (j + 1) * 128], hi_[:, :Tn],
                          start=False, stop=jj == 3)
                    kb.tt(PL, S_["si"][:, j, :], v3(b3[:, :Tn])[:, :, L - 1], v3(b4[:, :Tn])[:, :, L - 1], ALU.add)
                    tab_load(tab_i[0] + 1)
                    tab_i[0] += 1
                    if last:
                        dre = (osre if kind == "s" else pre)
                        dim_ = (osim if kind == "s" else pim)
                        kb.store_T(dre[l, :, j * 128:(j + 1) * 128], S_["sr"][:, j, :], nseq)
                        kb.store_T(dim_[l, :, j * 128:(j + 1) * 128], S_["si"][:, j, :], nseq)
                    yield
                ssm_tail()
                yield

            def lru_items():
                for m in range(2):
                    xe = XE[:, 0:nseq * (3 + L)].re("p (s l) -> p s l", l=3 + L)
                    kb.cp(PL, xe[:, :, 0:3], S_["lc"][:, m, :].re("p (s t) -> p s t", t=3))
                    kb.cp("act", xe[:, :, 3:3 + L], v3(PROJ[:, 7 + m, :Tn]))
                    xc3 = v3(XC[:, m, :Tn])
                    kb.act(xc3, xe[:, :, 3:3 + L], AF.Identity, scale=kb.pcol("lcw", l, m * 4 + 3), bias=kb.pcol("lcb", l, m))
                    for tap in (2, 1, 0):
                        kb.stt(xc3, xe[:, :, tap:tap + L], kb.pcol("lcw", l, m * 4 + tap), xc3, ALU.mult, ALU.add)
                    kb.cp(PL, S_["lc"][:, m, :].re("p (s t) -> p s t", t=3), xe[:, :, L:L + 3])
                    kb.cp(PL, XCB[:, m, :Tn], XC[:, m, :Tn])
                    if last:
                        dlc = oslc if kind == "s" else plc
                        kb.store_T(dlc[l, :, m * 128:(m + 1) * 128], S_["lc"][:, m, :], nseq * 3)
                    yield
                for m in range(2):
                    b = kb.bank()
                    kb.mm(b[:, :Tn], SWB[:, SW_OFF["wa"] + m * 128: SW_OFF["wa"] + (m + 1) * 128], XCB[:, m, :Tn])
                    r_ = kb.nxt("W")
                    kb.act(r_[:, :Tn], b[:, :Tn], AF.Sigmoid, bias=kb.pcol("ba", l, m))
                    b = kb.bank()
                    kb.mm(b[:, :Tn], SWB[:, SW_OFF["wi"] + m * 128: SW_OFF["wi"] + (m + 1) * 128], XCB[:, m, :Tn])
                    gi_ = kb.nxt("W")
                    kb.act(gi_[:, :Tn], b[:, :Tn], AF.Sigmoid, bias=kb.pcol("bi", l, m))
                    a_ = kb.nxt("W")
                    kb.act(a_[:, :Tn], r_[:, :Tn], AF.Exp, scale=NSP8[:, l, m:m + 1])
                    a2_ = kb.nxt("W")
                    kb.act(a2_[:, :Tn], r_[:, :Tn], AF.Exp, scale=NSP16[:, l, m:m + 1])
                    kb.ts("dve", a2_[:, :Tn], a2_[:, :Tn], 1.0, ALU.min)
                    mu = r_
                    kb.act(mu[:, :Tn], a2_[:, :Tn], AF.Sqrt, scale=-1.0, bias=kb.EPSC[:, 2:3])
                    bb = a2_
                    kb.tt("dve", bb[:, :Tn], mu[:, :Tn], gi_[:, :Tn], ALU.mult)
                    kb.tt("dve", bb[:, :Tn], bb[:, :Tn], XC[:, m, :Tn], ALU.mult)
                    if kind == "p" and k == 0:
                        kb.tt("dve", bb[:, 0:1], gi_[:, 0:1], XC[:, m, 0:1], ALU.mult)
                    tmp = kb.nxt("W")
                    kb.tt("dve", tmp[:, 0:nseq], v3(a_[:, :Tn])[:, :, 0], S_["lh"][:, m, :], ALU.mult)
                    kb.tt("dve", v3(bb[:, :Tn])[:, :, 0], v3(bb[:, :Tn])[:, :, 0], tmp[:, 0:nseq], ALU.add)
                    kb.memset(PL, v3(a_[:, :Tn])[:, :, 0], 0.0)
                    hh_ = gi_
                    kb.scan(hh_[:, :Tn], a_[:, :Tn], bb[:, :Tn])
                    kb.cp(PL, S_["lh"][:, m, :], v3(hh_[:, :Tn])[:, :, L - 1])
                    gy = kb.nxt("W")
                    kb.act(gy[:, :Tn], PROJ[:, 9 + m, :Tn], AF.Gelu_apprx_tanh)
                    kb.tt("dve", OL[:, m, :Tn], hh_[:, :Tn], gy[:, :Tn], ALU.mult)
                    if last:
                        dlh = oslh if kind == "s" else plh
                        kb.store_T(dlh[l, :, m * 128:(m + 1) * 128], S_["lh"][:, m, :], nseq)
                    yield
                outnorm([OL[:, m, :Tn] for m in range(2)], 1.0 / 256, 6)
                yield

            def step(gen, pool):
                kb.cur_pool = pool
                try:
                    next(gen)
                    alive = True
                except StopIteration:
                    alive = False
                kb.cur_pool = None
                return alive

            SWB = sSW.get()
            kb.cp(PL, UB[:, :, :Tn], PROJ[:, 5:7, :Tn])
            if kind == "p":
                def scores(qb, c):
                    first = (k == 0 and qb == 0)
                    qs = slice(qb * 128, (qb + 1) * 128)
                    bsc_ = kb.bank()
                    whiches = (["prev"] if not first else []) + ["diag"]
                    col = 0
                    for wch in whiches:
                        for half in range(2):
                            KRh = KRA if half == 0 else KRB
                            if wch == "prev":
                                lhs = S_["kprevA" if half == 0 else "kprevB"][:, :] if qb == 0 else KRh[:, (qb - 1) * 128: qb * 128]
                            else:
                                lhs = KRh[:, qs]
                            kb.mm(bsc_[:, col:col + 128], lhs, QR[:, c, qs])
                            col += 128
                    return bsc_, whiches, col

                def rest1(qb, c, sc):
                    bsc_, whiches, col = sc
                    qs = slice(qb * 128, (qb + 1) * 128)
                    n = col
                    E = kb.nxt("E")
                    kb.act(E[:, :n], bsc_[:, :n], AF.Exp, scale=0.125)
                    kb.tt(PL, E[:, :n], E[:, :n], MASK4[:, 512 - n:512], ALU.mult)
                    b2 = kb.bank()
                    nk = n // 256
                    for i in range(nk):
                        if whiches[i] == "prev":
                            vt = S_["vprev"][:, :] if qb == 0 else VT[:, qb - 1, :]
                        else:
                            vt = VT[:, qb, :]
                        kb.mm(b2[:, 0:256], vt, E[:, i * 256:(i + 1) * 256], start=i == 0, stop=i == nk - 1)
                    for i in range(nk):
                        kb.mm(b2[:, 256:512], ONEB[:, :], E[:, i * 256:(i + 1) * 256], start=i == 0, stop=i == nk - 1)
                    return b2

                def rest2(qb, c, b2):
                    qs = slice(qb * 128, (qb + 1) * 128)
                    for half in range(2):
                        R = slice(half * 64, half * 64 + 64)
                        cc = half * 128
                        kb.act(DEN[R, 0:128], b2[R, 256 + cc:256 + cc + 128], AF.Identity, bias=SINKE[R, l, c:c + 1])
                    kb.recip(DEN[:, 0:128], DEN[:, 0:128])
                    for half in range(2):
                        R = slice(half * 64, half * 64 + 64)
                        cc = half * 128
                        kb.tt("dve", OA[R, c, qs], b2[R, cc:cc + 128], DEN[R, 0:128], ALU.mult)

                def attn_items():
                    work = [(qb, c) for qb in range(4) for c in range(4)]
                    pend = scores(*work[0])
                    for wi, (qb, c) in enumerate(work):
                        nxt_sc = scores(*work[wi + 1]) if wi + 1 < len(work) else None
                        b2_ = rest1(qb, c, pend)
                        pend = nxt_sc
                        yield
                        rest2(qb, c, b2_)
                        yield
                    kb.cp(PL, S_["kprevA"][:, :], KRA[:, 384:512])
                    kb.cp(PL, S_["kprevB"][:, :], KRB[:, 384:512])
                    kb.cp(PL, S_["vprev"][:, :], VT[:, 3, :])
                    if last:
                        kb.store_T(pk[l, :, :], KF[:, 384:512], 128)
                        kb.dma(pv[l, :, :], VF[:, :])
                    outnorm([OA[:, c, :Tn] for c in range(4)], 1.0 / 512, 0)
                    yield

                ga, gs, gl = attn_items(), ssm_items(), lru_items()
                for i_ in range(33):
                    step(ga, "attn")
                    if i_ % 2 == 0:
                        step(gs, "ssm")
                    elif (i_ // 2) % 2 == 0:
                        step(gl, "lru")
                while step(ga, "attn"):
                    pass
                while step(gs, "ssm"):
                    pass
                while step(gl, "lru"):
                    pass
            else:
                bsc = (kb.bank(), kb.bank())
                for c in range(4):
                    for half in range(2):
                        R = slice(half * 64, half * 64 + 64)
                        hh = c * 2 + half
                        for s in range(NSQ):
                            kb.mm(bsc[half][:, hh * 64 + s * 4: hh * 64 + s * 4 + 4], KCT[R, s, :], QR[R, c, s * 4:(s + 1) * 4])
                Ec = kb.nxt("E")
                for half in range(2):
                    kb.act(Ec[:, :].re("p (c h x) -> p c h x", c=4, h=2)[:, :, half, :],
                           bsc[half][:, :].re("p (c h x) -> p c h x", c=4, h=2)[:, :, half, :], AF.Exp, scale=0.125)
                kb.tt("dve", Ec[:, :], Ec[:, :], MASKC[:, :], ALU.mult)
                chk(4.1)
                bsn = (kb.bank(), kb.bank())
                for c in range(4):
                    for half in range(2):
                        R = slice(half * 64, half * 64 + 64)
                        hh = c * 2 + half
                        kb.mm(bsn[half][0:64, hh * 64:(hh + 1) * 64], (KRA if half == 0 else KRB)[R, 0:64], QR[R, c, 0:64])
                En = kb.nxt("E")
                for half in range(2):
                    kb.act(En[0:64, :].re("p (c h x) -> p c h x", c=4, h=2)[:, :, half, :],
                           bsn[half][0:64, :].re("p (c h x) -> p c h x", c=4, h=2)[:, :, half, :], AF.Exp, scale=0.125)
                kb.tt("dve", En[0:64, :], En[0:64, :], MASKN[:, :], ALU.mult)
                chk(4.2)
                bnc = kb.bank()
                Ec4 = Ec[:, :].re("p (h s t) -> p h s t", h=8, s=NSQ)
                for s in range(NSQ):
                    kb.mm(bnc[:, s * 32:(s + 1) * 32].re("p (h t) -> p h t", h=8), VCT[:, s, :], Ec4[:, :, s, :])
                NUMS = kb.nxt("W")
                kb.cp("act", NUMS[:, :], bnc[:, :])
                chk(4.3)
                bnn = kb.bank()
                kb.mm(bnn[:, :], VT[0:64, 0, :], En[0:64, :])
                bdn = kb.bank()
                kb.mm(bdn[:, :], ONEB[:, :], Ec[:, :], start=True, stop=False)
                kb.mm(bdn[:, :], ONEB[0:64, :], En[0:64, :], start=False, stop=True)
                chk(4.4)
                for c in range(4):
                    for half in range(2):
                        R = slice(half * 64, half * 64 + 64)
                        hh = c * 2 + half
                        hc = slice(hh * 64, (hh + 1) * 64)
                        kb.ts("dve", DEN[R, hc], bdn[R, hc], SINKE[R, l, c:c + 1], ALU.add)
                        kb.recip(DEN[R, hc], DEN[R, hc])
                        w = kb.nxt("W")
                        kb.tt("dve", w[R, 0:64].re("p (s t) -> p s t", t=4),
                              NUMS[R, :].re("p (s h t) -> p s h t", s=NSQ, h=8)[:, :, hh, :],
                              bnn[R, hc].re("p (s t) -> p s t", t=4), ALU.add)
                        kb.tt("dve", OA[R, c, 0:64], w[R, 0:64], DEN[R, hc], ALU.mult)
                chk(4.5)
                b = kb.bank()
                kb.tr(b[0:64, 0:128], KF[:, 0:64], ident)
                stg = kb.nxt("STG")
                kb.cp("act", stg[0:64, :], b[0:64, 0:128])
                for s in range(NSQ):
                    kb.dma(osk[l, s, 124:128, :], stg[s * 4:(s + 1) * 4, :])
                    kb.dma(osv[l, s, 124:128, :], VF[s * 4:(s + 1) * 4, :])
                outnorm([OA[:, c, :Tn] for c in range(4)], 1.0 / 512, 0)
                for _ in ssm_items():
                    pass
                for _ in lru_items():
                    pass
            sSW.done()

            chk(8)
            if kind == "s" and l + 1 < NL:
                emit_casts(l + 1, 2)
            for mo in range(8):
                WO = sWO.get()
                b = kb.bank()
                for kc in range(8):
                    kb.mm(b[:, :Tn], WO[:, kc, :], Ob[:, kc, :Tn], start=kc == 0, stop=kc == 7)
                resid(b, mo, MODG1)
                sWO.done()

            modulate(H, MODA2, MODB2, "n2")

            chk(9)
            for j in range(NJ):
                cc_t = {}
                for half in range(2):
                    WUP = sUP.get()
                    ch = half * NJ + j
                    b = kb.bank()
                    for kc in range(8):
                        kb.mm(b[:, :Tn], WUP[:, kc, :], H[:, kc, :Tn], start=kc == 0, stop=kc == 7)
                    sUP.done()
                    ue = kb.nxt("UE")
                    ue3 = ue[:, 0:nseq * (2 + L)].re("p (s l) -> p s l", l=2 + L)
                    fcst = S_["fc"][:, ch, :].re("p (s t) -> p s t", t=2)
                    kb.cp(PL, ue3[:, :, 0:2], fcst)
                    kb.cp("act", ue3[:, :, 2:2 + L], v3(b[:, :Tn]))
                    cc = kb.nxt("W")
                    o_, _s = PP_OFF["fcw"]
                    wcol = lambda tap: kb.PP[:, l * NPP + o_ + ch * 3 + tap: l * NPP + o_ + ch * 3 + tap + 1]
                    kb.act(cc[:, :Tn], b[:, :Tn], AF.Identity, scale=wcol(2), bias=kb.pcol("fcb", l, ch))
                    cc3 = v3(cc[:, :Tn])
                    kb.stt(cc3, ue3[:, :, 1:1 + L], wcol(1), cc3, ALU.mult, ALU.add)
                    kb.stt(cc3, ue3[:, :, 0:L], wcol(0), cc3, ALU.mult, ALU.add)
                    kb.cp(PL, fcst, ue3[:, :, L:L + 2])
                    cc_t[half] = cc
                    if last:
                        dfc = osfc if kind == "s" else pfc
                        kb.store_T(dfc[l, :, ch * 128:(ch + 1) * 128], S_["fc"][:, ch, :], nseq * 2)
                gg = kb.nxt("W")
                kb.act(gg[:, :Tn], cc_t[0][:, :Tn], AF.Gelu_apprx_tanh)
                kb.tt("dve", G[:, j, :Tn], gg[:, :Tn], cc_t[1][:, :Tn], ALU.mult)

            chk(10)
            if kind == "s" and l + 1 < NL:
                emit_casts(l + 1, 3)
            for mo in range(8):
                b = kb.bank()
                for hj in range(2):
                    WDN = sDN.get()
                    for j2 in range(11):
                        j = hj * 11 + j2
                        kb.mm(b[:, :Tn], WDN[:, j2, :], G[:, j, :Tn], start=j == 0, stop=j == NJ - 1)
                    sDN.done()
                resid(b, mo, MODG2)

        dst = ys if kind == "s" else yp
        for tb in range(ntb):
            xt = kb.nxt("XT")
            for half in range(2):
                b = kb.bank()
                for q4 in range(4):
                    kc = half * 4 + q4
                    kb.tr(b[:tbs, q4 * 128:(q4 + 1) * 128], X[:, kc, tb * 128: tb * 128 + tbs], ident)
                kb.cp("act", xt[:tbs, half * 512:(half + 1) * 512], b[:tbs, :])
            kb.dma(dst[r0 + tb * 128: r0 + tb * 128 + tbs, :], xt[:tbs, :])


_CACHE = {}


def kernel(**inp):
    inp = {k: np.asarray(v) for k, v in inp.items()}
    f32 = np.float32
    if "nc" not in _CACHE:
        _CACHE["nc"] = build()
    nc = _CACHE["nc"]
    pp, sw = _host_params(inp)
    cst, cosP, sinP, msk = _host_consts()
    in_maps = []
    for c in range(8):
        b = c // 2
        sq = slice(c * NSQ, (c + 1) * NSQ)
        m = {
            "xp": np.ascontiguousarray(inp["x_prompt"][b]),
            "xs": np.ascontiguousarray(inp["x_sample"][sq].reshape(TS, D)),
            "ck": np.ascontiguousarray(inp["cache_k"][:, sq].reshape(NL, NSQ, 128, 128)),
            "cv": np.ascontiguousarray(inp["cache_v"][:, sq].reshape(NL, NSQ, 128, 128)),
            "sre": np.ascontiguousarray(inp["state_ssm_re"][:, sq].reshape(NL, NSQ, 1024)),
            "sim": np.ascontiguousarray(inp["state_ssm_im"][:, sq].reshape(NL, NSQ, 1024)),
            "slh": np.ascontiguousarray(inp["state_lru_h"][:, sq]),
            "slc": np.ascontiguousarray(inp["state_lru_conv"][:, sq].reshape(NL, NSQ * 3, 256)),
            "sfc": np.ascontiguousarray(inp["state_ffn_conv"][:, sq].reshape(NL, NSQ * 2, 5632)),
            "c17": np.ascontiguousarray(np.concatenate([inp["c_prompt"][b:b + 1], inp["c_sample"][sq]], axis=0)),
            "wada": inp["w_ada"], "win": inp["w_in"], "wo": inp["w_o"], "wup": inp["ffn_w_up"], "wdn": inp["ffn_w_down"],
            "pp": pp, "sw": sw, "cst": cst, "msk": msk, "cosP": cosP, "sinP": sinP,
        }
        in_maps.append({k_: np.ascontiguousarray(v, dtype=f32) for k_, v in m.items()})
    import os
    ncores = int(os.environ.get("KCORES", "8"))
    res = run_bass_kernel_spmd(nc, in_maps[:ncores], core_ids=list(range(ncores)))
    R = list(res.results)
    while len(R) < 8:
        R.append({k_: np.zeros_like(v) for k_, v in R[0].items()})
    ev = [R[2 * b] for b in range(4)]
    y_p = np.stack([r["yp"] for r in ev], 0)
    y_s = np.concatenate([r["ys"].reshape(NSQ, 4, D) for r in R], 0)

    def pst(name, shape):
        return np.stack([r[name] for r in ev], 1).reshape(shape)

    def sst(name, shape):
        return np.concatenate([r[name].reshape((NL, NSQ) + shape) for r in R], 1)

    outs = (y_p, y_s,
            pst("pk", (NL, 4, 128, 2, 64)), pst("pv", (NL, 4, 128, 2, 64)),
            pst("pre", (NL, 4, 16, 64)), pst("pim", (NL, 4, 16, 64)),
            pst("plh", (NL, 4, 256)), pst("plc", (NL, 4, 3, 256)), pst("pfc", (NL, 4, 2, 5632)),
            sst("osk", (128, 2, 64)), sst("osv", (128, 2, 64)),
            sst("osre", (16, 64)), sst("osim", (16, 64)),
            sst("oslh", (256,)), sst("oslc", (3, 256)), sst("osfc", (2, 5632)))
    return tuple(np.ascontiguousarray(o, dtype=f32) for o in outs)
```

```python
import contextlib
import math
import numpy as np
import ml_dtypes
import concourse.bass as bass
import concourse.mybir as mybir
from concourse.bass_utils import run_bass_kernel_spmd

F32 = mybir.dt.float32
BF16 = mybir.dt.bfloat16
AF = mybir.ActivationFunctionType
ALU = mybir.AluOpType

NL = 4
D = 1024
DFF = 2816
NJ = 22
TP = 512
NSEG = 8
TS = 64
NSQ = 16
EPS = 1e-6


class Buf:
    __slots__ = ("name", "w", "rd")

    def __init__(self, name):
        self.name = name
        self.w = None
        self.rd = []


class Op:
    __slots__ = ("eng", "fn", "deps", "idx", "dma", "sem", "semval", "prevdma", "wkey")

    def __init__(self, eng, fn, dma):
        self.eng = eng
        self.fn = fn
        self.deps = set()
        self.dma = dma
        self.sem = None
        self.semval = None
        self.prevdma = None


class Sched:
    ENG = ("pe", "act", "dve", "pool", "sp")

    def __init__(self, nc, n_dma_sems=14):
        self.nc = nc
        self.ops = []
        self.n_dma_sems = n_dma_sems

    def add(self, eng, fn, reads=(), writes=(), dma=False):
        op = Op(eng, fn, dma)
        op.idx = len(self.ops)
        op.wkey = id(writes[0]) if len(writes) else None
        for b in reads:
            if b.w is not None:
                op.deps.add(b.w)
        for b in writes:
            if b.w is not None:
                op.deps.add(b.w)
            for r in b.rd:
                op.deps.add(r)
        for b in reads:
            if not dma:
                b.rd = [r for r in b.rd if r.dma or r.eng != eng]
            b.rd.append(op)
        for b in writes:
            b.w = op
            b.rd = []
        op.deps.discard(op)
        self.ops.append(op)
        return op

    def emit(self):
        nc = self.nc
        ops = self.ops
        need = set()
        for op in ops:
            for d in op.deps:
                if d.eng == "pe" and op.eng == "pe" and not d.dma and not op.dma:
                    continue
                need.add(d)
        cnt = {e: 0 for e in self.ENG}
        dmacnt = {e: 0 for e in self.ENG}
        dma_last = {}
        for op in ops:
            if op.dma:
                slot = dmacnt[op.eng] % self.n_dma_sems
                dmacnt[op.eng] += 1
                key = (op.eng, slot)
                prev = dma_last.get(key)
                op.prevdma = prev
                op.sem = key
                op.semval = (prev.semval if prev else 0) + 16
                dma_last[key] = op
            elif op in need:
                cnt[op.eng] += 1
                op.sem = (op.eng, None)
                op.semval = cnt[op.eng]
        with contextlib.ExitStack() as st:
            sems = {}
            for e in self.ENG:
                sems[(e, None)] = st.enter_context(nc.semaphore("s_" + e))
                for k in range(min(self.n_dma_sems, dmacnt[e])):
                    sems[(e, k)] = st.enter_context(nc.semaphore("d_%s%d" % (e, k)))
            block = st.enter_context(nc.Block())
            decos = {"pe": block.tensor, "act": block.scalar, "dve": block.vector,
                     "pool": block.gpsimd, "sp": block.sync}
            last_dmas = list(dma_last.values())
            for e in self.ENG:
                mine = [op for op in ops if op.eng == e]

                def body(eng, mine=mine, e=e):
                    waited = {}

                    def wait(key, val):
                        if waited.get(key, 0) >= val:
                            return
                        eng.wait_ge(sems[key], val)
                        waited[key] = val

                    for i_, op in enumerate(mine):
                        for d in sorted(op.deps, key=lambda o: o.idx):
                            if d.sem is None:
                                continue
                            wait(d.sem, d.semval)
                        if e == "pe":
                            for nx in mine[i_ + 1:i_ + 40]:
                                if nx.wkey != op.wkey:
                                    break
                                bad = False
                                for d in nx.deps:
                                    if d.sem is None:
                                        continue
                                    if d.idx > op.idx and not (d.eng == "pe" and not d.dma):
                                        bad = True
                                if bad:
                                    break
                                for d in sorted(nx.deps, key=lambda o: o.idx):
                                    if d.sem is not None and d.idx < op.idx - 48:
                                        wait(d.sem, d.semval)
                        if op.dma and op.prevdma is not None:
                            wait(op.prevdma.sem, op.prevdma.semval)
                        ins = op.fn(eng)
                        if op.dma:
                            ins.then_inc(sems[op.sem], 16)
                        elif op.sem is not None:
                            ins.then_inc(sems[op.sem], 1)
                    if e == "sp":
                        for op in last_dmas:
                            wait(op.sem, op.semval)

                decos[e](body)


class V:
    __slots__ = ("ap", "bufs")

    def __init__(self, ap, bufs):
        self.ap = ap
        self.bufs = bufs

    def __getitem__(self, idx):
        return V(self.ap[idx], self.bufs)

    def re(self, pat, **kw):
        return V(self.ap.rearrange(pat, **kw), self.bufs)


class T:
    def __init__(self, ap, name, split=0, axis=1):
        self.t = ap
        self.split = split
        self.axis = axis
        self.bufs = [Buf("%s%d" % (name, i)) for i in range(split)] if split else [Buf(name)]

    def __getitem__(self, idx):
        ap = self.t[idx]
        if self.split:
            ax = self.axis
            if isinstance(idx, tuple):
                i1 = idx[ax] if len(idx) > ax else slice(None)
            else:
                i1 = idx if ax == 0 else slice(None)
            if isinstance(i1, int):
                bufs = [self.bufs[i1]]
            else:
                bufs = self.bufs[i1]
        else:
            bufs = self.bufs
        return V(ap, bufs)


def _ap(x):
    return x.ap if isinstance(x, V) else x


def _bufs(*xs):
    out = []
    for x in xs:
        if isinstance(x, V):
            out.extend(x.bufs)
    return out


def _pp_layout():
    sizes = [("n1", 8), ("n2", 8), ("on", 8), ("qn", 1), ("kn", 1), ("sk", 4), ("are", 8), ("aim", 8),
             ("ldt", 8), ("sd", 2), ("bglu", 2), ("lcw", 8), ("lcb", 2), ("ba", 2), ("bi", 2), ("lam", 2),
             ("fcw", 132), ("fcb", 44), ("bada", 48)]
    off = {}
    o = 0
    for n, s in sizes:
        off[n] = (o, s)
        o += s
    return off, o


PP_OFF, NPP = _pp_layout()
SW_OFF = {"bre": 0, "bim": 1024, "cre": 2048, "cim": 3072, "wa": 4096, "wi": 4352, "glu": 4608}
NSW = 5120
CST_OFF = {"ident": 0, "ones": 128, "blk": 256, "rot": 384, "coss": 512, "sins": 576}
NCST = 640


def _perm_attn_rows():
    idx = []
    for c in range(4):
        idx += list(range(64 * c, 64 * c + 64)) + list(range(256 + 64 * c, 256 + 64 * c + 64))
    return np.array(idx)


def _fm(v, nch):
    return np.ascontiguousarray(v.reshape(nch, 128).T)


def _host_params(inp):
    f32 = np.float32
    pp = np.zeros((128, NL, NPP), f32)
    sw = np.zeros((NL, 128, NSW), f32)
    perm = _perm_attn_rows()

    def put(l, name, arr):
        o, s = PP_OFF[name]
        pp[:, l, o:o + s] = arr.reshape(128, s)

    for l in range(NL):
        put(l, "n1", _fm(inp["norm1"][l], 8))
        put(l, "n2", _fm(inp["norm2"][l], 8))
        on = inp["out_norm"][l].copy()
        on[:512] = on[:512][perm]
        put(l, "on", _fm(on, 8))
        put(l, "qn", np.tile(inp["q_norm"][l], 2))
        put(l, "kn", np.tile(inp["k_norm"][l], 2))
        sk = np.zeros((128, 4), f32)
        for c in range(4):
            sk[:64, c] = inp["sinks"][l][c]
            sk[64:, c] = inp["sinks"][l][4 + c]
        put(l, "sk", sk)
        put(l, "are", _fm(inp["ssm_a_re"][l].reshape(-1), 8))
        put(l, "aim", _fm(inp["ssm_a_im"][l].reshape(-1), 8))
        put(l, "ldt", _fm(np.repeat(inp["ssm_log_dt"][l], 64), 8))
        put(l, "sd", _fm(inp["ssm_d"][l], 2))
        put(l, "bglu", _fm(inp["ssm_b_glu"][l], 2))
        lcw = np.zeros((128, 2, 4), f32)
        for j in range(4):
            lcw[:, :, j] = _fm(inp["lru_conv_w"][l][j], 2)
        put(l, "lcw", lcw)
        put(l, "lcb", _fm(inp["lru_conv_b"][l], 2))
        put(l, "ba", _fm(inp["lru_b_a"][l], 2))
        put(l, "bi", _fm(inp["lru_b_i"][l], 2))
        put(l, "lam", _fm(inp["lru_lambda"][l], 2))
        fcw = np.zeros((128, 44, 3), f32)
        for j in range(3):
            fcw[:, :, j] = _fm(inp["ffn_conv_w"][l][j], 44)
        put(l, "fcw", fcw)
        put(l, "fcb", _fm(inp["ffn_conv_b"][l], 44))
        put(l, "bada", _fm(inp["b_ada"][l], 48))
        bre = inp["ssm_b_re"][l]
        bim = inp["ssm_b_im"][l]
        cre = inp["ssm_c_re"][l]
        cim = inp["ssm_c_im"][l]
        for j in range(8):
            for gl in range(2):
                g = 2 * j + gl
                r0 = 32 * (j % 4) + 16 * gl
                sw[l, r0:r0 + 16, SW_OFF["bre"] + j * 128 + gl * 64: SW_OFF["bre"] + j * 128 + gl * 64 + 64] = bre[g].T
                sw[l, r0:r0 + 16, SW_OFF["bim"] + j * 128 + gl * 64: SW_OFF["bim"] + j * 128 + gl * 64 + 64] = bim[g].T
                sw[l, gl * 64:gl * 64 + 64, SW_OFF["cre"] + j * 128 + r0: SW_OFF["cre"] + j * 128 + r0 + 16] = cre[g].T
                sw[l, gl * 64:gl * 64 + 64, SW_OFF["cim"] + j * 128 + r0: SW_OFF["cim"] + j * 128 + r0 + 16] = cim[g].T
        for m in range(2):
            for hb in range(2):
                h = 2 * m + hb
                sw[l, hb * 64:hb * 64 + 64, SW_OFF["wa"] + m * 128 + hb * 64: SW_OFF["wa"] + m * 128 + hb * 64 + 64] = inp["lru_w_a"][l][h]
                sw[l, hb * 64:hb * 64 + 64, SW_OFF["wi"] + m * 128 + hb * 64: SW_OFF["wi"] + m * 128 + hb * 64 + 64] = inp["lru_w_i"][l][h]
        wg = inp["ssm_w_glu"][l]
        for kc in range(2):
            sw[l, :, SW_OFF["glu"] + kc * 256: SW_OFF["glu"] + kc * 256 + 256] = wg[kc * 128:(kc + 1) * 128, :]
    return pp.reshape(128, NL * NPP), sw


def _host_consts():
    f32 = np.float32
    cst = np.zeros((128, NCST), f32)
    cst[:, 0:128] = np.eye(128, dtype=f32)
    cst[:, 128:256] = 1.0
    cst[0:64, 256:320] = 1.0
    cst[64:128, 320:384] = 1.0
    rot = np.zeros((128, 128), f32)
    for m in range(128):
        if (m % 64) < 32:
            rot[m + 32, m] = -1.0
        else:
            rot[m - 32, m] = 1.0
    cst[:, 384:512] = rot
    s = np.arange(128)[:, None]
    q = np.arange(128)[None, :]
    prev = (s > q).astype(f32)
    diag = (s <= q).astype(f32)
    msk = np.zeros((128, 1536), f32)
    msk[:, 0:512] = np.concatenate([prev, prev, diag, diag], axis=1)
    t = np.arange(4)[None, :]
    mc = (np.arange(128)[:, None] > t).astype(f32)
    msk[:, 512:1024] = np.tile(mc, (1, 8 * 16))
    kk = np.arange(64)
    ks, kt = kk // 4, kk % 4
    mn = ((ks[:, None] == ks[None, :]) & (kt[:, None] <= kt[None, :])).astype(f32)
    msk[0:64, 1024:1536] = np.tile(mn, (1, 8))
    half = 32
    inv = (f32(10000.0) ** (-np.arange(half, dtype=f32) / f32(half))).astype(f32)
    fi = (np.arange(128) % 64) % 32

    def tabs(pos):
        ang = (pos.astype(f32)[:, None] * inv[None, :]).astype(f32)
        return np.ascontiguousarray(np.cos(ang).astype(f32)[:, fi].T), np.ascontiguousarray(np.sin(ang).astype(f32)[:, fi].T)

    cp, sp = tabs(np.arange(4096))
    cs, ss = tabs(8192 + np.arange(4))
    cst[:, 512:576] = np.tile(cs, (1, 16))
    cst[:, 576:640] = np.tile(ss, (1, 16))
    return cst, cp, sp, msk


class KB:
    def __init__(self):
        self.nc = bass.Bass("TRN2", target_bir_lowering=False)
        self.S = Sched(self.nc)
        self.st = contextlib.ExitStack()
        self.bank_i = 0
        self.pool_i = {}
        self.rot = {}

    def dram(self, name, shape, dtype=F32, kind="ExternalInput", split=0):
        h = self.nc.dram_tensor(name, list(shape), dtype, kind=kind)
        return T(h.ap(), name, split, axis=0)

    def sb(self, name, shape, dtype=F32, split=0):
        t = self.st.enter_context(self.nc.sbuf_tensor(name, list(shape), dtype))
        return T(t, name, split)

    def rots(self, name, n, shape, dtype=F32):
        self.rot[name] = ([self.sb("%s%d" % (name, i), shape, dtype) for i in range(n)], 0)

    def nxt(self, name):
        lst, i = self.rot[name]
        self.rot[name] = (lst, i + 1)
        return lst[i % len(lst)]

    POOLS = {"attn": [0, 1, 2], "ssm": [3, 4], "lru": [5]}
    cur_pool = None

    def bank(self):
        if self.cur_pool is not None:
            lst = self.POOLS[self.cur_pool]
            i = self.pool_i.get(self.cur_pool, 0)
            self.pool_i[self.cur_pool] = i + 1
            return self.ps[lst[i % len(lst)]]
        b = self.ps[self.bank_i % 6]
        self.bank_i += 1
        return b

    def mm(self, out, lhsT, rhs, start=True, stop=True):
        self.S.add("pe", lambda e: e.matmul(out.ap, lhsT.ap, rhs.ap, start=start, stop=stop),
                   reads=_bufs(lhsT, rhs), writes=_bufs(out))

    def tr(self, out, in_, ident):
        self.S.add("pe", lambda e: e.transpose(out.ap, in_.ap, ident.ap), reads=_bufs(in_, ident), writes=_bufs(out))

    def act(self, out, in_, func, scale=None, bias=None):
        kw = {}
        if scale is not None:
            kw["scale"] = _ap(scale)
        if bias is not None:
            kw["bias"] = _ap(bias)
        self.S.add("act", lambda e: e.activation(out.ap, in_.ap, func, **kw),
                   reads=_bufs(in_, scale, bias), writes=_bufs(out))

    def tt(self, eng, out, a, b, op):
        self.S.add(eng, lambda e: e.tensor_tensor(out.ap, a.ap, b.ap, op), reads=_bufs(a, b), writes=_bufs(out))

    def ts(self, eng, out, a, s1, op0, s2=None, op1=None):
        if op1 is None:
            fn = lambda e: e.tensor_scalar(out.ap, a.ap, _ap(s1), None, op0)
        else:
            fn = lambda e: e.tensor_scalar(out.ap, a.ap, _ap(s1), _ap(s2), op0, op1)
        self.S.add(eng, fn, reads=_bufs(a, s1, s2), writes=_bufs(out))

    def stt(self, out, a, scalar, b, op0, op1):
        self.S.add("dve", lambda e: e.scalar_tensor_tensor(out.ap, a.ap, _ap(scalar), b.ap, op0, op1),
                   reads=_bufs(a, scalar, b), writes=_bufs(out))

    def cp(self, eng, out, in_):
        if eng == "act":
            self.S.add("act", lambda e: e.copy(out.ap, in_.ap), reads=_bufs(in_), writes=_bufs(out))
        else:
            self.S.add(eng, lambda e: e.tensor_copy(out.ap, in_.ap), reads=_bufs(in_), writes=_bufs(out))

    def scan(self, out, d0, d1):
        self.S.add("dve", lambda e: e.tensor_tensor_scan(out.ap, d0.ap, d1.ap, 0.0, ALU.mult, ALU.add),
                   reads=_bufs(d0, d1), writes=_bufs(out))

    def recip(self, out, in_):
        self.S.add("dve", lambda e: e.reciprocal(out.ap, in_.ap), reads=_bufs(in_), writes=_bufs(out))

    def memset(self, eng, out, val):
        self.S.add(eng, lambda e: e.memset(out.ap, val), writes=_bufs(out))

    def dma(self, out, in_, eng="sp", nc_ok=False, **kw):
        if nc_ok:
            kw["allow_slow_non_contiguous"] = True
        self.S.add(eng, lambda e: e.dma_start(out=out.ap, in_=in_.ap, **kw), reads=_bufs(in_), writes=_bufs(out), dma=True)

    def pcol(self, name, l, i=0, n=1):
        o, s = PP_OFF[name]
        return self.PP[:, l * NPP + o + i: l * NPP + o + i + n]

    def cst(self, name, n=128, rows=slice(None)):
        o = CST_OFF[name]
        return self.CST[rows, o:o + n]

    def rstd_from(self, bank, Tn, inv_n):
        rs = self.nxt("RS")
        self.act(rs[:, :Tn], bank[:, :Tn], AF.Ln, scale=inv_n, bias=self.EPSC[:, 0:1])
        self.act(rs[:, :Tn], rs[:, :Tn], AF.Exp, scale=-0.5)
        return rs

    def load_T(self, dst, src, n):
        stg = self.nxt("STG")
        self.dma(stg[:n, :], src)
        b = self.bank()
        self.tr(b[:, :n], stg[:n, :], self.cst("ident", n, slice(0, n)))
        self.cp("act", dst, b[:, :n])

    def store_T(self, dst, src, n):
        b = self.bank()
        self.tr(b[:n, 0:128], src, self.cst("ident"))
        stg = self.nxt("STG")
        self.cp("act", stg[:n, :], b[:n, 0:128])
        self.dma(dst, stg[:n, :])


class Stream:
    def __init__(self, kb, name, nslots, shape, dtype, items):
        self.kb = kb
        self.tiles = [kb.sb("%s_%d" % (name, i), shape, dtype) for i in range(nslots)]
        self.items = items
        self.i = 0
        self.n = nslots

    def _load(self, idx):
        if idx < len(self.items):
            t = self.tiles[idx % self.n]
            self.kb.dma(t[:], self.items[idx])

    def start(self):
        for i in range(self.n):
            self._load(i)

    def get(self):
        return self.tiles[self.i % self.n]

    def done(self):
        self._load(self.i + self.n)
        self.i += 1


def build():
    kb = KB()
    nc = kb.nc
    st = kb.st
    with st:
        _build(kb)
    return nc


def _build(kb):
    nc = kb.nc
    xp = kb.dram("xp", [4096, D])
    xs = kb.dram("xs", [TS, D])
    ck = kb.dram("ck", [NL, NSQ, 128, 128])
    cv = kb.dram("cv", [NL, NSQ, 128, 128])
    sre = kb.dram("sre", [NL, NSQ, 1024])
    sim = kb.dram("sim", [NL, NSQ, 1024])
    slh = kb.dram("slh", [NL, NSQ, 256])
    slc = kb.dram("slc", [NL, NSQ * 3, 256])
    sfc = kb.dram("sfc", [NL, NSQ * 2, 5632])
    c17 = kb.dram("c17", [17, D])
    wada = kb.dram("wada", [NL, D, 6 * D])
    win = kb.dram("win", [NL, D, 1536])
    wo = kb.dram("wo", [NL, D, D])
    wup = kb.dram("wup", [NL, D, 2 * DFF])
    wdn = kb.dram("wdn", [NL, DFF, D])
    ppd = kb.dram("pp", [128, NL * NPP])
    swd = kb.dram("sw", [NL, 128, NSW])
    cstd = kb.dram("cst", [128, NCST])
    mskd = kb.dram("msk", [128, 1536])
    cosP = kb.dram("cosP", [128, 4096])
    sinP = kb.dram("sinP", [128, 4096])
    O = "ExternalOutput"
    yp = kb.dram("yp", [4096, D], kind=O)
    ys = kb.dram("ys", [TS, D], kind=O)
    pk = kb.dram("pk", [NL, 128, 128], kind=O)
    pv = kb.dram("pv", [NL, 128, 128], kind=O)
    pre = kb.dram("pre", [NL, 1, 1024], kind=O)
    pim = kb.dram("pim", [NL, 1, 1024], kind=O)
    plh = kb.dram("plh", [NL, 1, 256], kind=O)
    plc = kb.dram("plc", [NL, 3, 256], kind=O)
    pfc = kb.dram("pfc", [NL, 2, 5632], kind=O)
    osk = kb.dram("osk", [NL, NSQ, 128, 128], kind=O)
    osv = kb.dram("osv", [NL, NSQ, 128, 128], kind=O)
    osre = kb.dram("osre", [NL, NSQ, 1024], kind=O)
    osim = kb.dram("osim", [NL, NSQ, 1024], kind=O)
    oslh = kb.dram("oslh", [NL, NSQ, 256], kind=O)
    oslc = kb.dram("oslc", [NL, NSQ * 3, 256], kind=O)
    osfc = kb.dram("osfc", [NL, NSQ * 2, 5632], kind=O)
    I = "Internal"
    win_s = kb.dram("win_s", [NL, 12, 128, 8, 128], BF16, kind=I, split=NL)
    wo_s = kb.dram("wo_s", [NL, 8, 128, 8, 128], BF16, kind=I, split=NL)
    wup_s = kb.dram("wup_s", [NL, NJ, 128, 8, 256], BF16, kind=I, split=NL)
    wdn_s = kb.dram("wdn_s", [NL, 8, 128, NJ, 128], BF16, kind=I, split=NL)
    sw_s = kb.dram("sw_s", [NL, 128, NSW], BF16, kind=I, split=NL)
    tabp = kb.dram("tabp", [NL, 8, 128, 4, TP], F32, kind=I, split=NL)
    tabs = kb.dram("tabs", [NL, 8, 128, 4, TS], F32, kind=I, split=NL)

    kb.ps = [T(kb.st.enter_context(nc.psum_tensor("ps%d" % i, [128, 512], F32)), "ps%d" % i) for i in range(8)]
    kb.CST = kb.sb("CST", [128, NCST])
    kb.PP = kb.sb("PP", [128, NL * NPP])
    kb.EPSC = kb.sb("EPSC", [128, 4])
    kb.ZERO = kb.sb("ZERO", [128, TP])
    ONEB = kb.sb("ONEB", [128, 128], BF16)
    BLKB = kb.sb("BLKB", [128, 128], BF16)
    MASK4 = kb.sb("MASK4", [128, 512], BF16)
    MASKC = kb.sb("MASKC", [128, 512], BF16)
    MASKN = kb.sb("MASKN", [64, 512], BF16)
    X = kb.sb("X", [128, 8, TP], split=8)
    H = kb.sb("H", [128, 8, TP], BF16, split=8)
    Ob = H
    PROJ = kb.sb("PROJ", [128, 11, TP], split=11)

    class GAlias:
        def __getitem__(self, idx):
            p_, j_, c_ = idx
            v = PROJ[:, j_ // 2, :]
            ap = v.ap.bitcast(BF16)[p_, (j_ % 2) * TP:(j_ % 2 + 1) * TP][:, c_]
            return V(ap, v.bufs)

    class Sub:
        def __init__(self, off):
            self.off = off

        def __getitem__(self, idx):
            p_, c_, f_ = idx
            return PROJ[p_, c_ + self.off, f_]

    G = GAlias()
    MODA1 = kb.sb("MODA1", [128, NL, 8, 17])
    MODB1 = kb.sb("MODB1", [128, NL, 8, 17])
    MODG1 = kb.sb("MODG1", [128, NL, 8, 17])
    MODA2 = kb.sb("MODA2", [128, NL, 8, 17])
    MODB2 = kb.sb("MODB2", [128, NL, 8, 17])
    MODG2 = kb.sb("MODG2", [128, NL, 8, 17])
    RHO = kb.sb("RHO", [128, NL, 8])
    NSP8 = kb.sb("NSP8", [128, NL, 2])
    NSP16 = kb.sb("NSP16", [128, NL, 2])
    SINKE = kb.sb("SINKE", [128, NL, 4])
    kb.rots("RS", 2, [128, TP])
    kb.rots("SQ", 3, [128, TP], BF16)
    kb.rots("W", 7, [128, TP])
    kb.rots("STG", 5, [128, 128])
    kb.rots("XT", 1, [128, D])
    kb.rots("E", 2, [128, 512], BF16)
    kb.rots("UE", 2, [128, TP + 2 * NSQ])
    QR = kb.sb("QR", [128, 4, TP], BF16, split=4)
    KRA = kb.sb("KRA", [128, TP], BF16)
    KRB = kb.sb("KRB", [128, TP], BF16)
    KF = kb.sb("KF", [128, TP])
    VT = kb.sb("VT", [128, 4, 128], BF16)
    VF = kb.sb("VF", [128, 128])
    CS = kb.sb("CS", [128, TP])
    SN = kb.sb("SN", [128, TP])
    OA = Sub(0)
    DEN = kb.sb("DEN", [128, 512])
    UB = kb.sb("UB", [128, 2, TP], BF16)
    kb.rots("HR", 2, [128, TP], BF16)
    kb.rots("HI", 2, [128, TP], BF16)
    GLB = kb.sb("GLB", [128, 2, TP], BF16)
    GLF = kb.sb("GLF", [128, 2, TP])
    OS = Sub(5)
    XC = Sub(7)
    XCB = kb.sb("XCB", [128, 2, TP], BF16)
    XE = kb.sb("XE", [128, TP + 3 * NSQ])
    OL = Sub(9)
    stP = []
    for l in range(NL):
        stP.append(dict(kprevA=kb.sb("kprevA%d" % l, [128, 128], BF16), kprevB=kb.sb("kprevB%d" % l, [128, 128], BF16), vprev=kb.sb("vprev%d" % l, [128, 128], BF16),
                        sr=kb.sb("sr%d" % l, [128, 8, 1]), si=kb.sb("si%d" % l, [128, 8, 1]),
                        lh=kb.sb("lh%d" % l, [128, 2, 1]), lc=kb.sb("lc%d" % l, [128, 2, 3]),
                        fc=kb.sb("fc%d" % l, [128, 44, 2])))
    stS = dict(sr=kb.sb("srS", [128, 8, NSQ]), si=kb.sb("siS", [128, 8, NSQ]), lh=kb.sb("lhS", [128, 2, NSQ]),
               lc=kb.sb("lcS", [128, 2, NSQ * 3]), fc=kb.sb("fcS", [128, 44, NSQ * 2]))
    class XAlias:
        def __init__(self, kc0):
            self.kc0 = kc0

        def __getitem__(self, idx):
            p_, s_, f_ = idx
            kc = self.kc0 + s_ // 4
            v = X[:, kc, 64 + (s_ % 4) * 64: 64 + (s_ % 4 + 1) * 64]
            return V(v.ap.bitcast(BF16)[p_, f_], v.bufs)

    KCT = XAlias(0)
    VCT = XAlias(4)

    ident = kb.cst("ident")
    ones = kb.cst("ones")
    blk = kb.cst("blk")
    rotm = kb.cst("rot")

    kb.dma(kb.CST[:], cstd[:, :])
    kb.dma(kb.PP[:], ppd[:, :])
    kb.memset("dve", kb.EPSC[:, 0:1], EPS)
    kb.memset("dve", kb.EPSC[:, 1:2], math.pi / 2)
    kb.memset("dve", kb.EPSC[:, 2:3], 1.0)
    kb.memset("dve", kb.EPSC[:, 3:4], 0.0)
    kb.memset("pool", kb.ZERO[:], 0.0)
    kb.memset("pool", ONEB[:], 1.0)
    kb.cp("dve", BLKB[:], kb.cst("blk"))
    kb.memset("pool", KRA[:], 0.0)
    kb.memset("pool", KRB[:], 0.0)
    kb.dma(MASK4[:], mskd[:, 0:512], eng="pool")
    kb.dma(MASKC[:], mskd[:, 512:1024], eng="pool")
    kb.dma(MASKN[:], mskd[0:64, 1024:1536], eng="pool")
    for l in range(NL):
        for nm in ("sr", "si", "lh", "lc", "fc", "kprevA", "kprevB"):
            kb.memset("pool", stP[l][nm][:], 0.0)

    def emit_casts(l, part=None):
        def P(n):
            return part is None or part == n
        if P(0):
            for c in range(4):
                for two in range(2):
                    kb.dma(win_s[l, c, :, :, two * 64: two * 64 + 64],
                           win[l, :, two * 256 + c * 64: two * 256 + c * 64 + 64].re("(kc p) e -> p kc e", p=128), eng="pool")
            for ci in range(4, 12):
                kb.dma(win_s[l, ci, :, :, :], win[l, :, 512 + (ci - 4) * 128: 512 + (ci - 3) * 128].re("(kc p) e -> p kc e", p=128), eng="pool")
            kb.dma(sw_s[l, :, :], swd[l, :, :], eng="pool", max_dma_last_dim=4096)
        if P(1):
            for kc in range(4):
                for two in range(2):
                    kb.dma(wo_s[l, :, two * 64:(two + 1) * 64, kc, :].re("m p e -> p m e"),
                           wo[l, two * 256 + kc * 64: two * 256 + kc * 64 + 64, :].re("p (m e) -> p m e", e=128), eng="pool")
            for kc in range(4, 8):
                kb.dma(wo_s[l, :, :, kc, :].re("m p e -> p m e"),
                       wo[l, kc * 128:(kc + 1) * 128, :].re("p (m e) -> p m e", e=128), eng="pool")
            for kc in range(0, 3):
                rows = slice(kc * 128, (kc + 1) * 128)
                for two in range(2):
                    kb.dma(wup_s[l, :, :, kc, two * 128:(two + 1) * 128].re("j p e -> p j e"),
                           wup[l, rows, two * DFF:(two + 1) * DFF].re("p (j e) -> p j e", e=128), eng="pool")
        if P(2):
            for kc in range(3, 8):
                rows = slice(kc * 128, (kc + 1) * 128)
                for two in range(2):
                    kb.dma(wup_s[l, :, :, kc, two * 128:(two + 1) * 128].re("j p e -> p j e"),
                           wup[l, rows, two * DFF:(two + 1) * DFF].re("p (j e) -> p j e", e=128), eng="pool")
        if P(3):
            for j in range(NJ):
                kb.dma(wdn_s[l, :, :, j, :].re("m p e -> p m e"),
                       wdn[l, j * 128:(j + 1) * 128, :].re("p (m e) -> p m e", e=128), eng="pool")

    emit_casts(0)

    PW = kb.sb("PW", [128, 16, 8])
    PWL = kb.sb("PWL", [128, NL, 4, 8])
    TABT = kb.sb("TABT", [128, 4, TP])
    TB = TABT
    TBS = kb.sb("TBS", [128, 4, TS])
    for l in range(NL):
        kb.act(SINKE[:, l, :], kb.pcol("sk", l, 0, 4), AF.Exp)
        ex = PW[:, 15, 0:2]
        kb.act(ex, kb.pcol("lam", l, 0, 2), AF.Exp, scale=-1.0)
        kb.act(ex, ex, AF.Ln, bias=kb.EPSC[:, 2:3])
        kb.ts("dve", NSP8[:, l, :], ex, -8.0, ALU.mult)
        kb.ts("dve", NSP16[:, l, :], ex, -16.0, ALU.mult)
        dt, th, c1, s1, t0, t1, nr, ni, dn, cr, ci = [PW[:, i, :] for i in range(11)]
        are = kb.pcol("are", l, 0, 8)
        aim = kb.pcol("aim", l, 0, 8)
        kb.act(dt, kb.pcol("ldt", l, 0, 8), AF.Exp)
        kb.tt("dve", t0, are, dt, ALU.mult)
        kb.act(RHO[:, l, :], t0, AF.Exp)
        kb.tt("dve", th, aim, dt, ALU.mult)
        kb.act(s1, th, AF.Sin, scale=1.0 / 16)
        kb.act(c1, th, AF.Sin, scale=1.0 / 16, bias=kb.EPSC[:, 1:2])
        for _ in range(4):
            kb.tt("dve", t0, c1, c1, ALU.mult)
            kb.tt("dve", t1, s1, s1, ALU.mult)
            kb.tt("dve", s1, s1, c1, ALU.mult)
            kb.ts("dve", s1, s1, 2.0, ALU.mult)
            kb.tt("dve", c1, t0, t1, ALU.subtract)
        kb.tt("dve", nr, RHO[:, l, :], c1, ALU.mult)
        kb.ts("dve", nr, nr, -1.0, ALU.add)
        kb.tt("dve", ni, RHO[:, l, :], s1, ALU.mult)
        kb.tt("dve", t0, are, are, ALU.mult)
        kb.tt("dve", t1, aim, aim, ALU.mult)
        kb.tt("dve", dn, t0, t1, ALU.add)
        kb.recip(dn, dn)
        kb.tt("dve", t0, nr, are, ALU.mult)
        kb.tt("dve", t1, ni, aim, ALU.mult)
        kb.tt("dve", cr, t0, t1, ALU.add)
        kb.tt("dve", cr, cr, dn, ALU.mult)
        kb.tt("dve", t0, ni, are, ALU.mult)
        kb.tt("dve", t1, nr, aim, ALU.mult)
        kb.tt("dve", ci, t0, t1, ALU.subtract)
        kb.tt("dve", ci, ci, dn, ALU.mult)
        for i_, src_ in enumerate((c1, s1, cr, ci)):
            kb.cp("dve", PWL[:, l, i_, :], src_)
    C17 = kb.sb("C17", [128, 8, 17])
    for s in range(17):
        kb.dma(C17[:, :, s], c17[s, :].re("(kc p) -> p kc", p=128), nc_ok=True)
    SC17 = kb.sb("SC17", [128, 8, 17])
    kb.act(SC17[:], C17[:], AF.Silu)
    wav = PROJ.t[:, 0:8, :].rearrange("p a (b n) -> p (a b) n", b=2)
    kb.rot["WA"] = ([T(wav[:, 0:8, :], "WA0"), T(wav[:, 8:16, :], "WA1")], 0)
    MODT = kb.sb("MODT", [128, 48, 17])
    def tab_gen():
        for l in range(NL):
            c1, s1, cr, ci = [PWL[:, l, i_, :] for i_ in range(4)]
            for j in range(8):
                cosk = TB[:, 2, :]
                sink = TB[:, 3, :]
                kb.cp("dve", TB[:, 2, 0:1], c1[:, j:j + 1])
                kb.cp("dve", TB[:, 3, 0:1], s1[:, j:j + 1])
                n = 1
                while n < TP:
                    cn = TB[:, 2, n - 1:n]
                    sn = TB[:, 3, n - 1:n]
                    w0 = kb.nxt("W")
                    w1 = kb.nxt("W")
                    kb.ts("dve", w0[:, 0:n], TB[:, 3, 0:n], sn, ALU.mult)
                    kb.ts("dve", w1[:, 0:n], TB[:, 2, 0:n], sn, ALU.mult)
                    kb.stt(TB[:, 2, n:2 * n], TB[:, 2, 0:n], cn, w0[:, 0:n], ALU.mult, ALU.subtract)
                    kb.stt(TB[:, 3, n:2 * n], TB[:, 3, 0:n], cn, w1[:, 0:n], ALU.mult, ALU.add)
                    n *= 2
                w0 = kb.nxt("W")
                kb.ts("dve", w0[:], sink, ci[:, j:j + 1], ALU.mult)
                kb.stt(TB[:, 0, :], cosk, cr[:, j:j + 1], w0[:], ALU.mult, ALU.add)
                w1 = kb.nxt("W")
                kb.ts("dve", w1[:], sink, cr[:, j:j + 1], ALU.mult)
                kb.stt(TB[:, 1, :], cosk, ci[:, j:j + 1], w1[:], ALU.mult, ALU.subtract)
                kb.dma(tabp[l, j, :, :, :], TB[:])
                for s in range(NSQ):
                    kb.cp("dve", TBS[:, :, s * 4:(s + 1) * 4], TB[:, :, 0:4])
                kb.dma(tabs[l, j, :, :, :], TBS[:])
                yield

    def ada_gen():
        for l in range(NL):
            for m2 in range(24):
                wa = kb.nxt("WA")
                kb.dma(wa[:], wada[l, :, m2 * 256:(m2 + 1) * 256].re("(kc p) n -> p kc n", p=128))
                for mi in range(2):
                    m = m2 * 2 + mi
                    b = kb.bank()
                    for kc in range(8):
                        kb.mm(b[:, 0:17], wa[:, kc, mi * 128:(mi + 1) * 128], SC17[:, kc, :], start=kc == 0, stop=kc == 7)
                    kb.act(MODT[:, m, :], b[:, 0:17], AF.Identity, bias=kb.pcol("bada", l, m))
                    yield
            for kc in range(8):
                kb.cp("act", MODB1[:, l, kc, :], MODT[:, 0 + kc, :])
                kb.ts("dve", MODA1[:, l, kc, :], MODT[:, 8 + kc, :], 1.0, ALU.add, kb.pcol("n1", l, kc), ALU.mult)
                kb.cp("act", MODG1[:, l, kc, :], MODT[:, 16 + kc, :])
                kb.cp("act", MODB2[:, l, kc, :], MODT[:, 24 + kc, :])
                kb.ts("dve", MODA2[:, l, kc, :], MODT[:, 32 + kc, :], 1.0, ALU.add, kb.pcol("n2", l, kc), ALU.mult)
                kb.cp("act", MODG2[:, l, kc, :], MODT[:, 40 + kc, :])


    def _nx(g):
        try:
            next(g)
            return True
        except StopIteration:
            return False

    gt_, ga_ = tab_gen(), ada_gen()
    at_ = aa_ = True
    while at_ or aa_:
        for _ in range(6):
            if aa_:
                aa_ = _nx(ga_)
        if at_:
            at_ = _nx(gt_)

    wa_l = kb.rot["WA"][0]
    import os
    STAGE = int(os.environ.get("KSTAGE", "99"))
    segs = [("s", 0)] + [("p", k) for k in range(NSEG)]
    segs = segs[:STAGE]
    if STAGE == 0:
        kb.S.emit()
        return
    it_win, it_wo, it_sw, it_up, it_dn, it_tab = [], [], [], [], [], []
    for kind, k in segs:
        for l in range(NL):
            for ci in [0, 1, 2, 3, 4, 6, 7, 8, 9, 10, 11, 5]:
                it_win.append(win_s[l, ci, :, :, :])
            for mo in range(8):
                it_wo.append(wo_s[l, mo, :, :, :])
            it_sw.append(sw_s[l, :, :])
            for j in range(NJ):
                for half in range(2):
                    it_up.append(wup_s[l, j, :, :, half * 128:(half + 1) * 128])
            for m in range(8):
                for hj in range(2):
                    it_dn.append(wdn_s[l, m, :, hj * 11:(hj + 1) * 11, :])
            for j in range(8):
                it_tab.append(tabp[l, j, :, :, :] if kind == "p" else tabs[l, j, :, :, :])
    sWIN = Stream(kb, "WIN", 4, [128, 8, 128], BF16, it_win)
    sWO = Stream(kb, "WO", 3, [128, 8, 128], BF16, it_wo)
    sSW = Stream(kb, "SWB", 1, [128, NSW], BF16, it_sw)
    sUP = Stream(kb, "WUP", 6, [128, 8, 128], BF16, it_up)
    sDN = Stream(kb, "WDN", 3, [128, 11, 128], BF16, it_dn)
    tab_tiles = [TABT]
    tab_i = [0]

    def tab_load(idx):
        if idx < len(it_tab):
            src = it_tab[idx]
            tn = TS if idx < NL * 8 else TP
            kb.dma(tab_tiles[0][:, :, 0:tn], src)

    kb.memset("dve", V(PROJ.t[:, 0:8, 0:1], PROJ.bufs[0:8] + wa_l[0].bufs + wa_l[1].bufs), 0.0)
    for s_ in (sWIN, sWO, sSW, sUP, sDN):
        s_.start()
    tab_load(0)

    SUB = float(os.environ.get("KSUB", "99"))

    class _Stop(Exception):
        pass

    def chk(n):
        if n > SUB:
            raise _Stop()

    try:
        _main(kb, locals())
    except _Stop:
        pass
    kb.S.emit()


def _main(kb, env):
    globals().update({k_: v_ for k_, v_ in env.items() if k_ not in ("kb",)})
    for kind, k in segs:
        Tn = TS if kind == "s" else TP
        nseq = NSQ if kind == "s" else 1
        L = Tn // nseq
        ntb = (Tn + 127) // 128
        tbs = min(128, Tn)

        def v3(v):
            return v.re("p (s l) -> p s l", l=L)

        def sidx(s):
            return 0 if kind == "p" else 1 + s

        PL = "dve" if kind == "s" else "pool"

        src = xs if kind == "s" else xp
        r0 = 0 if kind == "s" else k * TP
        for tb in range(ntb):
            xt = kb.nxt("XT")
            kb.dma(xt[:tbs, :], src[r0 + tb * 128: r0 + tb * 128 + tbs, :])
            for half in range(2):
                b = kb.bank()
                for q4 in range(4):
                    kc = half * 4 + q4
                    kb.tr(b[:, q4 * 128: q4 * 128 + tbs], xt[:tbs, kc * 128:(kc + 1) * 128],
                          kb.cst("ident", tbs, slice(0, tbs)))
                kb.cp("act", X[:, half * 4:half * 4 + 4, tb * 128: tb * 128 + tbs],
                      b[:, :].re("p (q t) -> p q t", t=128)[:, :, 0:tbs])
        if kind == "s":
            kb.cp(PL, CS[:, :Tn], kb.cst("coss", 64))
            kb.cp(PL, SN[:, :Tn], kb.cst("sins", 64))
        else:
            kb.dma(CS[:], cosP[:, k * TP:(k + 1) * TP])
            kb.dma(SN[:], sinP[:, k * TP:(k + 1) * TP])

        for l in range(NL):
            S_ = stS if kind == "s" else stP[l]
            last = (kind == "s") or (k == NSEG - 1)

            def modulate(dst_bf, MA, MB, nrm_name):
                b = kb.bank()
                for kc in range(8):
                    sq = kb.nxt("SQ")
                    kb.act(sq[:, :Tn], X[:, kc, :Tn], AF.Square)
                    kb.mm(b[:, :Tn], ONEB[:, :], sq[:, :Tn], start=kc == 0, stop=kc == 7)
                rs = kb.rstd_from(b, Tn, 1.0 / D)
                for kc in range(8):
                    w = kb.nxt("W")
                    kb.tt("dve", w[:, :Tn], X[:, kc, :Tn], rs[:, :Tn], ALU.mult)
                    for s in range(nseq):
                        cs_ = slice(s * L, (s + 1) * L)
                        kb.act(dst_bf[:, kc, cs_], w[:, cs_], AF.Identity,
                               scale=MA[:, l, kc, sidx(s):sidx(s) + 1], bias=MB[:, l, kc, sidx(s):sidx(s) + 1])

            def resid(bank_, mo, MG):
                for s in range(nseq):
                    cs_ = slice(s * L, (s + 1) * L)
                    kb.stt(X[:, mo, cs_], bank_[:, cs_], MG[:, l, mo, sidx(s):sidx(s) + 1], X[:, mo, cs_],
                           ALU.mult, ALU.add)

            def outnorm(src_chunks, inv_n, o0):
                b = kb.bank()
                n = len(src_chunks)
                for i, sc in enumerate(src_chunks):
                    sq = kb.nxt("SQ")
                    kb.act(sq[:, :Tn], sc, AF.Square)
                    kb.mm(b[:, :Tn], ONEB[:, :], sq[:, :Tn], start=i == 0, stop=i == n - 1)
                rs = kb.rstd_from(b, Tn, inv_n)
                for i, sc in enumerate(src_chunks):
                    kb.stt(Ob[:, o0 + i, :Tn], sc, kb.pcol("on", l, o0 + i), rs[:, :Tn], ALU.mult, ALU.mult)

            if kind == "s":
                if l + 1 < NL:
                    emit_casts(l + 1, 0)
                for s in range(NSQ):
                    stg = kb.nxt("STG")
                    kb.dma(stg[:], ck[l, s, :, :])
                    b = kb.bank()
                    kb.tr(b[:, 0:128], stg[:], ident)
                    kb.cp("act", KCT[:, s, :], b[:, 0:128])
                    stg2 = kb.nxt("STG")
                    kb.dma(stg2[:], cv[l, s, :, :])
                    kb.cp(PL, VCT[:, s, :], stg2[:])
                    kb.dma(osk[l, s, 0:124, :], ck[l, s, 4:128, :], eng="pool")
                    kb.dma(osv[l, s, 0:124, :], cv[l, s, 4:128, :], eng="pool")
                for j in range(8):
                    kb.load_T(S_["sr"][:, j, :], sre[l, :, j * 128:(j + 1) * 128], NSQ)
                    kb.load_T(S_["si"][:, j, :], sim[l, :, j * 128:(j + 1) * 128], NSQ)
                for m in range(2):
                    kb.load_T(S_["lh"][:, m, :], slh[l, :, m * 128:(m + 1) * 128], NSQ)
                    kb.load_T(S_["lc"][:, m, :], slc[l, :, m * 128:(m + 1) * 128], NSQ * 3)
                for ch in range(44):
                    kb.load_T(S_["fc"][:, ch, :], sfc[l, :, ch * 128:(ch + 1) * 128], NSQ * 2)

            chk(1)
            modulate(H, MODA1, MODB1, "n1")
            chk(2)

            for i in range(11):
                WIN = sWIN.get()
                b = kb.bank()
                for kc in range(8):
                    kb.mm(b[:, :Tn], WIN[:, kc, :], H[:, kc, :Tn], start=kc == 0, stop=kc == 7)
                kb.cp("act", PROJ[:, i, :Tn], b[:, :Tn])
                sWIN.done()
            WIN = sWIN.get()
            for tb in range(ntb):
                b = kb.bank()
                for kc in range(8):
                    kb.mm(b[:tbs, 0:128], H[:, kc, tb * 128: tb * 128 + tbs], WIN[:, kc, :],
                          start=kc == 0, stop=kc == 7)
                kb.cp("act", VT[:tbs, tb, :], b[:tbs, 0:128])
                if tb == ntb - 1:
                    kb.cp("act", VF[:tbs, :], b[:tbs, 0:128])
            sWIN.done()

            chk(3)
            def rope(srcv, wname, out_bf, out_f):
                sq = kb.nxt("SQ")
                kb.act(sq[:, :Tn], srcv, AF.Square)
                b = kb.bank()
                kb.mm(b[:, :Tn], BLKB[:, :], sq[:, :Tn])
                yield
                rs = kb.rstd_from(b, Tn, 1.0 / 64)
                yield
                qn = kb.nxt("W")
                kb.stt(qn[:, :Tn], srcv, kb.pcol(wname, l), rs[:, :Tn], ALU.mult, ALU.mult)
                b2 = kb.bank()
                kb.mm(b2[:, :Tn], rotm, qn[:, :Tn])
                yield
                t1 = kb.nxt("W")
                t2 = kb.nxt("W")
                kb.tt("dve", t1[:, :Tn], b2[:, :Tn], SN[:, :Tn], ALU.mult)
                kb.tt(PL, t2[:, :Tn], qn[:, :Tn], CS[:, :Tn], ALU.mult)
                yield
                if isinstance(out_bf, list):
                    for R_, o_ in out_bf:
                        kb.tt("dve", o_, t1[R_, :Tn], t2[R_, :Tn], ALU.add)
                else:
                    kb.tt("dve", out_bf, t1[:, :Tn], t2[:, :Tn], ALU.add)
                if out_f is not None:
                    kb.tt(PL, out_f, t1[:, :Tn], t2[:, :Tn], ALU.add)

            def lockstep(gens):
                gens = list(gens)
                while gens:
                    for g in list(gens):
                        try:
                            next(g)
                        except StopIteration:
                            gens.remove(g)

            lockstep([rope(PROJ[:, 0, :Tn], "qn", QR[:, 0, :Tn], None), rope(PROJ[:, 1, :Tn], "qn", QR[:, 1, :Tn], None)])
            lockstep([rope(PROJ[:, 2, :Tn], "qn", QR[:, 2, :Tn], None), rope(PROJ[:, 3, :Tn], "qn", QR[:, 3, :Tn], None)])
            lockstep([rope(PROJ[:, 4, :Tn], "kn", [(slice(0, 64), KRA[0:64, :Tn]), (slice(64, 128), KRB[64:128, :Tn])], KF[:, :Tn])])

            chk(4)
            if kind == "s" and l + 1 < NL:
                emit_casts(l + 1, 1)
            def ssm_tail():
                for m in range(2):
                    b = kb.ps[6 + m]
                    y = kb.nxt("W")
                    kb.stt(y[:, :Tn], PROJ[:, 5 + m, :Tn], kb.pcol("sd", l, m), b[:, :Tn], ALU.mult, ALU.add)
                    kb.act(GLF[:, m, :Tn], y[:, :Tn], AF.Gelu_apprx_tanh)
                    kb.cp("act", GLB[:, m, :Tn], GLF[:, m, :Tn])
                for mo in range(2):
                    b = kb.bank()
                    for kc in range(2):
                        o_ = SW_OFF["glu"] + kc * 256 + mo * 128
                        kb.mm(b[:, :Tn], SWB[:, o_:o_ + 128], GLB[:, kc, :Tn], start=kc == 0, stop=kc == 1)
                    sg = kb.nxt("W")
                    kb.act(sg[:, :Tn], b[:, :Tn], AF.Sigmoid, bias=kb.pcol("bglu", l, mo))
                    kb.tt("dve", OS[:, mo, :Tn], GLF[:, mo, :Tn], sg[:, :Tn], ALU.mult)
                outnorm([OS[:, m, :Tn] for m in range(2)], 1.0 / 256, 4)

            def ssm_items():
                for j in range(8):
                    m = j // 4
                    br = kb.bank()
                    bi = kb.bank()
                    kb.mm(br[:, :Tn], SWB[:, SW_OFF["bre"] + j * 128: SW_OFF["bre"] + (j + 1) * 128], UB[:, m, :Tn])
                    kb.mm(bi[:, :Tn], SWB[:, SW_OFF["bim"] + j * 128: SW_OFF["bim"] + (j + 1) * 128], UB[:, m, :Tn])
                    TABt = tab_tiles[0]
                    er, ei, ckk, skk = [TABt[:, i, :Tn] for i in range(4)]
                    a1 = kb.nxt("W"); a2 = kb.nxt("W")
                    kb.tt("dve", a1[:, :Tn], br[:, :Tn], er, ALU.mult)
                    kb.tt("dve", a2[:, :Tn], bi[:, :Tn], ei, ALU.mult)
                    gr = kb.nxt("W")
                    kb.tt("dve", gr[:, :Tn], a1[:, :Tn], a2[:, :Tn], ALU.subtract)
                    a3 = kb.nxt("W"); a4 = kb.nxt("W")
                    kb.tt("dve", a3[:, :Tn], br[:, :Tn], ei, ALU.mult)
                    kb.tt("dve", a4[:, :Tn], bi[:, :Tn], er, ALU.mult)
                    gi = kb.nxt("W")
                    kb.tt("dve", gi[:, :Tn], a3[:, :Tn], a4[:, :Tn], ALU.add)
                    rho = RHO[:, l, j:j + 1]
                    kb.stt(v3(gr[:, :Tn])[:, :, 0], S_["sr"][:, j, :], rho, v3(gr[:, :Tn])[:, :, 0], ALU.mult, ALU.add)
                    kb.stt(v3(gi[:, :Tn])[:, :, 0], S_["si"][:, j, :], rho, v3(gi[:, :Tn])[:, :, 0], ALU.mult, ALU.add)
                    dec = kb.nxt("W")
                    kb.act(dec[:, :Tn], kb.ZERO[:, :Tn], AF.Identity, bias=rho)
                    kb.memset("dve", v3(dec[:, :Tn])[:, :, 0], 0.0)
                    g2r = a1
                    g2i = a2
                    kb.scan(g2r[:, :Tn], dec[:, :Tn], gr[:, :Tn])
                    kb.scan(g2i[:, :Tn], dec[:, :Tn], gi[:, :Tn])
                    b1 = a3; b2_ = a4; b3 = gr; b4 = gi
                    kb.tt("dve", b1[:, :Tn], g2r[:, :Tn], ckk, ALU.mult)
                    kb.tt("dve", b2_[:, :Tn], g2i[:, :Tn], skk, ALU.mult)
                    hr_ = kb.nxt("HR")
                    hi_ = kb.nxt("HI")
                    kb.tt("dve", hr_[:, :Tn], b1[:, :Tn], b2_[:, :Tn], ALU.subtract)
                    kb.tt(PL, S_["sr"][:, j, :], v3(b1[:, :Tn])[:, :, L - 1], v3(b2_[:, :Tn])[:, :, L - 1], ALU.subtract)
                    kb.tt("dve", b3[:, :Tn], g2i[:, :Tn], ckk, ALU.mult)
                    kb.tt("dve", b4[:, :Tn], g2r[:, :Tn], skk, ALU.mult)
                    kb.stt(hi_[:, :Tn], b3[:, :Tn], -1.0, b4[:, :Tn], ALU.mult, ALU.subtract)
                    yb = kb.ps[6 + m]
                    jj = j % 4
                    kb.mm(yb[:, :Tn], SWB[:, SW_OFF["cre"] + j * 128: SW_OFF["cre"] + (j + 1) * 128], hr_[:, :Tn],
                          start=jj == 0, stop=False)
                    kb.mm(yb[:, :Tn], SWB[:, SW_OFF["cim"] + j * 128: SW_OFF["cim"] + (j + 1) * 128], hi_[:, :Tn],
                          start=False, stop=jj == 3)
                    kb.tt(PL, S_["si"][:, j, :], v3(b3[:, :Tn])[:, :, L - 1], v3(b4[:, :Tn])[:, :, L - 1], ALU.add)
                    tab_load(tab_i[0] + 1)
                    tab_i[0] += 1
                    if last:
                        dre = (osre if kind == "s" else pre)
                        dim_ = (osim if kind == "s" else pim)
                        kb.store_T(dre[l, :, j * 128:(j + 1) * 128], S_["sr"][:, j, :], nseq)
                        kb.store_T(dim_[l, :, j * 128:(j + 1) * 128], S_["si"][:, j, :], nseq)
                    yield
                ssm_tail()
                yield

            def lru_items():
                for m in range(2):
                    xe = XE[:, 0:nseq * (3 + L)].re("p (s l) -> p s l", l=3 + L)
                    kb.cp(PL, xe[:, :, 0:3], S_["lc"][:, m, :].re("p (s t) -> p s t", t=3))
                    kb.cp("act", xe[:, :, 3:3 + L], v3(PROJ[:, 7 + m, :Tn]))
                    xc3 = v3(XC[:, m, :Tn])
                    kb.act(xc3, xe[:, :, 3:3 + L], AF.Identity, scale=kb.pcol("lcw", l, m * 4 + 3), bias=kb.pcol("lcb", l, m))
                    for tap in (2, 1, 0):
                        kb.stt(xc3, xe[:, :, tap:tap + L], kb.pcol("lcw", l, m * 4 + tap), xc3, ALU.mult, ALU.add)
                    kb.cp(PL, S_["lc"][:, m, :].re("p (s t) -> p s t", t=3), xe[:, :, L:L + 3])
                    kb.cp(PL, XCB[:, m, :Tn], XC[:, m, :Tn])
                    if last:
                        dlc = oslc if kind == "s" else plc
                        kb.store_T(dlc[l, :, m * 128:(m + 1) * 128], S_["lc"][:, m, :], nseq * 3)
                    yield
                for m in range(2):
                    b = kb.bank()
                    kb.mm(b[:, :Tn], SWB[:, SW_OFF["wa"] + m * 128: SW_OFF["wa"] + (m + 1) * 128], XCB[:, m, :Tn])
                    r_ = kb.nxt("W")
                    kb.act(r_[:, :Tn], b[:, :Tn], AF.Sigmoid, bias=kb.pcol("ba", l, m))
                    b = kb.bank()
                    kb.mm(b[:, :Tn], SWB[:, SW_OFF["wi"] + m * 128: SW_OFF["wi"] + (m + 1) * 128], XCB[:, m, :Tn])
                    gi_ = kb.nxt("W")
                    kb.act(gi_[:, :Tn], b[:, :Tn], AF.Sigmoid, bias=kb.pcol("bi", l, m))
                    a_ = kb.nxt("W")
                    kb.act(a_[:, :Tn], r_[:, :Tn], AF.Exp, scale=NSP8[:, l, m:m + 1])
                    a2_ = kb.nxt("W")
                    kb.act(a2_[:, :Tn], r_[:, :Tn], AF.Exp, scale=NSP16[:, l, m:m + 1])
                    kb.ts("dve", a2_[:, :Tn], a2_[:, :Tn], 1.0, ALU.min)
                    mu = r_
                    kb.act(mu[:, :Tn], a2_[:, :Tn], AF.Sqrt, scale=-1.0, bias=kb.EPSC[:, 2:3])
                    bb = a2_
                    kb.tt("dve", bb[:, :Tn], mu[:, :Tn], gi_[:, :Tn], ALU.mult)
                    kb.tt("dve", bb[:, :Tn], bb[:, :Tn], XC[:, m, :Tn], ALU.mult)
                    if kind == "p" and k == 0:
                        kb.tt("dve", bb[:, 0:1], gi_[:, 0:1], XC[:, m, 0:1], ALU.mult)
                    tmp = kb.nxt("W")
                    kb.tt("dve", tmp[:, 0:nseq], v3(a_[:, :Tn])[:, :, 0], S_["lh"][:, m, :], ALU.mult)
                    kb.tt("dve", v3(bb[:, :Tn])[:, :, 0], v3(bb[:, :Tn])[:, :, 0], tmp[:, 0:nseq], ALU.add)
                    kb.memset(PL, v3(a_[:, :Tn])[:, :, 0], 0.0)
                    hh_ = gi_
                    kb.scan(hh_[:, :Tn], a_[:, :Tn], bb[:, :Tn])
                    kb.cp(PL, S_["lh"][:, m, :], v3(hh_[:, :Tn])[:, :, L - 1])
                    gy = kb.nxt("W")
                    kb.act(gy[:, :Tn], PROJ[:, 9 + m, :Tn], AF.Gelu_apprx_tanh)
                    kb.tt("dve", OL[:, m, :Tn], hh_[:, :Tn], gy[:, :Tn], ALU.mult)
                    if last:
                        dlh = oslh if kind == "s" else plh
                        kb.store_T(dlh[l, :, m * 128:(m + 1) * 128], S_["lh"][:, m, :], nseq)
                    yield
                outnorm([OL[:, m, :Tn] for m in range(2)], 1.0 / 256, 6)
                yield

            def step(gen, pool):
                kb.cur_pool = pool
                try:
                    next(gen)
                    alive = True
                except StopIteration:
                    alive = False
                kb.cur_pool = None
                return alive

            SWB = sSW.get()
            kb.cp(PL, UB[:, :, :Tn], PROJ[:, 5:7, :Tn])
            if kind == "p":
                def scores(qb, c):
                    first = (k == 0 and qb == 0)
                    qs = slice(qb * 128, (qb + 1) * 128)
                    bsc_ = kb.bank()
                    whiches = (["prev"] if not first else []) + ["diag"]
                    col = 0
                    for wch in whiches:
                        for half in range(2):
                            KRh = KRA if half == 0 else KRB
                            if wch == "prev":
                                lhs = S_["kprevA" if half == 0 else "kprevB"][:, :] if qb == 0 else KRh[:, (qb - 1) * 128: qb * 128]
                            else:
                                lhs = KRh[:, qs]
                            kb.mm(bsc_[:, col:col + 128], lhs, QR[:, c, qs])
                            col += 128
                    return bsc_, whiches, col

                def rest1(qb, c, sc):
                    bsc_, whiches, col = sc
                    qs = slice(qb * 128, (qb + 1) * 128)
                    n = col
                    E = kb.nxt("E")
                    kb.act(E[:, :n], bsc_[:, :n], AF.Exp, scale=0.125)
                    kb.tt(PL, E[:, :n], E[:, :n], MASK4[:, 512 - n:512], ALU.mult)
                    b2 = kb.bank()
                    nk = n // 256
                    for i in range(nk):
                        if whiches[i] == "prev":
                            vt = S_["vprev"][:, :] if qb == 0 else VT[:, qb - 1, :]
                        else:
                            vt = VT[:, qb, :]
                        kb.mm(b2[:, 0:256], vt, E[:, i * 256:(i + 1) * 256], start=i == 0, stop=i == nk - 1)
                    for i in range(nk):
                        kb.mm(b2[:, 256:512], ONEB[:, :], E[:, i * 256:(i + 1) * 256], start=i == 0, stop=i == nk - 1)
                    return b2

                def rest2(qb, c, b2):
                    qs = slice(qb * 128, (qb + 1) * 128)
                    for half in range(2):
                        R = slice(half * 64, half * 64 + 64)
                        cc = half * 128
                        kb.act(DEN[R, 0:128], b2[R, 256 + cc:256 + cc + 128], AF.Identity, bias=SINKE[R, l, c:c + 1])
                    kb.recip(DEN[:, 0:128], DEN[:, 0:128])
                    for half in range(2):
                        R = slice(half * 64, half * 64 + 64)
                        cc = half * 128
                        kb.tt("dve", OA[R, c, qs], b2[R, cc:cc + 128], DEN[R, 0:128], ALU.mult)

                def attn_items():
                    work = [(qb, c) for qb in range(4) for c in range(4)]
                    pend = scores(*work[0])
                    for wi, (qb, c) in enumerate(work):
                        nxt_sc = scores(*work[wi + 1]) if wi + 1 < len(work) else None
                        b2_ = rest1(qb, c, pend)
                        pend = nxt_sc
                        yield
                        rest2(qb, c, b2_)
                        yield
                    kb.cp(PL, S_["kprevA"][:, :], KRA[:, 384:512])
                    kb.cp(PL, S_["kprevB"][:, :], KRB[:, 384:512])
                    kb.cp(PL, S_["vprev"][:, :], VT[:, 3, :])
                    if last:
                        kb.store_T(pk[l, :, :], KF[:, 384:512], 128)
                        kb.dma(pv[l, :, :], VF[:, :])
                    outnorm([OA[:, c, :Tn] for c in range(4)], 1.0 / 512, 0)
                    yield

                ga, gs, gl = attn_items(), ssm_items(), lru_items()
                for i_ in range(33):
                    step(ga, "attn")
                    if i_ % 2 == 0:
                        step(gs, "ssm")
                    elif (i_ // 2) % 2 == 0:
                        step(gl, "lru")
                while step(ga, "attn"):
                    pass
                while step(gs, "ssm"):
                    pass
                while step(gl, "lru"):
                    pass
            else:
                bsc = (kb.bank(), kb.bank())
                for c in range(4):
                    for half in range(2):
                        R = slice(half * 64, half * 64 + 64)
                        hh = c * 2 + half
                        for s in range(NSQ):
                            kb.mm(bsc[half][:, hh * 64 + s * 4: hh * 64 + s * 4 + 4], KCT[R, s, :], QR[R, c, s * 4:(s + 1) * 4])
                Ec = kb.nxt("E")
                for half in range(2):
                    kb.act(Ec[:, :].re("p (c h x) -> p c h x", c=4, h=2)[:, :, half, :],
                           bsc[half][:, :].re("p (c h x) -> p c h x", c=4, h=2)[:, :, half, :], AF.Exp, scale=0.125)
                kb.tt("dve", Ec[:, :], Ec[:, :], MASKC[:, :], ALU.mult)
                chk(4.1)
                bsn = (kb.bank(), kb.bank())
                for c in range(4):
                    for half in range(2):
                        R = slice(half * 64, half * 64 + 64)
                        hh = c * 2 + half
                        kb.mm(bsn[half][0:64, hh * 64:(hh + 1) * 64], (KRA if half == 0 else KRB)[R, 0:64], QR[R, c, 0:64])
                En = kb.nxt("E")
                for half in range(2):
                    kb.act(En[0:64, :].re("p (c h x) -> p c h x", c=4, h=2)[:, :, half, :],
                           bsn[half][0:64, :].re("p (c h x) -> p c h x", c=4, h=2)[:, :, half, :], AF.Exp, scale=0.125)
                kb.tt("dve", En[0:64, :], En[0:64, :], MASKN[:, :], ALU.mult)
                chk(4.2)
                bnc = kb.bank()
                Ec4 = Ec[:, :].re("p (h s t) -> p h s t", h=8, s=NSQ)
                for s in range(NSQ):
                    kb.mm(bnc[:, s * 32:(s + 1) * 32].re("p (h t) -> p h t", h=8), VCT[:, s, :], Ec4[:, :, s, :])
                NUMS = kb.nxt("W")
                kb.cp("act", NUMS[:, :], bnc[:, :])
                chk(4.3)
                bnn = kb.bank()
                kb.mm(bnn[:, :], VT[0:64, 0, :], En[0:64, :])
                bdn = kb.bank()
                kb.mm(bdn[:, :], ONEB[:, :], Ec[:, :], start=True, stop=False)
                kb.mm(bdn[:, :], ONEB[0:64, :], En[0:64, :], start=False, stop=True)
                chk(4.4)
                for c in range(4):
                    for half in range(2):
                        R = slice(half * 64, half * 64 + 64)
                        hh = c * 2 + half
                        hc = slice(hh * 64, (hh + 1) * 64)
                        kb.ts("dve", DEN[R, hc], bdn[R, hc], SINKE[R, l, c:c + 1], ALU.add)
                        kb.recip(DEN[R, hc], DEN[R, hc])
                        w = kb.nxt("W")
                        kb.tt("dve", w[R, 0:64].re("p (s t) -> p s t", t=4),
                              NUMS[R, :].re("p (s h t) -> p s h t", s=NSQ, h=8)[:, :, hh, :],
                              bnn[R, hc].re("p (s t) -> p s t", t=4), ALU.add)
                        kb.tt("dve", OA[R, c, 0:64], w[R, 0:64], DEN[R, hc], ALU.mult)
                chk(4.5)
                b = kb.bank()
                kb.tr(b[0:64, 0:128], KF[:, 0:64], ident)
                stg = kb.nxt("STG")
                kb.cp("act", stg[0:64, :], b[0:64, 0:128])
                for s in range(NSQ):
                    kb.dma(osk[l, s, 124:128, :], stg[s * 4:(s + 1) * 4, :])
                    kb.dma(osv[l, s, 124:128, :], VF[s * 4:(s + 1) * 4, :])
                outnorm([OA[:, c, :Tn] for c in range(4)], 1.0 / 512, 0)
                for _ in ssm_items():
                    pass
                for _ in lru_items():
                    pass
            sSW.done()

            chk(8)
            if kind == "s" and l + 1 < NL:
                emit_casts(l + 1, 2)
            for mo in range(8):
                WO = sWO.get()
                b = kb.bank()
                for kc in range(8):
                    kb.mm(b[:, :Tn], WO[:, kc, :], Ob[:, kc, :Tn], start=kc == 0, stop=kc == 7)
                resid(b, mo, MODG1)
                sWO.done()

            modulate(H, MODA2, MODB2, "n2")

            chk(9)
            for j in range(NJ):
                cc_t = {}
                for half in range(2):
                    WUP = sUP.get()
                    ch = half * NJ + j
                    b = kb.bank()
                    for kc in range(8):
                        kb.mm(b[:, :Tn], WUP[:, kc, :], H[:, kc, :Tn], start=kc == 0, stop=kc == 7)
                    sUP.done()
                    ue = kb.nxt("UE")
                    ue3 = ue[:, 0:nseq * (2 + L)].re("p (s l) -> p s l", l=2 + L)
                    fcst = S_["fc"][:, ch, :].re("p (s t) -> p s t", t=2)
                    kb.cp(PL, ue3[:, :, 0:2], fcst)
                    kb.cp("act", ue3[:, :, 2:2 + L], v3(b[:, :Tn]))
                    cc = kb.nxt("W")
                    o_, _s = PP_OFF["fcw"]
                    wcol = lambda tap: kb.PP[:, l * NPP + o_ + ch * 3 + tap: l * NPP + o_ + ch * 3 + tap + 1]
                    kb.act(cc[:, :Tn], b[:, :Tn], AF.Identity, scale=wcol(2), bias=kb.pcol("fcb", l, ch))
                    cc3 = v3(cc[:, :Tn])
                    kb.stt(cc3, ue3[:, :, 1:1 + L], wcol(1), cc3, ALU.mult, ALU.add)
                    kb.stt(cc3, ue3[:, :, 0:L], wcol(0), cc3, ALU.mult, ALU.add)
                    kb.cp(PL, fcst, ue3[:, :, L:L + 2])
                    cc_t[half] = cc
                    if last:
                        dfc = osfc if kind == "s" else pfc
                        kb.store_T(dfc[l, :, ch * 128:(ch + 1) * 128], S_["fc"][:, ch, :], nseq * 2)
                gg = kb.nxt("W")
                kb.act(gg[:, :Tn], cc_t[0][:, :Tn], AF.Gelu_apprx_tanh)
                kb.tt("dve", G[:, j, :Tn], gg[:, :Tn], cc_t[1][:, :Tn], ALU.mult)

            chk(10)
            if kind == "s" and l + 1 < NL:
                emit_casts(l + 1, 3)
            for mo in range(8):
                b = kb.bank()
                for hj in range(2):
                    WDN = sDN.get()
                    for j2 in range(11):
                        j = hj * 11 + j2
                        kb.mm(b[:, :Tn], WDN[:, j2, :], G[:, j, :Tn], start=j == 0, stop=j == NJ - 1)
                    sDN.done()
                resid(b, mo, MODG2)

        dst = ys if kind == "s" else yp
        for tb in range(ntb):
            xt = kb.nxt("XT")
            for half in range(2):
                b = kb.bank()
                for q4 in range(4):
                    kc = half * 4 + q4
                    kb.tr(b[:tbs, q4 * 128:(q4 + 1) * 128], X[:, kc, tb * 128: tb * 128 + tbs], ident)
                kb.cp("act", xt[:tbs, half * 512:(half + 1) * 512], b[:tbs, :])
            kb.dma(dst[r0 + tb * 128: r0 + tb * 128 + tbs, :], xt[:tbs, :])


_CACHE = {}


def kernel(**inp):
    inp = {k: np.asarray(v) for k, v in inp.items()}
    f32 = np.float32
    if "nc" not in _CACHE:
        _CACHE["nc"] = build()
    nc = _CACHE["nc"]
    pp, sw = _host_params(inp)
    cst, cosP, sinP, msk = _host_consts()
    in_maps = []
    for c in range(8):
        b = c // 2
        sq = slice(c * NSQ, (c + 1) * NSQ)
        m = {
            "xp": np.ascontiguousarray(inp["x_prompt"][b]),
            "xs": np.ascontiguousarray(inp["x_sample"][sq].reshape(TS, D)),
            "ck": np.ascontiguousarray(inp["cache_k"][:, sq].reshape(NL, NSQ, 128, 128)),
            "cv": np.ascontiguousarray(inp["cache_v"][:, sq].reshape(NL, NSQ, 128, 128)),
            "sre": np.ascontiguousarray(inp["state_ssm_re"][:, sq].reshape(NL, NSQ, 1024)),
            "sim": np.ascontiguousarray(inp["state_ssm_im"][:, sq].reshape(NL, NSQ, 1024)),
            "slh": np.ascontiguousarray(inp["state_lru_h"][:, sq]),
            "slc": np.ascontiguousarray(inp["state_lru_conv"][:, sq].reshape(NL, NSQ * 3, 256)),
            "sfc": np.ascontiguousarray(inp["state_ffn_conv"][:, sq].reshape(NL, NSQ * 2, 5632)),
            "c17": np.ascontiguousarray(np.concatenate([inp["c_prompt"][b:b + 1], inp["c_sample"][sq]], axis=0)),
            "wada": inp["w_ada"], "win": inp["w_in"], "wo": inp["w_o"], "wup": inp["ffn_w_up"], "wdn": inp["ffn_w_down"],
            "pp": pp, "sw": sw, "cst": cst, "msk": msk, "cosP": cosP, "sinP": sinP,
        }
        in_maps.append({k_: np.ascontiguousarray(v, dtype=f32) for k_, v in m.items()})
    import os
    ncores = int(os.environ.get("KCORES", "8"))
    res = run_bass_kernel_spmd(nc, in_maps[:ncores], core_ids=list(range(ncores)))
    R = list(res.results)
    while len(R) < 8:
        R.append({k_: np.zeros_like(v) for k_, v in R[0].items()})
    ev = [R[2 * b] for b in range(4)]
    y_p = np.stack([r["yp"] for r in ev], 0)
    y_s = np.concatenate([r["ys"].reshape(NSQ, 4, D) for r in R], 0)

    def pst(name, shape):
        return np.stack([r[name] for r in ev], 1).reshape(shape)

    def sst(name, shape):
        return np.concatenate([r[name].reshape((NL, NSQ) + shape) for r in R], 1)

    outs = (y_p, y_s,
            pst("pk", (NL, 4, 128, 2, 64)), pst("pv", (NL, 4, 128, 2, 64)),
            pst("pre", (NL, 4, 16, 64)), pst("pim", (NL, 4, 16, 64)),
            pst("plh", (NL, 4, 256)), pst("plc", (NL, 4, 3, 256)), pst("pfc", (NL, 4, 2, 5632)),
            sst("osk", (128, 2, 64)), sst("osv", (128, 2, 64)),
            sst("osre", (16, 64)), sst("osim", (16, 64)),
            sst("oslh", (256,)), sst("oslc", (3, 256)), sst("osfc", (2, 5632)))
    return tuple(np.ascontiguousarray(o, dtype=f32) for o in outs)
```

```python
import contextlib
import math
import numpy as np
import ml_dtypes
import concourse.bass as bass
import concourse.mybir as mybir
from concourse.bass_utils import run_bass_kernel_spmd

F32 = mybir.dt.float32
BF16 = mybir.dt.bfloat16
AF = mybir.ActivationFunctionType
ALU = mybir.AluOpType

NL = 4
D = 1024
DFF = 2816
NJ = 22
TP = 512
NSEG = 8
TS = 64
NSQ = 16
EPS = 1e-6


class Buf:
    __slots__ = ("name", "w", "rd")

    def __init__(self, name):
        self.name = name
        self.w = None
        self.rd = []


class Op:
    __slots__ = ("eng", "fn", "deps", "idx", "dma", "sem", "semval", "prevdma", "wkey")

    def __init__(self, eng, fn, dma):
        self.eng = eng
        self.fn = fn
        self.deps = set()
        self.dma = dma
        self.sem = None
        self.semval = None
        self.prevdma = None


class Sched:
    ENG = ("pe", "act", "dve", "pool", "sp")

    def __init__(self, nc, n_dma_sems=14):
        self.nc = nc
        self.ops = []
        self.n_dma_sems = n_dma_sems

    def add(self, eng, fn, reads=(), writes=(), dma=False):
        op = Op(eng, fn, dma)
        op.idx = len(self.ops)
        op.wkey = id(writes[0]) if len(writes) else None
        for b in reads:
            if b.w is not None:
                op.deps.add(b.w)
        for b in writes:
            if b.w is not None:
                op.deps.add(b.w)
            for r in b.rd:
                op.deps.add(r)
        for b in reads:
            if not dma:
                b.rd = [r for r in b.rd if r.dma or r.eng != eng]
            b.rd.append(op)
        for b in writes:
            b.w = op
            b.rd = []
        op.deps.discard(op)
        self.ops.append(op)
        return op

    def emit(self):
        nc = self.nc
        ops = self.ops
        need = set()
        for op in ops:
            for d in op.deps:
                if d.eng == "pe" and op.eng == "pe" and not d.dma and not op.dma:
                    continue
                need.add(d)
        cnt = {e: 0 for e in self.ENG}
        dmacnt = {e: 0 for e in self.ENG}
        dma_last = {}
        for op in ops:
            if op.dma:
                slot = dmacnt[op.eng] % self.n_dma_sems
                dmacnt[op.eng] += 1
                key = (op.eng, slot)
                prev = dma_last.get(key)
                op.prevdma = prev
                op.sem = key
                op.semval = (prev.semval if prev else 0) + 16
                dma_last[key] = op
            elif op in need:
                cnt[op.eng] += 1
                op.sem = (op.eng, None)
                op.semval = cnt[op.eng]
        with contextlib.ExitStack() as st:
            sems = {}
            for e in self.ENG:
                sems[(e, None)] = st.enter_context(nc.semaphore("s_" + e))
                for k in range(min(self.n_dma_sems, dmacnt[e])):
                    sems[(e, k)] = st.enter_context(nc.semaphore("d_%s%d" % (e, k)))
            block = st.enter_context(nc.Block())
            decos = {"pe": block.tensor, "act": block.scalar, "dve": block.vector,
                     "pool": block.gpsimd, "sp": block.sync}
            last_dmas = list(dma_last.values())
            for e in self.ENG:
                mine = [op for op in ops if op.eng == e]

                def body(eng, mine=mine, e=e):
                    waited = {}

                    def wait(key, val):
                        if waited.get(key, 0) >= val:
                            return
                        eng.wait_ge(sems[key], val)
                        waited[key] = val

                    for i_, op in enumerate(mine):
                        for d in sorted(op.deps, key=lambda o: o.idx):
                            if d.sem is None:
                                continue
                            wait(d.sem, d.semval)
                        if e == "pe":
                            for nx in mine[i_ + 1:i_ + 40]:
                                if nx.wkey != op.wkey:
                                    break
                                bad = False
                                for d in nx.deps:
                                    if d.sem is None:
                                        continue
                                    if d.idx > op.idx and not (d.eng == "pe" and not d.dma):
                                        bad = True
                                if bad:
                                    break
                                for d in sorted(nx.deps, key=lambda o: o.idx):
                                    if d.sem is not None and d.idx < op.idx - 48:
                                        wait(d.sem, d.semval)
                        if op.dma and op.prevdma is not None:
                            wait(op.prevdma.sem, op.prevdma.semval)
                        ins = op.fn(eng)
                        if op.dma:
                            ins.then_inc(sems[op.sem], 16)
                        elif op.sem is not None:
                            ins.then_inc(sems[op.sem], 1)
                    if e == "sp":
                        for op in last_dmas:
                            wait(op.sem, op.semval)

                decos[e](body)


class V:
    __slots__ = ("ap", "bufs")

    def __init__(self, ap, bufs):
        self.ap = ap
        self.bufs = bufs

    def __getitem__(self, idx):
        return V(self.ap[idx], self.bufs)

    def re(self, pat, **kw):
        return V(self.ap.rearrange(pat, **kw), self.bufs)


class T:
    def __init__(self, ap, name, split=0, axis=1):
        self.t = ap
        self.split = split
        self.axis = axis
        self.bufs = [Buf("%s%d" % (name, i)) for i in range(split)] if split else [Buf(name)]

    def __getitem__(self, idx):
        ap = self.t[idx]
        if self.split:
            ax = self.axis
            if isinstance(idx, tuple):
                i1 = idx[ax] if len(idx) > ax else slice(None)
            else:
                i1 = idx if ax == 0 else slice(None)
            if isinstance(i1, int):
                bufs = [self.bufs[i1]]
            else:
                bufs = self.bufs[i1]
        else:
            bufs = self.bufs
        return V(ap, bufs)


def _ap(x):
    return x.ap if isinstance(x, V) else x


def _bufs(*xs):
    out = []
    for x in xs:
        if isinstance(x, V):
            out.extend(x.bufs)
    return out


def _pp_layout():
    sizes = [("n1", 8), ("n2", 8), ("on", 8), ("qn", 1), ("kn", 1), ("sk", 4), ("are", 8), ("aim", 8),
             ("ldt", 8), ("sd", 2), ("bglu", 2), ("lcw", 8), ("lcb", 2), ("ba", 2), ("bi", 2), ("lam", 2),
             ("fcw", 132), ("fcb", 44), ("bada", 48)]
    off = {}
    o = 0
    for n, s in sizes:
        off[n] = (o, s)
        o += s
    return off, o


PP_OFF, NPP = _pp_layout()
SW_OFF = {"bre": 0, "bim": 1024, "cre": 2048, "cim": 3072, "wa": 4096, "wi": 4352, "glu": 4608}
NSW = 5120
CST_OFF = {"ident": 0, "ones": 128, "blk": 256, "rot": 384, "coss": 512, "sins": 576}
NCST = 640


def _perm_attn_rows():
    idx = []
    for c in range(4):
        idx += list(range(64 * c, 64 * c + 64)) + list(range(256 + 64 * c, 256 + 64 * c + 64))
    return np.array(idx)


def _fm(v, nch):
    return np.ascontiguousarray(v.reshape(nch, 128).T)


def _host_params(inp):
    f32 = np.float32
    pp = np.zeros((128, NL, NPP), f32)
    sw = np.zeros((NL, 128, NSW), f32)
    perm = _perm_attn_rows()

    def put(l, name, arr):
        o, s = PP_OFF[name]
        pp[:, l, o:o + s] = arr.reshape(128, s)

    for l in range(NL):
        put(l, "n1", _fm(inp["norm1"][l], 8))
        put(l, "n2", _fm(inp["norm2"][l], 8))
        on = inp["out_norm"][l].copy()
        on[:512] = on[:512][perm]
        put(l, "on", _fm(on, 8))
        put(l, "qn", np.tile(inp["q_norm"][l], 2))
        put(l, "kn", np.tile(inp["k_norm"][l], 2))
        sk = np.zeros((128, 4), f32)
        for c in range(4):
            sk[:64, c] = inp["sinks"][l][c]
            sk[64:, c] = inp["sinks"][l][4 + c]
        put(l, "sk", sk)
        put(l, "are", _fm(inp["ssm_a_re"][l].reshape(-1), 8))
        put(l, "aim", _fm(inp["ssm_a_im"][l].reshape(-1), 8))
        put(l, "ldt", _fm(np.repeat(inp["ssm_log_dt"][l], 64), 8))
        put(l, "sd", _fm(inp["ssm_d"][l], 2))
        put(l, "bglu", _fm(inp["ssm_b_glu"][l], 2))
        lcw = np.zeros((128, 2, 4), f32)
        for j in range(4):
            lcw[:, :, j] = _fm(inp["lru_conv_w"][l][j], 2)
        put(l, "lcw", lcw)
        put(l, "lcb", _fm(inp["lru_conv_b"][l], 2))
        put(l, "ba", _fm(inp["lru_b_a"][l], 2))
        put(l, "bi", _fm(inp["lru_b_i"][l], 2))
        put(l, "lam", _fm(inp["lru_lambda"][l], 2))
        fcw = np.zeros((128, 44, 3), f32)
        for j in range(3):
            fcw[:, :, j] = _fm(inp["ffn_conv_w"][l][j], 44)
        put(l, "fcw", fcw)
        put(l, "fcb", _fm(inp["ffn_conv_b"][l], 44))
        put(l, "bada", _fm(inp["b_ada"][l], 48))
        bre = inp["ssm_b_re"][l]
        bim = inp["ssm_b_im"][l]
        cre = inp["ssm_c_re"][l]
        cim = inp["ssm_c_im"][l]
        for j in range(8):
            for gl in range(2):
                g = 2 * j + gl
                r0 = 32 * (j % 4) + 16 * gl
                sw[l, r0:r0 + 16, SW_OFF["bre"] + j * 128 + gl * 64: SW_OFF["bre"] + j * 128 + gl * 64 + 64] = bre[g].T
                sw[l, r0:r0 + 16, SW_OFF["bim"] + j * 128 + gl * 64: SW_OFF["bim"] + j * 128 + gl * 64 + 64] = bim[g].T
                sw[l, gl * 64:gl * 64 + 64, SW_OFF["cre"] + j * 128 + r0: SW_OFF["cre"] + j * 128 + r0 + 16] = cre[g].T
                sw[l, gl * 64:gl * 64 + 64, SW_OFF["cim"] + j * 128 + r0: SW_OFF["cim"] + j * 128 + r0 + 16] = cim[g].T
        for m in range(2):
            for hb in range(2):
                h = 2 * m + hb
                sw[l, hb * 64:hb * 64 + 64, SW_OFF["wa"] + m * 128 + hb * 64: SW_OFF["wa"] + m * 128 + hb * 64 + 64] = inp["lru_w_a"][l][h]
                sw[l, hb * 64:hb * 64 + 64, SW_OFF["wi"] + m * 128 + hb * 64: SW_OFF["wi"] + m * 128 + hb * 64 + 64] = inp["lru_w_i"][l][h]
        wg = inp["ssm_w_glu"][l]
        for kc in range(2):
            sw[l, :, SW_OFF["glu"] + kc * 256: SW_OFF["glu"] + kc * 256 + 256] = wg[kc * 128:(kc + 1) * 128, :]
    return pp.reshape(128, NL * NPP), sw


def _host_consts():
    f32 = np.float32
    cst = np.zeros((128, NCST), f32)
    cst[:, 0:128] = np.eye(128, dtype=f32)
    cst[:, 128:256] = 1.0
    cst[0:64, 256:320] = 1.0
    cst[64:128, 320:384] = 1.0
    rot = np.zeros((128, 128), f32)
    for m in range(128):
        if (m % 64) < 32:
            rot[m + 32, m] = -1.0
        else:
            rot[m - 32, m] = 1.0
    cst[:, 384:512] = rot
    s = np.arange(128)[:, None]
    q = np.arange(128)[None, :]
    prev = (s > q).astype(f32)
    diag = (s <= q).astype(f32)
    msk = np.zeros((128, 1536), f32)
    msk[:, 0:512] = np.concatenate([prev, prev, diag, diag], axis=1)
    t = np.arange(4)[None, :]
    mc = (np.arange(128)[:, None] > t).astype(f32)
    msk[:, 512:1024] = np.tile(mc, (1, 8 * 16))
    kk = np.arange(64)
    ks, kt = kk // 4, kk % 4
    mn = ((ks[:, None] == ks[None, :]) & (kt[:, None] <= kt[None, :])).astype(f32)
    msk[0:64, 1024:1536] = np.tile(mn, (1, 8))
    half = 32
    inv = (f32(10000.0) ** (-np.arange(half, dtype=f32) / f32(half))).astype(f32)
    fi = (np.arange(128) % 64) % 32

    def tabs(pos):
        ang = (pos.astype(f32)[:, None] * inv[None, :]).astype(f32)
        return np.ascontiguousarray(np.cos(ang).astype(f32)[:, fi].T), np.ascontiguousarray(np.sin(ang).astype(f32)[:, fi].T)

    cp, sp = tabs(np.arange(4096))
    cs, ss = tabs(8192 + np.arange(4))
    cst[:, 512:576] = np.tile(cs, (1, 16))
    cst[:, 576:640] = np.tile(ss, (1, 16))
    return cst, cp, sp, msk


class KB:
    def __init__(self):
        self.nc = bass.Bass("TRN2", target_bir_lowering=False)
        self.S = Sched(self.nc)
        self.st = contextlib.ExitStack()
        self.bank_i = 0
        self.pool_i = {}
        self.rot = {}

    def dram(self, name, shape, dtype=F32, kind="ExternalInput", split=0):
        h = self.nc.dram_tensor(name, list(shape), dtype, kind=kind)
        return T(h.ap(), name, split, axis=0)

    def sb(self, name, shape, dtype=F32, split=0):
        t = self.st.enter_context(self.nc.sbuf_tensor(name, list(shape), dtype))
        return T(t, name, split)

    def rots(self, name, n, shape, dtype=F32):
        self.rot[name] = ([self.sb("%s%d" % (name, i), shape, dtype) for i in range(n)], 0)

    def nxt(self, name):
        lst, i = self.rot[name]
        self.rot[name] = (lst, i + 1)
        return lst[i % len(lst)]

    POOLS = {"attn": [0, 1, 2], "ssm": [3, 4], "lru": [5]}
    cur_pool = None

    def bank(self):
        if self.cur_pool is not None:
            lst = self.POOLS[self.cur_pool]
            i = self.pool_i.get(self.cur_pool, 0)
            self.pool_i[self.cur_pool] = i + 1
            return self.ps[lst[i % len(lst)]]
        b = self.ps[self.bank_i % 6]
        self.bank_i += 1
        return b

    def mm(self, out, lhsT, rhs, start=True, stop=True):
        self.S.add("pe", lambda e: e.matmul(out.ap, lhsT.ap, rhs.ap, start=start, stop=stop),
                   reads=_bufs(lhsT, rhs), writes=_bufs(out))

    def tr(self, out, in_, ident):
        self.S.add("pe", lambda e: e.transpose(out.ap, in_.ap, ident.ap), reads=_bufs(in_, ident), writes=_bufs(out))

    def act(self, out, in_, func, scale=None, bias=None):
        kw = {}
        if scale is not None:
            kw["scale"] = _ap(scale)
        if bias is not None:
            kw["bias"] = _ap(bias)
        self.S.add("act", lambda e: e.activation(out.ap, in_.ap, func, **kw),
                   reads=_bufs(in_, scale, bias), writes=_bufs(out))

    def tt(self, eng, out, a, b, op):
        self.S.add(eng, lambda e: e.tensor_tensor(out.ap, a.ap, b.ap, op), reads=_bufs(a, b), writes=_bufs(out))

    def ts(self, eng, out, a, s1, op0, s2=None, op1=None):
        if op1 is None:
            fn = lambda e: e.tensor_scalar(out.ap, a.ap, _ap(s1), None, op0)
        else:
            fn = lambda e: e.tensor_scalar(out.ap, a.ap, _ap(s1), _ap(s2), op0, op1)
        self.S.add(eng, fn, reads=_bufs(a, s1, s2), writes=_bufs(out))

    def stt(self, out, a, scalar, b, op0, op1):
        self.S.add("dve", lambda e: e.scalar_tensor_tensor(out.ap, a.ap, _ap(scalar), b.ap, op0, op1),
                   reads=_bufs(a, scalar, b), writes=_bufs(out))

    def cp(self, eng, out, in_):
        if eng == "act":
            self.S.add("act", lambda e: e.copy(out.ap, in_.ap), reads=_bufs(in_), writes=_bufs(out))
        else:
            self.S.add(eng, lambda e: e.tensor_copy(out.ap, in_.ap), reads=_bufs(in_), writes=_bufs(out))

    def scan(self, out, d0, d1):
        self.S.add("dve", lambda e: e.tensor_tensor_scan(out.ap, d0.ap, d1.ap, 0.0, ALU.mult, ALU.add),
                   reads=_bufs(d0, d1), writes=_bufs(out))

    def recip(self, out, in_):
        self.S.add("dve", lambda e: e.reciprocal(out.ap, in_.ap), reads=_bufs(in_), writes=_bufs(out))

    def memset(self, eng, out, val):
        self.S.add(eng, lambda e: e.memset(out.ap, val), writes=_bufs(out))

    def dma(self, out, in_, eng="sp", nc_ok=False, **kw):
        if nc_ok:
            kw["allow_slow_non_contiguous"] = True
        self.S.add(eng, lambda e: e.dma_start(out=out.ap, in_=in_.ap, **kw), reads=_bufs(in_), writes=_bufs(out), dma=True)

    def pcol(self, name, l, i=0, n=1):
        o, s = PP_OFF[name]
        return self.PP[:, l * NPP + o + i: l * NPP + o + i + n]

    def cst(self, name, n=128, rows=slice(None)):
        o = CST_OFF[name]
        return self.CST[rows, o:o + n]

    def rstd_from(self, bank, Tn, inv_n):
        rs = self.nxt("RS")
        self.act(rs[:, :Tn], bank[:, :Tn], AF.Ln, scale=inv_n, bias=self.EPSC[:, 0:1])
        self.act(rs[:, :Tn], rs[:, :Tn], AF.Exp, scale=-0.5)
        return rs

    def load_T(self, dst, src, n):
        stg = self.nxt("STG")
        self.dma(stg[:n, :], src)
        b = self.bank()
        self.tr(b[:, :n], stg[:n, :], self.cst("ident", n, slice(0, n)))
        self.cp("act", dst, b[:, :n])

    def store_T(self, dst, src, n):
        b = self.bank()
        self.tr(b[:n, 0:128], src, self.cst("ident"))
        stg = self.nxt("STG")
        self.cp("act", stg[:n, :], b[:n, 0:128])
        self.dma(dst, stg[:n, :])


class Stream:
    def __init__(self, kb, name, nslots, shape, dtype, items):
        self.kb = kb
        self.tiles = [kb.sb("%s_%d" % (name, i), shape, dtype) for i in range(nslots)]
        self.items = items
        self.i = 0
        self.n = nslots

    def _load(self, idx):
        if idx < len(self.items):
            t = self.tiles[idx % self.n]
            self.kb.dma(t[:], self.items[idx])

    def start(self):
        for i in range(self.n):
            self._load(i)

    def get(self):
        return self.tiles[self.i % self.n]

    def done(self):
        self._load(self.i + self.n)
        self.i += 1


def build():
    kb = KB()
    nc = kb.nc
    st = kb.st
    with st:
        _build(kb)
    return nc


def _build(kb):
    nc = kb.nc
    xp = kb.dram("xp", [4096, D])
    xs = kb.dram("xs", [TS, D])
    ck = kb.dram("ck", [NL, NSQ, 128, 128])
    cv = kb.dram("cv", [NL, NSQ, 128, 128])
    sre = kb.dram("sre", [NL, NSQ, 1024])
    sim = kb.dram("sim", [NL, NSQ, 1024])
    slh = kb.dram("slh", [NL, NSQ, 256])
    slc = kb.dram("slc", [NL, NSQ * 3, 256])
    sfc = kb.dram("sfc", [NL, NSQ * 2, 5632])
    c17 = kb.dram("c17", [17, D])
    wada = kb.dram("wada", [NL, D, 6 * D])
    win = kb.dram("win", [NL, D, 1536])
    wo = kb.dram("wo", [NL, D, D])
    wup = kb.dram("wup", [NL, D, 2 * DFF])
    wdn = kb.dram("wdn", [NL, DFF, D])
    ppd = kb.dram("pp", [128, NL * NPP])
    swd = kb.dram("sw", [NL, 128, NSW])
    cstd = kb.dram("cst", [128, NCST])
    mskd = kb.dram("msk", [128, 1536])
    cosP = kb.dram("cosP", [128, 4096])
    sinP = kb.dram("sinP", [128, 4096])
    O = "ExternalOutput"
    yp = kb.dram("yp", [4096, D], kind=O)
    ys = kb.dram("ys", [TS, D], kind=O)
    pk = kb.dram("pk", [NL, 128, 128], kind=O)
    pv = kb.dram("pv", [NL, 128, 128], kind=O)
    pre = kb.dram("pre", [NL, 1, 1024], kind=O)
    pim = kb.dram("pim", [NL, 1, 1024], kind=O)
    plh = kb.dram("plh", [NL, 1, 256], kind=O)
    plc = kb.dram("plc", [NL, 3, 256], kind=O)
    pfc = kb.dram("pfc", [NL, 2, 5632], kind=O)
    osk = kb.dram("osk", [NL, NSQ, 128, 128], kind=O)
    osv = kb.dram("osv", [NL, NSQ, 128, 128], kind=O)
    osre = kb.dram("osre", [NL, NSQ, 1024], kind=O)
    osim = kb.dram("osim", [NL, NSQ, 1024], kind=O)
    oslh = kb.dram("oslh", [NL, NSQ, 256], kind=O)
    oslc = kb.dram("oslc", [NL, NSQ * 3, 256], kind=O)
    osfc = kb.dram("osfc", [NL, NSQ * 2, 5632], kind=O)
    I = "Internal"
    win_s = kb.dram("win_s", [NL, 12, 128, 8, 128], BF16, kind=I, split=NL)
    wo_s = kb.dram("wo_s", [NL, 8, 128, 8, 128], BF16, kind=I, split=NL)
    wup_s = kb.dram("wup_s", [NL, NJ, 128, 8, 256], BF16, kind=I, split=NL)
    wdn_s = kb.dram("wdn_s", [NL, 8, 128, NJ, 128], BF16, kind=I, split=NL)
    sw_s = kb.dram("sw_s", [NL, 128, NSW], BF16, kind=I, split=NL)
    tabp = kb.dram("tabp", [NL, 8, 128, 4, TP], F32, kind=I, split=NL)
    tabs = kb.dram("tabs", [NL, 8, 128, 4, TS], F32, kind=I, split=NL)

    kb.ps = [T(kb.st.enter_context(nc.psum_tensor("ps%d" % i, [128, 512], F32)), "ps%d" % i) for i in range(8)]
    kb.CST = kb.sb("CST", [128, NCST])
    kb.PP = kb.sb("PP", [128, NL * NPP])
    kb.EPSC = kb.sb("EPSC", [128, 4])
    kb.ZERO = kb.sb("ZERO", [128, TP])
    ONEB = kb.sb("ONEB", [128, 128], BF16)
    BLKB = kb.sb("BLKB", [128, 128], BF16)
    MASK4 = kb.sb("MASK4", [128, 512], BF16)
    MASKC = kb.sb("MASKC", [128, 512], BF16)
    MASKN = kb.sb("MASKN", [64, 512], BF16)
    X = kb.sb("X", [128, 8, TP], split=8)
    H = kb.sb("H", [128, 8, TP], BF16, split=8)
    Ob = H
    PROJ = kb.sb("PROJ", [128, 11, TP], split=11)

    class GAlias:
        def __getitem__(self, idx):
            p_, j_, c_ = idx
            v = PROJ[:, j_ // 2, :]
            ap = v.ap.bitcast(BF16)[p_, (j_ % 2) * TP:(j_ % 2 + 1) * TP][:, c_]
            return V(ap, v.bufs)

    class Sub:
        def __init__(self, off):
            self.off = off

        def __getitem__(self, idx):
            p_, c_, f_ = idx
            return PROJ[p_, c_ + self.off, f_]

    G = GAlias()
    MODA1 = kb.sb("MODA1", [128, NL, 8, 17])
    MODB1 = kb.sb("MODB1", [128, NL, 8, 17])
    MODG1 = kb.sb("MODG1", [128, NL, 8, 17])
    MODA2 = kb.sb("MODA2", [128, NL, 8, 17])
    MODB2 = kb.sb("MODB2", [128, NL, 8, 17])
    MODG2 = kb.sb("MODG2", [128, NL, 8, 17])
    RHO = kb.sb("RHO", [128, NL, 8])
    NSP8 = kb.sb("NSP8", [128, NL, 2])
    NSP16 = kb.sb("NSP16", [128, NL, 2])
    SINKE = kb.sb("SINKE", [128, NL, 4])
    kb.rots("RS", 2, [128, TP])
    kb.rots("SQ", 3, [128, TP], BF16)
    kb.rots("W", 7, [128, TP])
    kb.rots("STG", 5, [128, 128])
    kb.rots("XT", 1, [128, D])
    kb.rots("E", 2, [128, 512], BF16)
    kb.rots("UE", 2, [128, TP + 2 * NSQ])
    QR = kb.sb("QR", [128, 4, TP], BF16, split=4)
    KRA = kb.sb("KRA", [128, TP], BF16)
    KRB = kb.sb("KRB", [128, TP], BF16)
    KF = kb.sb("KF", [128, TP])
    VT = kb.sb("VT", [128, 4, 128], BF16)
    VF = kb.sb("VF", [128, 128])
    CS = kb.sb("CS", [128, TP])
    SN = kb.sb("SN", [128, TP])
    OA = Sub(0)
    DEN = kb.sb("DEN", [128, 512])
    UB = kb.sb("UB", [128, 2, TP], BF16)
    kb.rots("HR", 2, [128, TP], BF16)
    kb.rots("HI", 2, [128, TP], BF16)
    GLB = kb.sb("GLB", [128, 2, TP], BF16)
    GLF = kb.sb("GLF", [128, 2, TP])
    OS = Sub(5)
    XC = Sub(7)
    XCB = kb.sb("XCB", [128, 2, TP], BF16)
    XE = kb.sb("XE", [128, TP + 3 * NSQ])
    OL = Sub(9)
    stP = []
    for l in range(NL):
        stP.append(dict(kprevA=kb.sb("kprevA%d" % l, [128, 128], BF16), kprevB=kb.sb("kprevB%d" % l, [128, 128], BF16), vprev=kb.sb("vprev%d" % l, [128, 128], BF16),
                        sr=kb.sb("sr%d" % l, [128, 8, 1]), si=kb.sb("si%d" % l, [128, 8, 1]),
                        lh=kb.sb("lh%d" % l, [128, 2, 1]), lc=kb.sb("lc%d" % l, [128, 2, 3]),
                        fc=kb.sb("fc%d" % l, [128, 44, 2])))
    stS = dict(sr=kb.sb("srS", [128, 8, NSQ]), si=kb.sb("siS", [128, 8, NSQ]), lh=kb.sb("lhS", [128, 2, NSQ]),
               lc=kb.sb("lcS", [128, 2, NSQ * 3]), fc=kb.sb("fcS", [128, 44, NSQ * 2]))
    class XAlias:
        def __init__(self, kc0):
            self.kc0 = kc0

        def __getitem__(self, idx):
            p_, s_, f_ = idx
            kc = self.kc0 + s_ // 4
            v = X[:, kc, 64 + (s_ % 4) * 64: 64 + (s_ % 4 + 1) * 64]
            return V(v.ap.bitcast(BF16)[p_, f_], v.bufs)

    KCT = XAlias(0)
    VCT = XAlias(4)

    ident = kb.cst("ident")
    ones = kb.cst("ones")
    blk = kb.cst("blk")
    rotm = kb.cst("rot")

    kb.dma(kb.CST[:], cstd[:, :])
    kb.dma(kb.PP[:], ppd[:, :])
    kb.memset("dve", kb.EPSC[:, 0:1], EPS)
    kb.memset("dve", kb.EPSC[:, 1:2], math.pi / 2)
    kb.memset("dve", kb.EPSC[:, 2:3], 1.0)
    kb.memset("dve", kb.EPSC[:, 3:4], 0.0)
    kb.memset("pool", kb.ZERO[:], 0.0)
    kb.memset("pool", ONEB[:], 1.0)
    kb.cp("dve", BLKB[:], kb.cst("blk"))
    kb.memset("pool", KRA[:], 0.0)
    kb.memset("pool", KRB[:], 0.0)
    kb.dma(MASK4[:], mskd[:, 0:512], eng="pool")
    kb.dma(MASKC[:], mskd[:, 512:1024], eng="pool")
    kb.dma(MASKN[:], mskd[0:64, 1024:1536], eng="pool")
    for l in range(NL):
        for nm in ("sr", "si", "lh", "lc", "fc", "kprevA", "kprevB"):
            kb.memset("pool", stP[l][nm][:], 0.0)

    def emit_casts(l, part=None):
        def P(n):
            return part is None or part == n
        if P(0):
            for c in range(4):
                for two in range(2):
                    kb.dma(win_s[l, c, :, :, two * 64: two * 64 + 64],
                           win[l, :, two * 256 + c * 64: two * 256 + c * 64 + 64].re("(kc p) e -> p kc e", p=128), eng="pool")
            for ci in range(4, 12):
                kb.dma(win_s[l, ci, :, :, :], win[l, :, 512 + (ci - 4) * 128: 512 + (ci - 3) * 128].re("(kc p) e -> p kc e", p=128), eng="pool")
            kb.dma(sw_s[l, :, :], swd[l, :, :], eng="pool", max_dma_last_dim=4096)
        if P(1):
            for kc in range(4):
                for two in range(2):
                    kb.dma(wo_s[l, :, two * 64:(two + 1) * 64, kc, :].re("m p e -> p m e"),
                           wo[l, two * 256 + kc * 64: two * 256 + kc * 64 + 64, :].re("p (m e) -> p m e", e=128), eng="pool")
            for kc in range(4, 8):
                kb.dma(wo_s[l, :, :, kc, :].re("m p e -> p m e"),
                       wo[l, kc * 128:(kc + 1) * 128, :].re("p (m e) -> p m e", e=128), eng="pool")
            for kc in range(0, 3):
                rows = slice(kc * 128, (kc + 1) * 128)
                for two in range(2):
                    kb.dma(wup_s[l, :, :, kc, two * 128:(two + 1) * 128].re("j p e -> p j e"),
                           wup[l, rows, two * DFF:(two + 1) * DFF].re("p (j e) -> p j e", e=128), eng="pool")
        if P(2):
            for kc in range(3, 8):
                rows = slice(kc * 128, (kc + 1) * 128)
                for two in range(2):
                    kb.dma(wup_s[l, :, :, kc, two * 128:(two + 1) * 128].re("j p e -> p j e"),
                           wup[l, rows, two * DFF:(two + 1) * DFF].re("p (j e) -> p j e", e=128), eng="pool")
        if P(3):
            for j in range(NJ):
                kb.dma(wdn_s[l, :, :, j, :].re("m p e -> p m e"),
                       wdn[l, j * 128:(j + 1) * 128, :].re("p (m e) -> p m e", e=128), eng="pool")

    emit_casts(0)

    PW = kb.sb("PW", [128, 16, 8])
    PWL = kb.sb("PWL", [128, NL, 4, 8])
    TABT = kb.sb("TABT", [128, 4, TP])
    TB = TABT
    TBS = kb.sb("TBS", [128, 4, TS])
    for l in range(NL):
        kb.act(SINKE[:, l, :], kb.pcol("sk", l, 0, 4), AF.Exp)
        ex = PW[:, 15, 0:2]
        kb.act(ex, kb.pcol("lam", l, 0, 2), AF.Exp, scale=-1.0)
        kb.act(ex, ex, AF.Ln, bias=kb.EPSC[:, 2:3])
        kb.ts("dve", NSP8[:, l, :], ex, -8.0, ALU.mult)
        kb.ts("dve", NSP16[:, l, :], ex, -16.0, ALU.mult)
        dt, th, c1, s1, t0, t1, nr, ni, dn, cr, ci = [PW[:, i, :] for i in range(11)]
        are = kb.pcol("are", l, 0, 8)
        aim = kb.pcol("aim", l, 0, 8)
        kb.act(dt, kb.pcol("ldt", l, 0, 8), AF.Exp)
        kb.tt("dve", t0, are, dt, ALU.mult)
        kb.act(RHO[:, l, :], t0, AF.Exp)
        kb.tt("dve", th, aim, dt, ALU.mult)
        kb.act(s1, th, AF.Sin, scale=1.0 / 16)
        kb.act(c1, th, AF.Sin, scale=1.0 / 16, bias=kb.EPSC[:, 1:2])
        for _ in range(4):
            kb.tt("dve", t0, c1, c1, ALU.mult)
            kb.tt("dve", t1, s1, s1, ALU.mult)
            kb.tt("dve", s1, s1, c1, ALU.mult)
            kb.ts("dve", s1, s1, 2.0, ALU.mult)
            kb.tt("dve", c1, t0, t1, ALU.subtract)
        kb.tt("dve", nr, RHO[:, l, :], c1, ALU.mult)
        kb.ts("dve", nr, nr, -1.0, ALU.add)
        kb.tt("dve", ni, RHO[:, l, :], s1, ALU.mult)
        kb.tt("dve", t0, are, are, ALU.mult)
        kb.tt("dve", t1, aim, aim, ALU.mult)
        kb.tt("dve", dn, t0, t1, ALU.add)
        kb.recip(dn, dn)
        kb.tt("dve", t0, nr, are, ALU.mult)
        kb.tt("dve", t1, ni, aim, ALU.mult)
        kb.tt("dve", cr, t0, t1, ALU.add)
        kb.tt("dve", cr, cr, dn, ALU.mult)
        kb.tt("dve", t0, ni, are, ALU.mult)
        kb.tt("dve", t1, nr, aim, ALU.mult)
        kb.tt("dve", ci, t0, t1, ALU.subtract)
        kb.tt("dve", ci, ci, dn, ALU.mult)
        for i_, src_ in enumerate((c1, s1, cr, ci)):
            kb.cp("dve", PWL[:, l, i_, :], src_)
    C17 = kb.sb("C17", [128, 8, 17])
    for s in range(17):
        kb.dma(C17[:, :, s], c17[s, :].re("(kc p) -> p kc", p=128), nc_ok=True)
    SC17 = kb.sb("SC17", [128, 8, 17])
    kb.act(SC17[:], C17[:], AF.Silu)
    wav = PROJ.t[:, 0:8, :].rearrange("p a (b n) -> p (a b) n", b=2)
    kb.rot["WA"] = ([T(wav[:, 0:8, :], "WA0"), T(wav[:, 8:16, :], "WA1")], 0)
    MODT = kb.sb("MODT", [128, 48, 17])
    def tab_gen():
        for l in range(NL):
            c1, s1, cr, ci = [PWL[:, l, i_, :] for i_ in range(4)]
            for j in range(8):
                cosk = TB[:, 2, :]
                sink = TB[:, 3, :]
                kb.cp("dve", TB[:, 2, 0:1], c1[:, j:j + 1])
                kb.cp("dve", TB[:, 3, 0:1], s1[:, j:j + 1])
                n = 1
                while n < TP:
                    cn = TB[:, 2, n - 1:n]
                    sn = TB[:, 3, n - 1:n]
                    w0 = kb.nxt("W")
                    w1 = kb.nxt("W")
                    kb.ts("dve", w0[:, 0:n], TB[:, 3, 0:n], sn, ALU.mult)
                    kb.ts("dve", w1[:, 0:n], TB[:, 2, 0:n], sn, ALU.mult)
                    kb.stt(TB[:, 2, n:2 * n], TB[:, 2, 0:n], cn, w0[:, 0:n], ALU.mult, ALU.subtract)
                    kb.stt(TB[:, 3, n:2 * n], TB[:, 3, 0:n], cn, w1[:, 0:n], ALU.mult, ALU.add)
                    n *= 2
                w0 = kb.nxt("W")
                kb.ts("dve", w0[:], sink, ci[:, j:j + 1], ALU.mult)
                kb.stt(TB[:, 0, :], cosk, cr[:, j:j + 1], w0[:], ALU.mult, ALU.add)
                w1 = kb.nxt("W")
                kb.ts("dve", w1[:], sink, cr[:, j:j + 1], ALU.mult)
                kb.stt(TB[:, 1, :], cosk, ci[:, j:j + 1], w1[:], ALU.mult, ALU.subtract)
                kb.dma(tabp[l, j, :, :, :], TB[:])
                for s in range(NSQ):
                    kb.cp("dve", TBS[:, :, s * 4:(s + 1) * 4], TB[:, :, 0:4])
                kb.dma(tabs[l, j, :, :, :], TBS[:])
                yield

    def ada_gen():
        for l in range(NL):
            for m2 in range(24):
                wa = kb.nxt("WA")
                kb.dma(wa[:], wada[l, :, m2 * 256:(m2 + 1) * 256].re("(kc p) n -> p kc n", p=128))
                for mi in range(2):
                    m = m2 * 2 + mi
                    b = kb.bank()
                    for kc in range(8):
                        kb.mm(b[:, 0:17], wa[:, kc, mi * 128:(mi + 1) * 128], SC17[:, kc, :], start=kc == 0, stop=kc == 7)
                    kb.act(MODT[:, m, :], b[:, 0:17], AF.Identity, bias=kb.pcol("bada", l, m))
                    yield
            for kc in range(8):
                kb.cp("act", MODB1[:, l, kc, :], MODT[:, 0 + kc, :])
                kb.ts("dve", MODA1[:, l, kc, :], MODT[:, 8 + kc, :], 1.0, ALU.add, kb.pcol("n1", l, kc), ALU.mult)
                kb.cp("act", MODG1[:, l, kc, :], MODT[:, 16 + kc, :])
                kb.cp("act", MODB2[:, l, kc, :], MODT[:, 24 + kc, :])
                kb.ts("dve", MODA2[:, l, kc, :], MODT[:, 32 + kc, :], 1.0, ALU.add, kb.pcol("n2", l, kc), ALU.mult)
                kb.cp("act", MODG2[:, l, kc, :], MODT[:, 40 + kc, :])


    def _nx(g):
        try:
            next(g)
            return True
        except StopIteration:
            return False

    gt_, ga_ = tab_gen(), ada_gen()
    at_ = aa_ = True
    while at_ or aa_:
        for _ in range(6):
            if aa_:
                aa_ = _nx(ga_)
        if at_:
            at_ = _nx(gt_)

    wa_l = kb.rot["WA"][0]
    import os
    STAGE = int(os.environ.get("KSTAGE", "99"))
    segs = [("s", 0)] + [("p", k) for k in range(NSEG)]
    segs = segs[:STAGE]
    if STAGE == 0:
        kb.S.emit()
        return
    it_win, it_wo, it_sw, it_up, it_dn, it_tab = [], [], [], [], [], []
    for kind, k in segs:
        for l in range(NL):
            for ci in [0, 1, 2, 3, 4, 6, 7, 8, 9, 10, 11, 5]:
                it_win.append(win_s[l, ci, :, :, :])
            for mo in range(8):
                it_wo.append(wo_s[l, mo, :, :, :])
            it_sw.append(sw_s[l, :, :])
            for j in range(NJ):
                for half in range(2):
                    it_up.append(wup_s[l, j, :, :, half * 128:(half + 1) * 128])
            for m in range(8):
                for hj in range(2):
                    it_dn.append(wdn_s[l, m, :, hj * 11:(hj + 1) * 11, :])
            for j in range(8):
                it_tab.append(tabp[l, j, :, :, :] if kind == "p" else tabs[l, j, :, :, :])
    sWIN = Stream(kb, "WIN", 4, [128, 8, 128], BF16, it_win)
    sWO = Stream(kb, "WO", 3, [128, 8, 128], BF16, it_wo)
    sSW = Stream(kb, "SWB", 1, [128, NSW], BF16, it_sw)
    sUP = Stream(kb, "WUP", 6, [128, 8, 128], BF16, it_up)
    sDN = Stream(kb, "WDN", 3, [128, 11, 128], BF16, it_dn)
    tab_tiles = [TABT]
    tab_i = [0]

    def tab_load(idx):
        if idx < len(it_tab):
            src = it_tab[idx]
            tn = TS if idx < NL * 8 else TP
            kb.dma(tab_tiles[0][:, :, 0:tn], src)

    kb.memset("dve", V(PROJ.t[:, 0:8, 0:1], PROJ.bufs[0:8] + wa_l[0].bufs + wa_l[1].bufs), 0.0)
    for s_ in (sWIN, sWO, sSW, sUP, sDN):
        s_.start()
    tab_load(0)

    SUB = float(os.environ.get("KSUB", "99"))

    class _Stop(Exception):
        pass

    def chk(n):
        if n > SUB:
            raise _Stop()

    try:
        _main(kb, locals())
    except _Stop:
        pass
    kb.S.emit()


def _main(kb, env):
    globals().update({k_: v_ for k_, v_ in env.items() if k_ not in ("kb",)})
    for kind, k in segs:
        Tn = TS if kind == "s" else TP
        nseq = NSQ if kind == "s" else 1
        L = Tn // nseq
        ntb = (Tn + 127) // 128
        tbs = min(128, Tn)

        def v3(v):
            return v.re("p (s l) -> p s l", l=L)

        def sidx(s):
            return 0 if kind == "p" else 1 + s

        PL = "dve" if kind == "s" else "pool"

        src = xs if kind == "s" else xp
        r0 = 0 if kind == "s" else k * TP
        for tb in range(ntb):
            xt = kb.nxt("XT")
            kb.dma(xt[:tbs, :], src[r0 + tb * 128: r0 + tb * 128 + tbs, :])
            for half in range(2):
                b = kb.bank()
                for q4 in range(4):
                    kc = half * 4 + q4
                    kb.tr(b[:, q4 * 128: q4 * 128 + tbs], xt[:tbs, kc * 128:(kc + 1) * 128],
                          kb.cst("ident", tbs, slice(0, tbs)))
                kb.cp("act", X[:, half * 4:half * 4 + 4, tb * 128: tb * 128 + tbs],
                      b[:, :].re("p (q t) -> p q t", t=128)[:, :, 0:tbs])
        if kind == "s":
            kb.cp(PL, CS[:, :Tn], kb.cst("coss", 64))
            kb.cp(PL, SN[:, :Tn], kb.cst("sins", 64))
        else:
            kb.dma(CS[:], cosP[:, k * TP:(k + 1) * TP])
            kb.dma(SN[:], sinP[:, k * TP:(k + 1) * TP])

        for l in range(NL):
            S_ = stS if kind == "s" else stP[l]
            last = (kind == "s") or (k == NSEG - 1)

            def modulate(dst_bf, MA, MB, nrm_name):
                b = kb.bank()
                for kc in range(8):
                    sq = kb.nxt("SQ")
                    kb.act(sq[:, :Tn], X[:, kc, :Tn], AF.Square)
                    kb.mm(b[:, :Tn], ONEB[:, :], sq[:, :Tn], start=kc == 0, stop=kc == 7)
                rs = kb.rstd_from(b, Tn, 1.0 / D)
                for kc in range(8):
                    w = kb.nxt("W")
                    kb.tt("dve", w[:, :Tn], X[:, kc, :Tn], rs[:, :Tn], ALU.mult)
                    for s in range(nseq):
                        cs_ = slice(s * L, (s + 1) * L)
                        kb.act(dst_bf[:, kc, cs_], w[:, cs_], AF.Identity,
                               scale=MA[:, l, kc, sidx(s):sidx(s) + 1], bias=MB[:, l, kc, sidx(s):sidx(s) + 1])

            def resid(bank_, mo, MG):
                for s in range(nseq):
                    cs_ = slice(s * L, (s + 1) * L)
                    kb.stt(X[:, mo, cs_], bank_[:, cs_], MG[:, l, mo, sidx(s):sidx(s) + 1], X[:, mo, cs_],
                           ALU.mult, ALU.add)

            def outnorm(src_chunks, inv_n, o0):
                b = kb.bank()
                n = len(src_chunks)
                for i, sc in enumerate(src_chunks):
                    sq = kb.nxt("SQ")
                    kb.act(sq[:, :Tn], sc, AF.Square)
                    kb.mm(b[:, :Tn], ONEB[:, :], sq[:, :Tn], start=i == 0, stop=i == n - 1)
                rs = kb.rstd_from(b, Tn, inv_n)
                for i, sc in enumerate(src_chunks):
                    kb.stt(Ob[:, o0 + i, :Tn], sc, kb.pcol("on", l, o0 + i), rs[:, :Tn], ALU.mult, ALU.mult)

            if kind == "s":
                if l + 1 < NL:
                    emit_casts(l + 1, 0)
                for s in range(NSQ):
                    stg = kb.nxt("STG")
                    kb.dma(stg[:], ck[l, s, :, :])
                    b = kb.bank()
                    kb.tr(b[:, 0:128], stg[:], ident)
                    kb.cp("act", KCT[:, s, :], b[:, 0:128])
                    stg2 = kb.nxt("STG")
                    kb.dma(stg2[:], cv[l, s, :, :])
                    kb.cp(PL, VCT[:, s, :], stg2[:])
                    kb.dma(osk[l, s, 0:124, :], ck[l, s, 4:128, :], eng="pool")
                    kb.dma(osv[l, s, 0:124, :], cv[l, s, 4:128, :], eng="pool")
                for j in range(8):
                    kb.load_T(S_["sr"][:, j, :], sre[l, :, j * 128:(j + 1) * 128], NSQ)
                    kb.load_T(S_["si"][:, j, :], sim[l, :, j * 128:(j + 1) * 128], NSQ)
                for m in range(2):
                    kb.load_T(S_["lh"][:, m, :], slh[l, :, m * 128:(m + 1) * 128], NSQ)
                    kb.load_T(S_["lc"][:, m, :], slc[l, :, m * 128:(m + 1) * 128], NSQ * 3)
                for ch in range(44):
                    kb.load_T(S_["fc"][:, ch, :], sfc[l, :, ch * 128:(ch + 1) * 128], NSQ * 2)

            chk(1)
            modulate(H, MODA1, MODB1, "n1")
            chk(2)

            for i in range(11):
                WIN = sWIN.get()
                b = kb.bank()
                for kc in range(8):
                    kb.mm(b[:, :Tn], WIN[:, kc, :], H[:, kc, :Tn], start=kc == 0, stop=kc == 7)
                kb.cp("act", PROJ[:, i, :Tn], b[:, :Tn])
                sWIN.done()
            WIN = sWIN.get()
            for tb in range(ntb):
                b = kb.bank()
                for kc in range(8):
                    kb.mm(b[:tbs, 0:128], H[:, kc, tb * 128: tb * 128 + tbs], WIN[:, kc, :],
                          start=kc == 0, stop=kc == 7)
                kb.cp("act", VT[:tbs, tb, :], b[:tbs, 0:128])
                if tb == ntb - 1:
                    kb.cp("act", VF[:tbs, :], b[:tbs, 0:128])
            sWIN.done()

            chk(3)
            def rope(srcv, wname, out_bf, out_f):
                sq = kb.nxt("SQ")
                kb.act(sq[:, :Tn], srcv, AF.Square)
                b = kb.bank()
                kb.mm(b[:, :Tn], BLKB[:, :], sq[:, :Tn])
                yield
                rs = kb.rstd_from(b, Tn, 1.0 / 64)
                yield
                qn = kb.nxt("W")
                kb.stt(qn[:, :Tn], srcv, kb.pcol(wname, l), rs[:, :Tn], ALU.mult, ALU.mult)
                b2 = kb.bank()
                kb.mm(b2[:, :Tn], rotm, qn[:, :Tn])
                yield
                t1 = kb.nxt("W")
                t2 = kb.nxt("W")
                kb.tt("dve", t1[:, :Tn], b2[:, :Tn], SN[:, :Tn], ALU.mult)
                kb.tt(PL, t2[:, :Tn], qn[:, :Tn], CS[:, :Tn], ALU.mult)
                yield
                if isinstance(out_bf, list):
                    for R_, o_ in out_bf:
                        kb.tt("dve", o_, t1[R_, :Tn], t2[R_, :Tn], ALU.add)
                else:
                    kb.tt("dve", out_bf, t1[:, :Tn], t2[:, :Tn], ALU.add)
                if out_f is not None:
                    kb.tt(PL, out_f, t1[:, :Tn], t2[:, :Tn], ALU.add)

            def lockstep(gens):
                gens = list(gens)
                while gens:
                    for g in list(gens):
                        try:
                            next(g)
                        except StopIteration:
                            gens.remove(g)

            lockstep([rope(PROJ[:, 0, :Tn], "qn", QR[:, 0, :Tn], None), rope(PROJ[:, 1, :Tn], "qn", QR[:, 1, :Tn], None)])
            lockstep([rope(PROJ[:, 2, :Tn], "qn", QR[:, 2, :Tn], None), rope(PROJ[:, 3, :Tn], "qn", QR[:, 3, :Tn], None)])
            lockstep([rope(PROJ[:, 4, :Tn], "kn", [(slice(0, 64), KRA[0:64, :Tn]), (slice(64, 128), KRB[64:128, :Tn])], KF[:, :Tn])])

            chk(4)
            if kind == "s" and l + 1 < NL:
                emit_casts(l + 1, 1)
            def ssm_tail():
                for m in range(2):
                    b = kb.ps[6 + m]
                    y = kb.nxt("W")
                    kb.stt(y[:, :Tn], PROJ[:, 5 + m, :Tn], kb.pcol("sd", l, m), b[:, :Tn], ALU.mult, ALU.add)
                    kb.act(GLF[:, m, :Tn], y[:, :Tn], AF.Gelu_apprx_tanh)
                    kb.cp("act", GLB[:, m, :Tn], GLF[:, m, :Tn])
                for mo in range(2):
                    b = kb.bank()
                    for kc in range(2):
                        o_ = SW_OFF["glu"] + kc * 256 + mo * 128
                        kb.mm(b[:, :Tn], SWB[:, o_:o_ + 128], GLB[:, kc, :Tn], start=kc == 0, stop=kc == 1)
                    sg = kb.nxt("W")
                    kb.act(sg[:, :Tn], b[:, :Tn], AF.Sigmoid, bias=kb.pcol("bglu", l, mo))
                    kb.tt("dve", OS[:, mo, :Tn], GLF[:, mo, :Tn], sg[:, :Tn], ALU.mult)
                outnorm([OS[:, m, :Tn] for m in range(2)], 1.0 / 256, 4)

            def ssm_items():
                for j in range(8):
                    m = j // 4
                    br = kb.bank()
                    bi = kb.bank()
                    kb.mm(br[:, :Tn], SWB[:, SW_OFF["bre"] + j * 128: SW_OFF["bre"] + (j + 1) * 128], UB[:, m, :Tn])
                    kb.mm(bi[:, :Tn], SWB[:, SW_OFF["bim"] + j * 128: SW_OFF["bim"] + (j + 1) * 128], UB[:, m, :Tn])
                    TABt = tab_tiles[0]
                    er, ei, ckk, skk = [TABt[:, i, :Tn] for i in range(4)]
                    a1 = kb.nxt("W"); a2 = kb.nxt("W")
                    kb.tt("dve", a1[:, :Tn], br[:, :Tn], er, ALU.mult)
                    kb.tt("dve", a2[:, :Tn], bi[:, :Tn], ei, ALU.mult)
                    gr = kb.nxt("W")
                    kb.tt("dve", gr[:, :Tn], a1[:, :Tn], a2[:, :Tn], ALU.subtract)
                    a3 = kb.nxt("W"); a4 = kb.nxt("W")
                    kb.tt("dve", a3[:, :Tn], br[:, :Tn], ei, ALU.mult)
                    kb.tt("dve", a4[:, :Tn], bi[:, :Tn], er, ALU.mult)
                    gi = kb.nxt("W")
                    kb.tt("dve", gi[:, :Tn], a3[:, :Tn], a4[:, :Tn], ALU.add)
                    rho = RHO[:, l, j:j + 1]
                    kb.stt(v3(gr[:, :Tn])[:, :, 0], S_["sr"][:, j, :], rho, v3(gr[:, :Tn])[:, :, 0], ALU.mult, ALU.add)
                    kb.stt(v3(gi[:, :Tn])[:, :, 0], S_["si"][:, j, :], rho, v3(gi[:, :Tn])[:, :, 0], ALU.mult, ALU.add)
                    dec = kb.nxt("W")
                    kb.act(dec[:, :Tn], kb.ZERO[:, :Tn], AF.Identity, bias=rho)
                    kb.memset("dve", v3(dec[:, :Tn])[:, :, 0], 0.0)
                    g2r = a1
                    g2i = a2
                    kb.scan(g2r[:, :Tn], dec[:, :Tn], gr[:, :Tn])
                    kb.scan(g2i[:, :Tn], dec[:, :Tn], gi[:, :Tn])
                    b1 = a3; b2_ = a4; b3 = gr; b4 = gi
                    kb.tt("dve", b1[:, :Tn], g2r[:, :Tn], ckk, ALU.mult)
                    kb.tt("dve", b2_[:, :Tn], g2i[:, :Tn], skk, ALU.mult)
                    hr_ = kb.nxt("HR")
                    hi_ = kb.nxt("HI")
                    kb.tt("dve", hr_[:, :Tn], b1[:, :Tn], b2_[:, :Tn], ALU.subtract)
                    kb.tt(PL, S_["sr"][:, j, :], v3(b1[:, :Tn])[:, :, L - 1], v3(b2_[:, :Tn])[:, :, L - 1], ALU.subtract)
                    kb.tt("dve", b3[:, :Tn], g2i[:, :Tn], ckk, ALU.mult)
                    kb.tt("dve", b4[:, :Tn], g2r[:, :Tn], skk, ALU.mult)
                    kb.stt(hi_[:, :Tn], b3[:, :Tn], -1.0, b4[:, :Tn], ALU.mult, ALU.subtract)
                    yb = kb.ps[6 + m]
                    jj = j % 4
                    kb.mm(yb[:, :Tn], SWB[:, SW_OFF["cre"] + j * 128: SW_OFF["cre"] + (j + 1) * 128], hr_[:, :Tn],
                          start=jj == 0, stop=False)
                    kb.mm(yb[:, :Tn], SWB[:, SW_OFF["cim"] + j * 128: SW_OFF["cim"] + (j + 1) * 128], hi_[:, :Tn],
                          start=False, stop=jj == 3)
                    kb.tt(PL, S_["si"][:, j, :], v3(b3[:, :Tn])[:, :, L - 1], v3(b4[:, :Tn])[:, :, L - 1], ALU.add)
                    tab_load(tab_i[0] + 1)
                    tab_i[0] += 1
                    if last:
                        dre = (osre if kind == "s" else pre)
                        dim_ = (osim if kind == "s" else pim)
                        kb.store_T(dre[l, :, j * 128:(j + 1) * 128], S_["sr"][:, j, :], nseq)
                        kb.store_T(dim_[l, :, j * 128:(j + 1) * 128], S_["si"][:, j, :], nseq)
                    yield
                ssm_tail()
                yield

            def lru_items():
                for m in range(2):
                    xe = XE[:, 0:nseq * (3 + L)].re("p (s l) -> p s l", l=3 + L)
                    kb.cp(PL, xe[:, :, 0:3], S_["lc"][:, m, :].re("p (s t) -> p s t", t=3))
                    kb.cp("act", xe[:, :, 3:3 + L], v3(PROJ[:, 7 + m, :Tn]))
                    xc3 = v3(XC[:, m, :Tn])
                    kb.act(xc3, xe[:, :, 3:3 + L], AF.Identity, scale=kb.pcol("lcw", l, m * 4 + 3), bias=kb.pcol("lcb", l, m))
                    for tap in (2, 1, 0):
                        kb.stt(xc3, xe[:, :, tap:tap + L], kb.pcol("lcw", l, m * 4 + tap), xc3, ALU.mult, ALU.add)
                    kb.cp(PL, S_["lc"][:, m, :].re("p (s t) -> p s t", t=3), xe[:, :, L:L + 3])
                    kb.cp(PL, XCB[:, m, :Tn], XC[:, m, :Tn])
                    if last:
                        dlc = oslc if kind == "s" else plc
                        kb.store_T(dlc[l, :, m * 128:(m + 1) * 128], S_["lc"][:, m, :], nseq * 3)
                    yield
                for m in range(2):
                    b = kb.bank()
                    kb.mm(b[:, :Tn], SWB[:, SW_OFF["wa"] + m * 128: SW_OFF["wa"] + (m + 1) * 128], XCB[:, m, :Tn])
                    r_ = kb.nxt("W")
                    kb.act(r_[:, :Tn], b[:, :Tn], AF.Sigmoid, bias=kb.pcol("ba", l, m))
                    b = kb.bank()
                    kb.mm(b[:, :Tn], SWB[:, SW_OFF["wi"] + m * 128: SW_OFF["wi"] + (m + 1) * 128], XCB[:, m, :Tn])
                    gi_ = kb.nxt("W")
                    kb.act(gi_[:, :Tn], b[:, :Tn], AF.Sigmoid, bias=kb.pcol("bi", l, m))
                    a_ = kb.nxt("W")
                    kb.act(a_[:, :Tn], r_[:, :Tn], AF.Exp, scale=NSP8[:, l, m:m + 1])
                    a2_ = kb.nxt("W")
                    kb.act(a2_[:, :Tn], r_[:, :Tn], AF.Exp, scale=NSP16[:, l, m:m + 1])
                    kb.ts("dve", a2_[:, :Tn], a2_[:, :Tn], 1.0, ALU.min)
                    mu = r_
                    kb.act(mu[:, :Tn], a2_[:, :Tn], AF.Sqrt, scale=-1.0, bias=kb.EPSC[:, 2:3])
                    bb = a2_
                    kb.tt("dve", bb[:, :Tn], mu[:, :Tn], gi_[:, :Tn], ALU.mult)
                    kb.tt("dve", bb[:, :Tn], bb[:, :Tn], XC[:, m, :Tn], ALU.mult)
                    if kind == "p" and k == 0:
                        kb.tt("dve", bb[:, 0:1], gi_[:, 0:1], XC[:, m, 0:1], ALU.mult)
                    tmp = kb.nxt("W")
                    kb.tt("dve", tmp[:, 0:nseq], v3(a_[:, :Tn])[:, :, 0], S_["lh"][:, m, :], ALU.mult)
                    kb.tt("dve", v3(bb[:, :Tn])[:, :, 0], v3(bb[:, :Tn])[:, :, 0], tmp[:, 0:nseq], ALU.add)
                    kb.memset(PL, v3(a_[:, :Tn])[:, :, 0], 0.0)
                    hh_ = gi_
                    kb.scan(hh_[:, :Tn], a_[:, :Tn], bb[:, :Tn])
                    kb.cp(PL, S_["lh"][:, m, :], v3(hh_[:, :Tn])[:, :, L - 1])
                    gy = kb.nxt("W")
                    kb.act(gy[:, :Tn], PROJ[:, 9 + m, :Tn], AF.Gelu_apprx_tanh)
                    kb.tt("dve", OL[:, m, :Tn], hh_[:, :Tn], gy[:, :Tn], ALU.mult)
                    if last:
                        dlh = oslh if kind == "s" else plh
                        kb.store_T(dlh[l, :, m * 128:(m + 1) * 128], S_["lh"][:, m, :], nseq)
                    yield
                outnorm([OL[:, m, :Tn] for m in range(2)], 1.0 / 256, 6)
                yield

            def step(gen, pool):
                kb.cur_pool = pool
                try:
                    next(gen)
                    alive = True
                except StopIteration:
                    alive = False
                kb.cur_pool = None
                return alive

            SWB = sSW.get()
            kb.cp(PL, UB[:, :, :Tn], PROJ[:, 5:7, :Tn])
            if kind == "p":
                def scores(qb, c):
                    first = (k == 0 and qb == 0)
                    qs = slice(qb * 128, (qb + 1) * 128)
                    bsc_ = kb.bank()
                    whiches = (["prev"] if not first else []) + ["diag"]
                    col = 0
                    for wch in whiches:
                        for half in range(2):
                            KRh = KRA if half == 0 else KRB
                            if wch == "prev":
                                lhs = S_["kprevA" if half == 0 else "kprevB"][:, :] if qb == 0 else KRh[:, (qb - 1) * 128: qb * 128]
                            else:
                                lhs = KRh[:, qs]
                            kb.mm(bsc_[:, col:col + 128], lhs, QR[:, c, qs])
                            col += 128
                    return bsc_, whiches, col

                def rest1(qb, c, sc):
                    bsc_, whiches, col = sc
                    qs = slice(qb * 128, (qb + 1) * 128)
                    n = col
                    E = kb.nxt("E")
                    kb.act(E[:, :n], bsc_[:, :n], AF.Exp, scale=0.125)
                    kb.tt(PL, E[:, :n], E[:, :n], MASK4[:, 512 - n:512], ALU.mult)
                    b2 = kb.bank()
                    nk = n // 256
                    for i in range(nk):
                        if whiches[i] == "prev":
                            vt = S_["vprev"][:, :] if qb == 0 else VT[:, qb - 1, :]
                        else:
                            vt = VT[:, qb, :]
                        kb.mm(b2[:, 0:256], vt, E[:, i * 256:(i + 1) * 256], start=i == 0, stop=i == nk - 1)
                    for i in range(nk):
                        kb.mm(b2[:, 256:512], ONEB[:, :], E[:, i * 256:(i + 1) * 256], start=i == 0, stop=i == nk - 1)
                    return b2

                def rest2(qb, c, b2):
                    qs = slice(qb * 128, (qb + 1) * 128)
                    for half in range(2):
                        R = slice(half * 64, half * 64 + 64)
                        cc = half * 128
                        kb.act(DEN[R, 0:128], b2[R, 256 + cc:256 + cc + 128], AF.Ln, bias=SINKE[R, l, c:c + 1])
                    kb.act(DEN[:, 0:128], DEN[:, 0:128], AF.Exp, scale=-1.0)
                    for half in range(2):
                        R = slice(half * 64, half * 64 + 64)
                        cc = half * 128
                        kb.tt("dve", OA[R, c, qs], b2[R, cc:cc + 128], DEN[R, 0:128], ALU.mult)

                def attn_items():
                    work = [(qb, c) for qb in range(4) for c in range(4)]
                    pend = scores(*work[0])
                    for wi, (qb, c) in enumerate(work):
                        nxt_sc = scores(*work[wi + 1]) if wi + 1 < len(work) else None
                        b2_ = rest1(qb, c, pend)
                        pend = nxt_sc
                        yield
                        rest2(qb, c, b2_)
                        yield
                    kb.cp(PL, S_["kprevA"][:, :], KRA[:, 384:512])
                    kb.cp(PL, S_["kprevB"][:, :], KRB[:, 384:512])
                    kb.cp(PL, S_["vprev"][:, :], VT[:, 3, :])
                    if last:
                        kb.store_T(pk[l, :, :], KF[:, 384:512], 128)
                        kb.dma(pv[l, :, :], VF[:, :])
                    outnorm([OA[:, c, :Tn] for c in range(4)], 1.0 / 512, 0)
                    yield

                ga, gs, gl = attn_items(), ssm_items(), lru_items()
                for i_ in range(33):
                    step(ga, "attn")
                    if i_ % 4 == 0:
                        step(gs, "ssm")
                    elif i_ % 8 == 2:
                        step(gl, "lru")
                while step(ga, "attn"):
                    pass
                while step(gs, "ssm"):
                    pass
                while step(gl, "lru"):
                    pass
            else:
                bsc = (kb.bank(), kb.bank())
                for c in range(4):
                    for half in range(2):
                        R = slice(half * 64, half * 64 + 64)
                        hh = c * 2 + half
                        for s in range(NSQ):
                            kb.mm(bsc[half][:, hh * 64 + s * 4: hh * 64 + s * 4 + 4], KCT[R, s, :], QR[R, c, s * 4:(s + 1) * 4])
                Ec = kb.nxt("E")
                for half in range(2):
                    kb.act(Ec[:, :].re("p (c h x) -> p c h x", c=4, h=2)[:, :, half, :],
                           bsc[half][:, :].re("p (c h x) -> p c h x", c=4, h=2)[:, :, half, :], AF.Exp, scale=0.125)
                kb.tt("dve", Ec[:, :], Ec[:, :], MASKC[:, :], ALU.mult)
                chk(4.1)
                bsn = (kb.bank(), kb.bank())
                for c in range(4):
                    for half in range(2):
                        R = slice(half * 64, half * 64 + 64)
                        hh = c * 2 + half
                        kb.mm(bsn[half][0:64, hh * 64:(hh + 1) * 64], (KRA if half == 0 else KRB)[R, 0:64], QR[R, c, 0:64])
                En = kb.nxt("E")
                for half in range(2):
                    kb.act(En[0:64, :].re("p (c h x) -> p c h x", c=4, h=2)[:, :, half, :],
                           bsn[half][0:64, :].re("p (c h x) -> p c h x", c=4, h=2)[:, :, half, :], AF.Exp, scale=0.125)
                kb.tt("dve", En[0:64, :], En[0:64, :], MASKN[:, :], ALU.mult)
                chk(4.2)
                bnc = kb.bank()
                Ec4 = Ec[:, :].re("p (h s t) -> p h s t", h=8, s=NSQ)
                for s in range(NSQ):
                    kb.mm(bnc[:, s * 32:(s + 1) * 32].re("p (h t) -> p h t", h=8), VCT[:, s, :], Ec4[:, :, s, :])
                NUMS = kb.nxt("W")
                kb.cp("act", NUMS[:, :], bnc[:, :])
                chk(4.3)
                bnn = kb.bank()
                kb.mm(bnn[:, :], VT[0:64, 0, :], En[0:64, :])
                bdn = kb.bank()
                kb.mm(bdn[:, :], ONEB[:, :], Ec[:, :], start=True, stop=False)
                kb.mm(bdn[:, :], ONEB[0:64, :], En[0:64, :], start=False, stop=True)
                chk(4.4)
                for c in range(4):
                    for half in range(2):
                        R = slice(half * 64, half * 64 + 64)
                        hh = c * 2 + half
                        hc = slice(hh * 64, (hh + 1) * 64)
                        kb.ts("dve", DEN[R, hc], bdn[R, hc], SINKE[R, l, c:c + 1], ALU.add)
                        kb.recip(DEN[R, hc], DEN[R, hc])
                        w = kb.nxt("W")
                        kb.tt("dve", w[R, 0:64].re("p (s t) -> p s t", t=4),
                              NUMS[R, :].re("p (s h t) -> p s h t", s=NSQ, h=8)[:, :, hh, :],
                              bnn[R, hc].re("p (s t) -> p s t", t=4), ALU.add)
                        kb.tt("dve", OA[R, c, 0:64], w[R, 0:64], DEN[R, hc], ALU.mult)
                chk(4.5)
                b = kb.bank()
                kb.tr(b[0:64, 0:128], KF[:, 0:64], ident)
                stg = kb.nxt("STG")
                kb.cp("act", stg[0:64, :], b[0:64, 0:128])
                for s in range(NSQ):
                    kb.dma(osk[l, s, 124:128, :], stg[s * 4:(s + 1) * 4, :])
                    kb.dma(osv[l, s, 124:128, :], VF[s * 4:(s + 1) * 4, :])
                outnorm([OA[:, c, :Tn] for c in range(4)], 1.0 / 512, 0)
                for _ in ssm_items():
                    pass
                for _ in lru_items():
                    pass
            sSW.done()

            chk(8)
            if kind == "s" and l + 1 < NL:
                emit_casts(l + 1, 2)
            for mo in range(8):
                WO = sWO.get()
                b = kb.bank()
                for kc in range(8):
                    kb.mm(b[:, :Tn], WO[:, kc, :], Ob[:, kc, :Tn], start=kc == 0, stop=kc == 7)
                resid(b, mo, MODG1)
                sWO.done()

            modulate(H, MODA2, MODB2, "n2")

            chk(9)
            for j in range(NJ):
                cc_t = {}
                for half in range(2):
                    WUP = sUP.get()
                    ch = half * NJ + j
                    b = kb.bank()
                    for kc in range(8):
                        kb.mm(b[:, :Tn], WUP[:, kc, :], H[:, kc, :Tn], start=kc == 0, stop=kc == 7)
                    sUP.done()
                    ue = kb.nxt("UE")
                    ue3 = ue[:, 0:nseq * (2 + L)].re("p (s l) -> p s l", l=2 + L)
                    fcst = S_["fc"][:, ch, :].re("p (s t) -> p s t", t=2)
                    kb.cp(PL, ue3[:, :, 0:2], fcst)
                    kb.cp("act", ue3[:, :, 2:2 + L], v3(b[:, :Tn]))
                    cc = kb.nxt("W")
                    o_, _s = PP_OFF["fcw"]
                    wcol = lambda tap: kb.PP[:, l * NPP + o_ + ch * 3 + tap: l * NPP + o_ + ch * 3 + tap + 1]
                    kb.act(cc[:, :Tn], b[:, :Tn], AF.Identity, scale=wcol(2), bias=kb.pcol("fcb", l, ch))
                    cc3 = v3(cc[:, :Tn])
                    kb.stt(cc3, ue3[:, :, 1:1 + L], wcol(1), cc3, ALU.mult, ALU.add)
                    kb.stt(cc3, ue3[:, :, 0:L], wcol(0), cc3, ALU.mult, ALU.add)
                    kb.cp(PL, fcst, ue3[:, :, L:L + 2])
                    cc_t[half] = cc
                    if last:
                        dfc = osfc if kind == "s" else pfc
                        kb.store_T(dfc[l, :, ch * 128:(ch + 1) * 128], S_["fc"][:, ch, :], nseq * 2)
                gg = kb.nxt("W")
                kb.act(gg[:, :Tn], cc_t[0][:, :Tn], AF.Gelu_apprx_tanh)
                kb.tt("dve", G[:, j, :Tn], gg[:, :Tn], cc_t[1][:, :Tn], ALU.mult)

            chk(10)
            if kind == "s" and l + 1 < NL:
                emit_casts(l + 1, 3)
            for mo in range(8):
                b = kb.bank()
                for hj in range(2):
                    WDN = sDN.get()
                    for j2 in range(11):
                        j = hj * 11 + j2
                        kb.mm(b[:, :Tn], WDN[:, j2, :], G[:, j, :Tn], start=j == 0, stop=j == NJ - 1)
                    sDN.done()
                resid(b, mo, MODG2)

        dst = ys if kind == "s" else yp
        for tb in range(ntb):
            xt = kb.nxt("XT")
            for half in range(2):
                b = kb.bank()
                for q4 in range(4):
                    kc = half * 4 + q4
                    kb.tr(b[:tbs, q4 * 128:(q4 + 1) * 128], X[:, kc, tb * 128: tb * 128 + tbs], ident)
                kb.cp("act", xt[:tbs, half * 512:(half + 1) * 512], b[:tbs, :])
            kb.dma(dst[r0 + tb * 128: r0 + tb * 128 + tbs, :], xt[:tbs, :])


_CACHE = {}


def kernel(**inp):
    inp = {k: np.asarray(v) for k, v in inp.items()}
    f32 = np.float32
    if "nc" not in _CACHE:
        _CACHE["nc"] = build()
    nc = _CACHE["nc"]
    pp, sw = _host_params(inp)
    cst, cosP, sinP, msk = _host_consts()
    in_maps = []
    for c in range(8):
        b = c // 2
        sq = slice(c * NSQ, (c + 1) * NSQ)
        m = {
            "xp": np.ascontiguousarray(inp["x_prompt"][b]),
            "xs": np.ascontiguousarray(inp["x_sample"][sq].reshape(TS, D)),
            "ck": np.ascontiguousarray(inp["cache_k"][:, sq].reshape(NL, NSQ, 128, 128)),
            "cv": np.ascontiguousarray(inp["cache_v"][:, sq].reshape(NL, NSQ, 128, 128)),
            "sre": np.ascontiguousarray(inp["state_ssm_re"][:, sq].reshape(NL, NSQ, 1024)),
            "sim": np.ascontiguousarray(inp["state_ssm_im"][:, sq].reshape(NL, NSQ, 1024)),
            "slh": np.ascontiguousarray(inp["state_lru_h"][:, sq]),
            "slc": np.ascontiguousarray(inp["state_lru_conv"][:, sq].reshape(NL, NSQ * 3, 256)),
            "sfc": np.ascontiguousarray(inp["state_ffn_conv"][:, sq].reshape(NL, NSQ * 2, 5632)),
            "c17": np.ascontiguousarray(np.concatenate([inp["c_prompt"][b:b + 1], inp["c_sample"][sq]], axis=0)),
            "wada": inp["w_ada"], "win": inp["w_in"], "wo": inp["w_o"], "wup": inp["ffn_w_up"], "wdn": inp["ffn_w_down"],
            "pp": pp, "sw": sw, "cst": cst, "msk": msk, "cosP": cosP, "sinP": sinP,
        }
        in_maps.append({k_: np.ascontiguousarray(v, dtype=f32) for k_, v in m.items()})
    import os
    ncores = int(os.environ.get("KCORES", "8"))
    res = run_bass_kernel_spmd(nc, in_maps[:ncores], core_ids=list(range(ncores)))
    R = list(res.results)
    while len(R) < 8:
        R.append({k_: np.zeros_like(v) for k_, v in R[0].items()})
    ev = [R[2 * b] for b in range(4)]
    y_p = np.stack([r["yp"] for r in ev], 0)
    y_s = np.concatenate([r["ys"].reshape(NSQ, 4, D) for r in R], 0)

    def pst(name, shape):
        return np.stack([r[name] for r in ev], 1).reshape(shape)

    def sst(name, shape):
        return np.concatenate([r[name].reshape((NL, NSQ) + shape) for r in R], 1)

    outs = (y_p, y_s,
            pst("pk", (NL, 4, 128, 2, 64)), pst("pv", (NL, 4, 128, 2, 64)),
            pst("pre", (NL, 4, 16, 64)), pst("pim", (NL, 4, 16, 64)),
            pst("plh", (NL, 4, 256)), pst("plc", (NL, 4, 3, 256)), pst("pfc", (NL, 4, 2, 5632)),
            sst("osk", (128, 2, 64)), sst("osv", (128, 2, 64)),
            sst("osre", (16, 64)), sst("osim", (16, 64)),
            sst("oslh", (256,)), sst("oslc", (3, 256)), sst("osfc", (2, 5632)))
    return tuple(np.ascontiguousarray(o, dtype=f32) for o in outs)
```
